# Optimizing a Trainium2 kernel written in Bass

```python
import math
import jax, jax.numpy as jnp
from jax import lax
import numpy as np

D_MODEL = 2048
BATCH = 16
SEQ = 2048
DEPTH = 2

CHUNK = 64
N_META = 16
META_PAD = (-N_META) % CHUNK
D_HG = D_MODEL // 2
HG_HEADS = 8
HG_DK = D_HG // HG_HEADS
D_S5 = D_MODEL - D_HG
S5_GROUP = 16
S5_GROUPS = D_S5 // S5_GROUP
S5_STATE = 64
D_IN = 4 * D_HG + D_S5
D_FF = 5632
CONV_W = 3
EPS = 1e-6
F_FLOOR = 1e-6
DT_MIN = 1e-3
DT_MAX = 1e-1

kernel_name = "hymba_hgrn2_s5_convffn_block"


def rmsnorm(x, g):
    xf = x.astype(jnp.float32)
    y = xf * lax.rsqrt(jnp.mean(xf * xf, axis=-1, keepdims=True) + EPS)
    return (y * g.astype(jnp.float32)).astype(x.dtype)


def hgrn2_mixer(q, f_logit, i, g_out, lb, gain):
    f32 = jnp.float32
    b_sz, seq_len, _ = q.shape
    z = f_logit.astype(f32)
    lb = lb.astype(f32)
    f = lb + (1.0 - lb) * jax.nn.sigmoid(z)
    log_f = jnp.log(jnp.maximum(f, F_FLOOR))
    k = (1.0 - lb) * jax.nn.sigmoid(-z)
    qf = jax.nn.silu(q.astype(f32))
    v = i.astype(f32)
    lp = seq_len + META_PAD
    n_chunks = lp // CHUNK

    def to_chunks(t):
        t = jnp.pad(t, ((0, 0), (META_PAD, 0), (0, 0)))
        return t.reshape(b_sz, n_chunks, CHUNK, HG_HEADS, HG_DK).transpose(1, 0, 3, 2, 4)

    causal = jnp.tril(jnp.ones((CHUNK, CHUNK), dtype=bool))[:, :, None]

    def step(state, inp):
        qc, kc, vc, gc = inp
        b = jnp.cumsum(gc, axis=2)
        b_last = b[:, :, -1:, :]
        o_inter = jnp.einsum('bhtd,bhde->bhte', qc * jnp.exp(b), state)
        diff = b[:, :, :, None, :] - b[:, :, None, :, :]
        decay = jnp.exp(jnp.where(causal, diff, -jnp.inf))
        scores = jnp.einsum('bhtd,bhsd,bhtsd->bhts', qc, kc, decay)
        o_intra = jnp.einsum('bhts,bhse->bhte', scores, vc)
        new_state = (jnp.exp(b_last[:, :, 0, :])[..., None] * state
                     + jnp.einsum('bhsd,bhse->bhde', kc * jnp.exp(b_last - b), vc))
        return new_state, o_inter + o_intra

    s0 = jnp.zeros((b_sz, HG_HEADS, HG_DK, HG_DK), f32)
    _, o = lax.scan(step, s0, (to_chunks(qf), to_chunks(k), to_chunks(v), to_chunks(log_f)))
    o = o.transpose(1, 0, 3, 2, 4).reshape(b_sz, lp, HG_HEADS, HG_DK)[:, META_PAD:]
    o = o * lax.rsqrt(jnp.mean(o * o, axis=-1, keepdims=True) + EPS) * gain.astype(f32).reshape(HG_HEADS, HG_DK)
    return o.reshape(b_sz, seq_len, D_HG) * jax.nn.silu(g_out.astype(f32))


def _complex_scan_combine(e1, e2):
    a1r, a1i, b1r, b1i = e1
    a2r, a2i, b2r, b2i = e2
    ar = a2r * a1r - a2i * a1i
    ai = a2r * a1i + a2i * a1r
    a2r_, a2i_ = a2r[:, None], a2i[:, None]
    br = a2r_ * b1r - a2i_ * b1i + b2r
    bi = a2r_ * b1i + a2i_ * b1r + b2i
    return (ar, ai, br, bi)


def s5_mixer(u, lam_re, lam_im, log_step, b_re, b_im, c_re, c_im, d_skip, w_glu, b_glu, gain):
    f32 = jnp.float32
    b_sz, seq_len, _ = u.shape
    uf = u.astype(f32)
    ug = uf.reshape(b_sz, seq_len, S5_GROUPS, S5_GROUP)
    a_re = jnp.minimum(lam_re.astype(f32), -1e-4)
    a_im = lam_im.astype(f32)
    dt = jnp.exp(log_step.astype(f32))[:, None]
    mag = jnp.exp(a_re * dt)
    ab_re = mag * jnp.cos(a_im * dt)
    ab_im = mag * jnp.sin(a_im * dt)
    den = a_re * a_re + a_im * a_im
    x_re, x_im = ab_re - 1.0, ab_im
    z_re = (x_re * a_re + x_im * a_im) / den
    z_im = (x_im * a_re - x_re * a_im) / den
    br, bi = b_re.astype(f32), b_im.astype(f32)
    bb_re = z_re[..., None] * br - z_im[..., None] * bi
    bb_im = z_re[..., None] * bi + z_im[..., None] * br
    bu_re = jnp.einsum('blgh,gph->lbgp', ug, bb_re)
    bu_im = jnp.einsum('blgh,gph->lbgp', ug, bb_im)
    a_seq_re = jnp.broadcast_to(ab_re[None], (seq_len, S5_GROUPS, S5_STATE))
    a_seq_im = jnp.broadcast_to(ab_im[None], (seq_len, S5_GROUPS, S5_STATE))
    _, _, st_re, st_im = lax.associative_scan(_complex_scan_combine, (a_seq_re, a_seq_im, bu_re, bu_im), axis=0)
    y = (jnp.einsum('lbgp,ghp->blgh', st_re, c_re.astype(f32))
         - jnp.einsum('lbgp,ghp->blgh', st_im, c_im.astype(f32)))
    y = y.reshape(b_sz, seq_len, D_S5) + d_skip.astype(f32) * uf
    y = jax.nn.gelu(y)
    y = y * jax.nn.sigmoid(y @ w_glu.astype(f32) + b_glu.astype(f32))
    return rmsnorm(y, gain)


def conv_ffn(h, w_gate, w_up, conv_w, conv_b, w_down):
    a = h @ w_gate
    seq_len = a.shape[1]
    ap = jnp.pad(a, ((0, 0), (CONV_W - 1, 0), (0, 0)))
    conv = conv_b
    for j in range(CONV_W):
        conv = conv + conv_w[j] * ap[:, j:j + seq_len]
    return (jax.nn.silu(conv) * (h @ w_up)) @ w_down


def setup_inputs(seed: int = 0) -> dict:
    key = jax.random.key(seed)
    ks = jax.random.split(key, 32)
    f32 = jnp.float32

    def nrm(k, shape, s):
        return s * jax.random.normal(k, shape, f32)

    lam_im0 = jnp.pi * jnp.arange(S5_STATE, dtype=f32)
    return {
        "x": nrm(ks[0], (BATCH, SEQ, D_MODEL), 1.0),
        "meta_tokens": nrm(ks[1], (N_META, D_MODEL), 1.0),
        "lb_logits": nrm(ks[2], (DEPTH, D_HG), 0.1),
        "norm_mix": 1.0 + nrm(ks[3], (DEPTH, D_MODEL), 0.01),
        "w_in": nrm(ks[4], (DEPTH, D_MODEL, D_IN), D_MODEL ** -0.5),
        "hg_norm": 1.0 + nrm(ks[5], (DEPTH, D_HG), 0.01),
        "s5_lambda_re": -0.5 + nrm(ks[6], (DEPTH, S5_GROUPS, S5_STATE), 0.01),
        "s5_lambda_im": lam_im0 + nrm(ks[7], (DEPTH, S5_GROUPS, S5_STATE), 0.01),
        "s5_log_step": jax.random.uniform(ks[8], (DEPTH, S5_GROUPS), f32, math.log(DT_MIN), math.log(DT_MAX)),
        "s5_b_re": nrm(ks[9], (DEPTH, S5_GROUPS, S5_STATE, S5_GROUP), (2 * S5_GROUP) ** -0.5),
        "s5_b_im": nrm(ks[10], (DEPTH, S5_GROUPS, S5_STATE, S5_GROUP), (2 * S5_GROUP) ** -0.5),
        "s5_c_re": nrm(ks[11], (DEPTH, S5_GROUPS, S5_GROUP, S5_STATE), S5_STATE ** -0.5),
        "s5_c_im": nrm(ks[12], (DEPTH, S5_GROUPS, S5_GROUP, S5_STATE), S5_STATE ** -0.5),
        "s5_d": nrm(ks[13], (DEPTH, D_S5), 0.5),
        "w_glu": nrm(ks[14], (DEPTH, D_S5, D_S5), D_S5 ** -0.5),
        "b_glu": nrm(ks[15], (DEPTH, D_S5), 0.01),
        "s5_norm": 1.0 + nrm(ks[16], (DEPTH, D_S5), 0.01),
        "w_out": nrm(ks[17], (DEPTH, D_HG + D_S5, D_MODEL), (D_HG + D_S5) ** -0.5),
        "norm_ffn": 1.0 + nrm(ks[18], (DEPTH, D_MODEL), 0.01),
        "w_ffn_gate": nrm(ks[19], (DEPTH, D_MODEL, D_FF), D_MODEL ** -0.5),
        "w_ffn_up": nrm(ks[20], (DEPTH, D_MODEL, D_FF), D_MODEL ** -0.5),
        "ffn_conv_w": nrm(ks[21], (DEPTH, CONV_W, D_FF), CONV_W ** -0.5),
        "ffn_conv_b": nrm(ks[22], (DEPTH, D_FF), 0.01),
        "w_ffn_down": nrm(ks[23], (DEPTH, D_FF, D_MODEL), D_FF ** -0.5),
        "final_norm": 1.0 + nrm(ks[24], (D_MODEL,), 0.01),
    }


def reference(x, meta_tokens, lb_logits, norm_mix, w_in, hg_norm, s5_lambda_re, s5_lambda_im,
              s5_log_step, s5_b_re, s5_b_im, s5_c_re, s5_c_im, s5_d, w_glu, b_glu, s5_norm,
              w_out, norm_ffn, w_ffn_gate, w_ffn_up, ffn_conv_w, ffn_conv_b, w_ffn_down, final_norm):
    b_sz = x.shape[0]
    meta = jnp.broadcast_to(meta_tokens.astype(x.dtype)[None], (b_sz, N_META, D_MODEL))
    h = jnp.concatenate([meta, x], axis=1)
    sm = jax.nn.softmax(lb_logits.astype(jnp.float32), axis=0)
    lb_all = jnp.cumsum(sm, axis=0) - sm[0:1]
    for l in range(DEPTH):
        xn = rmsnorm(h, norm_mix[l])
        proj = xn @ w_in[l]
        q, f_logit, i_in, g_out, u = jnp.split(proj, [D_HG, 2 * D_HG, 3 * D_HG, 4 * D_HG], axis=-1)
        o_hg = hgrn2_mixer(q, f_logit, i_in, g_out, lb_all[l], hg_norm[l])
        o_s5 = s5_mixer(u, s5_lambda_re[l], s5_lambda_im[l], s5_log_step[l], s5_b_re[l], s5_b_im[l],
                        s5_c_re[l], s5_c_im[l], s5_d[l], w_glu[l], b_glu[l], s5_norm[l])
        mix = jnp.concatenate([o_hg, o_s5.astype(jnp.float32)], axis=-1).astype(h.dtype)
        h = h + mix @ w_out[l]
        h = h + conv_ffn(rmsnorm(h, norm_ffn[l]), w_ffn_gate[l], w_ffn_up[l], ffn_conv_w[l],
                         ffn_conv_b[l], w_ffn_down[l])
    return rmsnorm(h[:, N_META:], final_norm)
```

```python
import math
from contextlib import ExitStack
import numpy as np
import concourse.bass as bass
import concourse.mybir as mybir
from concourse.bass_utils import run_bass_kernel_spmd

F32 = mybir.dt.float32
BF16 = mybir.dt.bfloat16
I32 = mybir.dt.int32
AF = mybir.ActivationFunctionType
ALU = mybir.AluOpType
P = 128
EPS = 1e-6
F_FLOOR = 1e-6
TT = 192
NT = 2 * TT
NW = NT // 8
NWS = TT // 8


class Cfg:
    def __init__(self, D=2048, DFF=5632, SEQ=2048, L=2, NB=16):
        self.D = D; self.DFF = DFF; self.SEQ = SEQ; self.L = L; self.NB = NB
        self.DHG = D // 2; self.HH = self.DHG // 128
        self.DS5 = D - self.DHG; self.G = self.DS5 // 16; self.GB = self.G // 8
        self.KS = self.DS5 // 128
        self.DIN = 4 * self.DHG + self.DS5
        self.KD = D // 128; self.KF = DFF // 128
        self.LP = SEQ + 64
        assert self.LP % TT == 0
        self.NTILE = self.LP // TT
        self.UW = min(512, self.DS5); self.NU = self.DS5 // self.UW
        self.OW = min(512, D); self.NO = D // self.OW
        self.FW = 512 if DFF % 512 == 0 else (384 if DFF % 384 == 0 else 128)
        self.NF = DFF // self.FW
        self.KPIECES = [(k0, min(16, self.KF - k0)) for k0 in range(0, self.KF, 16)]


class Buf:
    __slots__ = ("name", "w", "r")

    def __init__(self, name):
        self.name = name; self.w = None; self.r = {}


class Eng:
    def __init__(self, name, h, kind):
        self.name = name; self.h = h; self.kind = kind; self.n = 0; self.waited = {}


class DQ:
    def __init__(self, name, eng, sems):
        self.name = name; self.eng = eng; self.sems = sems; self.i = 0


CAP = 20000


class Sync:
    def __init__(self, nc, es):
        self.nc = nc; self.es = es
        self.engs = {}
        self.qs = {}
        self.semlists = {}
        self.trace = {}

    def add_eng(self, name, h, kind="c"):
        self.engs[name] = Eng(name, h, kind); self.semlists[name] = []

    def add_q(self, name, engname, K=8):
        sems = [self.es.enter_context(self.nc.semaphore(f"q_{name}_{i}")) for i in range(K)]
        self.qs[name] = DQ(name, self.engs[engname], sems)

    def _sem(self, name, idx):
        lst = self.semlists[name]
        while len(lst) <= idx:
            lst.append(self.es.enter_context(self.nc.semaphore(f"e_{name}_{len(lst)}")))
        return lst[idx]

    def _wait(self, E, k, v):
        if v <= 0 or E.waited.get(k, 0) >= v:
            return
        if k[0] in self.qs:
            E.h.wait_ge(self.qs[k[0]].sems[k[1]], 16 * v)
        else:
            E.h.wait_ge(self._sem(k[0], k[1]), v)
        E.waited[k] = v
        self.trace.setdefault(E.name, []).append(("w", k, v))

    def _deps(self, E, R, W):
        deps = {}
        for b in R:
            t = b.w
            if t is not None and deps.get(t[0], 0) < t[1]:
                deps[t[0]] = t[1]
        for b in W:
            t = b.w
            if t is not None and deps.get(t[0], 0) < t[1]:
                deps[t[0]] = t[1]
            for t in b.r.values():
                if deps.get(t[0], 0) < t[1]:
                    deps[t[0]] = t[1]
        for k, v in deps.items():
            if k[0] == E.name and E.kind == "pe":
                continue
            self._wait(E, k, v)

    def _mark(self, tk, R, W):
        for b in R:
            o = b.r.get(tk[0])
            if o is None or o[1] < tk[1]:
                b.r[tk[0]] = tk
        for b in W:
            b.w = tk; b.r = {}

    def _tk(self, en, n):
        return ((en, (n - 1) // CAP), (n - 1) % CAP + 1)

    def op(self, en, fn, R=(), W=(), sig=True):
        E = self.engs[en]
        self._deps(E, R, W)
        ins = fn(E.h)
        if sig:
            E.n += 1
            tk = self._tk(en, E.n)
            ins.then_inc(self._sem(en, tk[0][1]), 1)
            self.trace.setdefault(en, []).append(("s", tk[0], tk[1]))
        else:
            tk = self._tk(en, E.n + 1)
        self._mark(tk, R, W)
        return tk

    def dma(self, qn, out, in_, R=(), W=()):
        q = self.qs[qn]; E = q.eng
        self._deps(E, R, W)
        K = len(q.sems); i = q.i; q.i += 1
        lane = i % K; cnt = i // K + 1
        self._wait(E, (qn, lane), cnt - 1)
        ins = E.h.dma_start(out=out, in_=in_)
        ins.then_inc(q.sems[lane], 16)
        tk = ((qn, lane), cnt)
        self.trace.setdefault(E.name, []).append(("s", tk[0], tk[1]))
        self._mark(tk, R, W)
        return tk

    def check_deadlock(self):
        pc = {e: 0 for e in self.trace}
        sem = {}
        progress = True
        while progress:
            progress = False
            for e, tr in self.trace.items():
                while pc[e] < len(tr):
                    kind, k, v = tr[pc[e]]
                    if kind == "w":
                        if sem.get(k, 0) >= v:
                            pc[e] += 1; progress = True
                        else:
                            break
                    else:
                        assert sem.get(k, 0) == v - 1, (e, k, v, sem.get(k, 0))
                        sem[k] = v; pc[e] += 1; progress = True
        stuck = {e: (pc[e], len(tr), tr[pc[e]]) for e, tr in self.trace.items() if pc[e] < len(tr)}
        return stuck

    def final_wait(self, en, tks):
        E = self.engs[en]
        for tk in tks:
            self._wait(E, tk[0], tk[1])
        for qn, q in self.qs.items():
            K = len(q.sems)
            for lane in range(K):
                cnt = (q.i - lane + K - 1) // K
                self._wait(E, (qn, lane), cnt)

    def barrier(self, en, bufs):
        tk = self.op(en, lambda e: e.memset(self.bar_t[:], 0.0), R=tuple(bufs), W=tuple(bufs) + (self.bar_b,))
        for E in self.engs.values():
            self._wait(E, tk[0], tk[1])


def make_consts():
    c = {}
    c["ident"] = np.eye(128, dtype=np.float32)
    c["ones"] = np.ones((128, 128), np.float32)
    s = np.arange(128)
    same = (s[:, None] // 64) == (s[None, :] // 64)
    c["tri2"] = (same & (s[:, None] <= s[None, :])).astype(np.float32)
    sel = np.zeros((128, 8, 8, 128), np.float32)
    for a_ in range(8):
        for b_ in range(8):
            for hi in range(16):
                sel[16 * a_ + hi, a_, b_, 16 * b_ + hi] = 1.0
    c["sel"] = sel
    a = np.arange(128) // 16
    c["mmask"] = (a[None, :] >= a[:, None]).astype(np.float32)
    cm = np.ones((128, NT), np.float32); cm[:, ::64] = 0.0
    c["cmask"] = cm
    return c


def host_prep(inp, cfg):
    L = cfg.L; f = np.float32
    sh = {}
    perm = []
    for h in range(cfg.HH):
        for part in range(4):
            perm.extend(range(part * cfg.DHG + h * 128, part * cfg.DHG + (h + 1) * 128))
    perm.extend(range(4 * cfg.DHG, cfg.DIN))
    sh["w_in"] = np.ascontiguousarray(np.asarray(inp["w_in"], f)[:, :, perm])
    for k in ("w_glu", "w_out", "w_ffn_gate", "w_ffn_up", "w_ffn_down"):
        sh[k] = np.ascontiguousarray(np.asarray(inp[k], f))

    def fm(v, nblk):
        return np.ascontiguousarray(np.asarray(v, f).reshape(L, nblk, 128).transpose(2, 0, 1))
    sh["g_mix"] = fm(inp["norm_mix"], cfg.KD)
    sh["g_ffn"] = fm(inp["norm_ffn"], cfg.KD)
    sh["g_fin"] = np.ascontiguousarray(np.asarray(inp["final_norm"], f).reshape(cfg.KD, 128).T)
    sh["hg_gain"] = fm(inp["hg_norm"], cfg.HH)
    sh["s5_gain"] = fm(inp["s5_norm"], cfg.KS)
    sh["s5_d"] = fm(inp["s5_d"], cfg.KS)
    sh["b_glu"] = fm(inp["b_glu"], cfg.KS)
    sh["lbl_fm"] = fm(inp["lb_logits"], cfg.HH)
    cw = np.asarray(inp["ffn_conv_w"], f)
    sh["conv_w"] = np.ascontiguousarray(cw.reshape(L, 3, cfg.KF, 128).transpose(3, 0, 1, 2))
    sh["conv_b"] = fm(inp["ffn_conv_b"], cfg.KF)
    sh["lam_re"] = np.ascontiguousarray(np.asarray(inp["s5_lambda_re"], f).transpose(2, 0, 1))
    sh["lam_im"] = np.ascontiguousarray(np.asarray(inp["s5_lambda_im"], f).transpose(2, 0, 1))
    sh["lstep"] = np.ascontiguousarray(np.broadcast_to(np.asarray(inp["s5_log_step"], f)[None], (64, L, cfg.G)))
    sh["b_re"] = np.ascontiguousarray(np.asarray(inp["s5_b_re"], f).transpose(2, 0, 1, 3))
    sh["b_im"] = np.ascontiguousarray(np.asarray(inp["s5_b_im"], f).transpose(2, 0, 1, 3))
    sh["c_re"] = np.ascontiguousarray(np.asarray(inp["s5_c_re"], f).transpose(3, 0, 1, 2))
    sh["c_im"] = np.ascontiguousarray(np.asarray(inp["s5_c_im"], f).transpose(3, 0, 1, 2))
    sh.update(make_consts())
    x = np.asarray(inp["x"], f); meta = np.asarray(inp["meta_tokens"], f)
    ncores = cfg.NB // 2
    maps = []
    for c in range(ncores):
        xp = np.zeros((2, cfg.LP, cfg.D), f)
        xp[:, 48:64] = meta[None]
        xp[:, 64:] = x[2 * c:2 * c + 2]
        m = dict(sh); m["xp"] = xp
        maps.append(m)
    return maps


def input_shapes(cfg):
    L = cfg.L
    return {
        "xp": [2, cfg.LP, cfg.D], "w_in": [L, cfg.D, cfg.DIN], "w_glu": [L, cfg.DS5, cfg.DS5],
        "w_out": [L, cfg.D, cfg.D], "w_ffn_gate": [L, cfg.D, cfg.DFF], "w_ffn_up": [L, cfg.D, cfg.DFF],
        "w_ffn_down": [L, cfg.DFF, cfg.D],
        "g_mix": [128, L, cfg.KD], "g_ffn": [128, L, cfg.KD], "g_fin": [128, cfg.KD],
        "hg_gain": [128, L, cfg.HH], "s5_gain": [128, L, cfg.KS], "s5_d": [128, L, cfg.KS],
        "b_glu": [128, L, cfg.KS], "lbl_fm": [128, L, cfg.HH],
        "conv_w": [128, L, 3, cfg.KF], "conv_b": [128, L, cfg.KF],
        "lam_re": [64, L, cfg.G], "lam_im": [64, L, cfg.G], "lstep": [64, L, cfg.G],
        "b_re": [64, L, cfg.G, 16], "b_im": [64, L, cfg.G, 16], "c_re": [64, L, cfg.G, 16], "c_im": [64, L, cfg.G, 16],
        "ident": [128, 128], "ones": [128, 128], "tri2": [128, 128],
        "sel": [128, 8, 8, 128], "mmask": [128, 128], "cmask": [128, NT],
    }


class Builder:
    def __init__(self, cfg, stop_after=None):
        self.cfg = cfg
        self.stop_after = stop_after

    def sb(self, name, shape, dt=F32):
        return self.es.enter_context(self.nc.sbuf_tensor(name, list(shape), dt))

    def build(self):
        cfg = self.cfg
        nc = bass.Bass("TRN2", target_bir_lowering=False)
        self.nc = nc
        with ExitStack() as es:
            self.es = es
            S = Sync(nc, es); self.S = S
            S.add_eng("pe", nc.tensor, "pe"); S.add_eng("act", nc.scalar); S.add_eng("dve", nc.vector)
            S.add_eng("pool", nc.gpsimd); S.add_eng("sp", nc.sync)
            S.add_q("qs", "sp", 8); S.add_q("qg", "pool", 8)
            S.bar_t = self.sb("bar_t", [128, 1]); S.bar_b = Buf("bar")
            self.dram = {}
            for k, shp in input_shapes(cfg).items():
                self.dram[k] = nc.dram_tensor(k, shp, F32, kind="ExternalInput").ap()
            self.dram["out"] = nc.dram_tensor("out", [2, cfg.LP, cfg.D], F32, kind="ExternalOutput").ap()
            if self.stop_after == "s5p":
                self.dram["dbg"] = nc.dram_tensor("dbg", [128, 8192], F32, kind="ExternalOutput").ap()
            if self.stop_after == "s5f":
                self.dram["dbg2"] = nc.dram_tensor("dbg2", [128, 32768], F32, kind="ExternalOutput").ap()
            self.psum = [es.enter_context(nc.psum_tensor(f"ps{i}", [128, 512], F32)) for i in range(8)]
            self.pbuf = [Buf(f"ps{i}") for i in range(8)]
            self.emit_all()
            stuck = S.check_deadlock()
            if stuck:
                raise RuntimeError(f"sync deadlock: {stuck}")
        return nc

    def bank(self, i):
        return (self.psum[i], self.pbuf[i])

    def emit_all(self):
        cfg = self.cfg; nc = self.nc; S = self.S; sb = self.sb
        L = cfg.L; KD = cfg.KD
        self.out_tks = []
        cst = {}
        self.cst = cst
        self.b_const = Buf("const")
        bc_ = self.b_const
        for key in ("g_mix", "g_ffn", "g_fin", "hg_gain", "s5_gain", "s5_d", "b_glu", "lbl_fm", "conv_w", "conv_b",
                    "ident", "ones", "tri2", "mmask", "cmask"):
            t = sb("c_" + key, input_shapes(cfg)[key], F32)
            S.dma("qs", t[:], self.dram[key], W=(bc_,))
            cst[key] = t
        cst["ones_bf"] = sb("ones_bf", [128, 128], BF16)
        cst["tri_bf"] = sb("tri_bf", [128, 128], BF16)
        cst["sel_bf"] = sb("sel_bf", [128, 8, 8, 128], BF16)
        cst["eps"] = sb("eps_t", [128, 1], F32)
        S.op("dve", lambda e: e.tensor_copy(out=cst["ones_bf"][:], in_=cst["ones"][:]), R=(bc_,), W=(bc_,))
        S.op("dve", lambda e: e.tensor_copy(out=cst["tri_bf"][:], in_=cst["tri2"][:]), R=(bc_,), W=(bc_,))
        S.op("dve", lambda e: e.memset(cst["eps"][:], EPS), W=(bc_,))
        with nc.sbuf_tensor("sel_tmp", [128, 8, 8, 128], F32) as selt:
            bt = Buf("selt")
            S.dma("qs", selt[:], self.dram["sel"], W=(bt,))
            S.op("dve", lambda e: e.tensor_copy(out=cst["sel_bf"][:], in_=selt[:]), R=(bt,), W=(bc_,))
            S.barrier("dve", (bt, bc_))
        self.scr = {}
        for l in range(L):
            d = {}
            d["head"] = nc.dram_tensor(f"s_head{l}", [cfg.HH, 128, KD, 512], BF16, kind="Internal").ap()
            d["u"] = nc.dram_tensor(f"s_u{l}", [cfg.NU, 128, KD, cfg.UW], BF16, kind="Internal").ap()
            d["glu"] = nc.dram_tensor(f"s_glu{l}", [cfg.NU, 128, cfg.KS, cfg.UW], BF16, kind="Internal").ap()
            d["out"] = nc.dram_tensor(f"s_out{l}", [cfg.NO, 128, KD, cfg.OW], BF16, kind="Internal").ap()
            d["gate"] = nc.dram_tensor(f"s_gate{l}", [cfg.NF, 128, KD, cfg.FW], BF16, kind="Internal").ap()
            d["up"] = nc.dram_tensor(f"s_up{l}", [cfg.NF, 128, KD, cfg.FW], BF16, kind="Internal").ap()
            d["down"] = nc.dram_tensor(f"s_down{l}", [cfg.NO, len(cfg.KPIECES), 128, 16, cfg.OW], BF16, kind="Internal").ap()
            d["M"] = nc.dram_tensor(f"s_M{l}", [128, cfg.G, 128], BF16, kind="Internal").ap()
            d["B"] = nc.dram_tensor(f"s_B{l}", [128, cfg.G, 128], BF16, kind="Internal").ap()
            d["C"] = nc.dram_tensor(f"s_C{l}", [64, 2, cfg.G, 128], BF16, kind="Internal").ap()
            self.scr[l] = d
        self.slab_ready = {}
        def fin():
            for b in list(self.slab_ready.values()) + [self.b_const]:
                if b.w is not None:
                    self.out_tks.append(b.w)
            S.barrier("dve", (self.b_const,))
            S.final_wait("sp", self.out_tks)
        if self.stop_after == "consts":
            return fin()
        import os
        if not os.environ.get("SKIP_PRE"):
            self.emit_precast()
        if self.stop_after == "precast":
            return fin()
        self.emit_lb()
        if self.stop_after == "lb":
            return fin()
        if not os.environ.get("SKIP_PRE"):
            self.emit_s5_prologue()
        if self.stop_after == "s5p":
            return fin()
        self.emit_main()
        S.final_wait("sp", self.out_tks)

    def emit_precast(self):
        cfg = self.cfg; S = self.S; D = self.dram; nc = self.nc
        jobs = []

        def v(ap):
            return ap.rearrange("(k p) c -> p k c", p=128)
        for l in range(cfg.L):
            sc = self.scr[l]
            for h in range(cfg.HH):
                jobs.append((("head", l, h), sc["head"][h], v(D["w_in"][l, :, h * 512:(h + 1) * 512]), cfg.KD, 512))
            for j in range(cfg.NU):
                c0 = cfg.HH * 512 + j * cfg.UW
                jobs.append((("u", l, j), sc["u"][j], v(D["w_in"][l, :, c0:c0 + cfg.UW]), cfg.KD, cfg.UW))
            for j in range(cfg.NU):
                jobs.append((("glu", l, j), sc["glu"][j], v(D["w_glu"][l, :, j * cfg.UW:(j + 1) * cfg.UW]), cfg.KS, cfg.UW))
            for j in range(cfg.NO):
                jobs.append((("out", l, j), sc["out"][j], v(D["w_out"][l, :, j * cfg.OW:(j + 1) * cfg.OW]), cfg.KD, cfg.OW))
            for j in range(cfg.NF):
                jobs.append((("gate", l, j), sc["gate"][j], v(D["w_ffn_gate"][l, :, j * cfg.FW:(j + 1) * cfg.FW]), cfg.KD, cfg.FW))
                jobs.append((("up", l, j), sc["up"][j], v(D["w_ffn_up"][l, :, j * cfg.FW:(j + 1) * cfg.FW]), cfg.KD, cfg.FW))
            for j in range(cfg.NO):
                for pi, (k0, nk) in enumerate(cfg.KPIECES):
                    jobs.append((("down", l, j, pi), sc["down"][j, pi, :, 0:nk, :],
                                 v(D["w_ffn_down"][l, k0 * 128:(k0 + nk) * 128, j * cfg.OW:(j + 1) * cfg.OW]), nk, cfg.OW))
        NB_ = 3
        with ExitStack() as es2:
            s32 = [es2.enter_context(nc.sbuf_tensor(f"pc32_{i}", [128, 8192], F32)) for i in range(NB_)]
            s16 = [es2.enter_context(nc.sbuf_tensor(f"pc16_{i}", [128, 8192], BF16)) for i in range(NB_)]
            b32 = [Buf(f"pc32_{i}") for i in range(NB_)]; b16 = [Buf(f"pc16_{i}") for i in range(NB_)]
            engs = ["dve", "act", "pool"]
            for n, (key, dst, src, nk, cw) in enumerate(jobs):
                i = n % NB_
                v32 = s32[i][:, 0:nk * cw].rearrange("p (k c) -> p k c", c=cw)
                v16 = s16[i][:, 0:nk * cw].rearrange("p (k c) -> p k c", c=cw)
                S.dma("qs", v32, src, W=(b32[i],))
                en = engs[n % 3]
                if en == "act":
                    S.op("act", lambda e, i=i, nk=nk, cw=cw: e.activation(out=s16[i][:, 0:nk * cw], in_=s32[i][:, 0:nk * cw], func=AF.Copy), R=(b32[i],), W=(b16[i],))
                else:
                    S.op(en, lambda e, i=i, nk=nk, cw=cw: e.tensor_copy(out=s16[i][:, 0:nk * cw], in_=s32[i][:, 0:nk * cw]), R=(b32[i],), W=(b16[i],))
                rb = Buf(str(key)); self.slab_ready[key] = rb
                S.dma("qs", dst, v16, R=(b16[i],), W=(rb,))
            S.barrier("dve", tuple(b32) + tuple(b16) + tuple(self.slab_ready.values()))

    def emit_lb(self):
        cfg = self.cfg; S = self.S; sb = self.sb; L = cfg.L; W_ = cfg.HH
        lg = self.cst["lbl_fm"]; b = Buf("lb"); rb = (self.b_const, b)
        mx = sb("lb_mx", [128, W_]); ex = sb("lb_ex", [128, L, W_]); sm = sb("lb_sm", [128, W_])
        lb = sb("lb_fm", [128, L, W_]); oml = sb("oml_fm", [128, L, W_]); noml = sb("noml_fm", [128, L, W_])
        V = lambda fn: S.op("dve", fn, R=rb, W=(b,))
        V(lambda e: e.tensor_copy(out=mx[:], in_=lg[:, 0, :]))
        for l in range(1, L):
            V(lambda e, l=l: e.tensor_tensor(out=mx[:], in0=mx[:], in1=lg[:, l, :], op=ALU.max))
        for l in range(L):
            V(lambda e, l=l: e.tensor_tensor(out=ex[:, l, :], in0=lg[:, l, :], in1=mx[:], op=ALU.subtract))
        S.op("act", lambda e: e.activation(out=ex[:], in_=ex[:], func=AF.Exp), R=(b,), W=(b,))
        V(lambda e: e.tensor_copy(out=sm[:], in_=ex[:, 0, :]))
        for l in range(1, L):
            V(lambda e, l=l: e.tensor_tensor(out=sm[:], in0=sm[:], in1=ex[:, l, :], op=ALU.add))
        V(lambda e: e.reciprocal(out=sm[:], in_=sm[:]))
        V(lambda e: e.memset(lb[:, 0, :], 0.0))
        for l in range(1, L):
            V(lambda e, l=l: e.tensor_tensor(out=ex[:, l, :], in0=ex[:, l, :], in1=sm[:], op=ALU.mult))
            V(lambda e, l=l: e.tensor_tensor(out=lb[:, l, :], in0=lb[:, l - 1, :], in1=ex[:, l, :], op=ALU.add))
        V(lambda e: e.tensor_scalar(out=oml[:], in0=lb[:], scalar1=-1.0, scalar2=1.0, op0=ALU.mult, op1=ALU.add))
        V(lambda e: e.tensor_scalar(out=noml[:], in0=oml[:], scalar1=-1.0, scalar2=None, op0=ALU.mult))
        self.lb = (lb, oml, noml, b)

    def emit_s5_prologue(self):
        cfg = self.cfg; S = self.S; nc = self.nc; G = cfg.G
        TWO_PI = 2.0 * math.pi
        self.r8 = {}
        self.s5_ready = {}
        for l in range(cfg.L):
            r8re = self.sb(f"r8re{l}", [64, G, 2]); r8im = self.sb(f"r8im{l}", [64, G, 2]); r8imn = self.sb(f"r8imn{l}", [64, G, 2])
            br8 = Buf(f"r8_{l}")
            self.r8[l] = (r8re, r8im, r8imn, br8)
            for nm in ("M", "B", "C"):
                self.s5_ready[(nm, l)] = Buf(f"s5r{nm}{l}")
            with ExitStack() as es2:
                def t(name, shape, dt=F32):
                    return es2.enter_context(nc.sbuf_tensor(f"p{l}_{name}", list(shape), dt))
                b = Buf("s5p")
                lre = t("lre", [64, G]); lim = t("lim", [64, G]); lst = t("lst", [64, G])
                S.dma("qs", lre[:], self.dram["lam_re"][:, l, :], W=(b,))
                S.dma("qs", lim[:], self.dram["lam_im"][:, l, :], W=(b,))
                S.dma("qs", lst[:], self.dram["lstep"][:, l, :], W=(b,))
                bre = t("bre", [64, G, 16]); bim = t("bim", [64, G, 16]); cre = t("cre", [64, G, 16]); cim = t("cim", [64, G, 16])
                for tt_, key in ((bre, "b_re"), (bim, "b_im"), (cre, "c_re"), (cim, "c_im")):
                    S.dma("qs", tt_[:], self.dram[key][:, l], W=(b,))
                are = t("are", [64, G]); dt_ = t("dt", [64, G]); xr = t("xr", [64, G]); th = t("th", [64, G])
                V = lambda fn: S.op("dve", fn, R=(b,), W=(b,))
                A = lambda fn: S.op("act", fn, R=(b,), W=(b,))
                V(lambda e: e.tensor_scalar(out=are[:], in0=lre[:], scalar1=-1e-4, scalar2=None, op0=ALU.min))
                A(lambda e: e.activation(out=dt_[:], in_=lst[:], func=AF.Exp))
                V(lambda e: e.tensor_tensor(out=xr[:], in0=are[:], in1=dt_[:], op=ALU.mult))
                V(lambda e: e.tensor_tensor(out=th[:], in0=lim[:], in1=dt_[:], op=ALU.mult))
                Ere = t("Ere", [64, 9, G]); Eim = t("Eim", [64, 9, G]); Nre = t("Nre", [64, 9, G]); Nim = t("Nim", [64, 9, G])
                mp = t("mp", [64, G]); mn = t("mn", [64, G]); ang = t("ang", [64, G]); kf = t("kf", [64, G]); ki = t("ki", [64, G], I32)
                sn = t("sn", [64, G]); cs = t("cs", [64, G])
                V(lambda e: e.memset(Ere[:, 0, :], 1.0)); V(lambda e: e.memset(Eim[:, 0, :], 0.0))
                V(lambda e: e.memset(Nre[:, 0, :], 1.0)); V(lambda e: e.memset(Nim[:, 0, :], 0.0))

                def sin_of(dst, k, shift):
                    V(lambda e: e.tensor_scalar(out=ang[:], in0=th[:], scalar1=float(k), scalar2=float(shift), op0=ALU.mult, op1=ALU.add))
                    V(lambda e: e.tensor_scalar(out=kf[:], in0=ang[:], scalar1=1.0 / TWO_PI, scalar2=None, op0=ALU.mult))
                    V(lambda e: e.tensor_copy(out=ki[:], in_=kf[:]))
                    V(lambda e: e.tensor_copy(out=kf[:], in_=ki[:]))
                    V(lambda e: e.scalar_tensor_tensor(out=ang[:], in0=kf[:], scalar=-TWO_PI, in1=ang[:], op0=ALU.mult, op1=ALU.add))
                    V(lambda e: e.tensor_scalar(out=ang[:], in0=ang[:], scalar1=math.pi, scalar2=-math.pi, op0=ALU.min, op1=ALU.max))
                    A(lambda e: e.activation(out=dst[:], in_=ang[:], func=AF.Sin))
                for k in range(1, 9):
                    A(lambda e, k=k: e.activation(out=mp[:], in_=xr[:], func=AF.Exp, scale=float(k)))
                    A(lambda e, k=k: e.activation(out=mn[:], in_=xr[:], func=AF.Exp, scale=float(-k)))
                    sin_of(sn, k, 0.0); sin_of(cs, k, math.pi / 2)
                    V(lambda e, k=k: e.tensor_tensor(out=Ere[:, k, :], in0=mp[:], in1=cs[:], op=ALU.mult))
                    V(lambda e, k=k: e.tensor_tensor(out=Eim[:, k, :], in0=mp[:], in1=sn[:], op=ALU.mult))
                    V(lambda e, k=k: e.tensor_tensor(out=Nre[:, k, :], in0=mn[:], in1=cs[:], op=ALU.mult))
                    V(lambda e, k=k: e.scalar_tensor_tensor(out=Nim[:, k, :], in0=mn[:], scalar=-1.0, in1=sn[:], op0=ALU.mult, op1=ALU.mult))
                for s_ in range(2):
                    S.op("dve", lambda e, s_=s_: e.tensor_copy(out=r8re[:, :, s_], in_=Ere[:, 8, :]), R=(b,), W=(br8,))
                    S.op("dve", lambda e, s_=s_: e.tensor_copy(out=r8im[:, :, s_], in_=Eim[:, 8, :]), R=(b,), W=(br8,))
                    S.op("dve", lambda e, s_=s_: e.tensor_scalar(out=r8imn[:, :, s_], in0=Eim[:, 8, :], scalar1=-1.0, scalar2=None, op0=ALU.mult), R=(b,), W=(br8,))
                den = t("den", [64, G]); zre = t("zre", [64, G]); zim = t("zim", [64, G]); t1 = t("t1", [64, G]); x1 = t("x1", [64, G])
                V(lambda e: e.tensor_tensor(out=den[:], in0=are[:], in1=are[:], op=ALU.mult))
                V(lambda e: e.tensor_tensor(out=t1[:], in0=lim[:], in1=lim[:], op=ALU.mult))
                V(lambda e: e.tensor_tensor(out=den[:], in0=den[:], in1=t1[:], op=ALU.add))
                V(lambda e: e.reciprocal(out=den[:], in_=den[:]))
                V(lambda e: e.tensor_scalar(out=x1[:], in0=Ere[:, 1, :], scalar1=-1.0, scalar2=None, op0=ALU.add))
                V(lambda e: e.tensor_tensor(out=zre[:], in0=x1[:], in1=are[:], op=ALU.mult))
                V(lambda e: e.tensor_tensor(out=t1[:], in0=Eim[:, 1, :], in1=lim[:], op=ALU.mult))
                V(lambda e: e.tensor_tensor(out=zre[:], in0=zre[:], in1=t1[:], op=ALU.add))
                V(lambda e: e.tensor_tensor(out=zre[:], in0=zre[:], in1=den[:], op=ALU.mult))
                V(lambda e: e.tensor_tensor(out=zim[:], in0=Eim[:, 1, :], in1=are[:], op=ALU.mult))
                V(lambda e: e.tensor_tensor(out=t1[:], in0=x1[:], in1=lim[:], op=ALU.mult))
                V(lambda e: e.tensor_tensor(out=zim[:], in0=zim[:], in1=t1[:], op=ALU.subtract))
                V(lambda e: e.tensor_tensor(out=zim[:], in0=zim[:], in1=den[:], op=ALU.mult))
                Bbre = t("Bbre", [64, G, 16]); Bbim = t("Bbim", [64, G, 16]); tg16 = t("tg16", [64, G, 16])

                def bcG(ap2, n):
                    return ap2.unsqueeze(2).broadcast_to([64, n, 16])

                def cmul(ore, oim, are_, aim_, bre_, bim_, tmp):
                    V(lambda e: e.tensor_tensor(out=ore, in0=bre_, in1=are_, op=ALU.mult))
                    V(lambda e: e.tensor_tensor(out=tmp, in0=bim_, in1=aim_, op=ALU.mult))
                    V(lambda e: e.tensor_tensor(out=ore, in0=ore, in1=tmp, op=ALU.subtract))
                    V(lambda e: e.tensor_tensor(out=oim, in0=bim_, in1=are_, op=ALU.mult))
                    V(lambda e: e.tensor_tensor(out=tmp, in0=bre_, in1=aim_, op=ALU.mult))
                    V(lambda e: e.tensor_tensor(out=oim, in0=oim, in1=tmp, op=ALU.add))
                cmul(Bbre[:], Bbim[:], bcG(zre[:], G), bcG(zim[:], G), bre[:], bim[:], tg16[:])
                dbg_on = (self.stop_after == "s5p" and l == 0)

                def dump(ap2, parts, n, off):
                    tk = S.dma("qs", self.dram["dbg"][0:parts, off:off + n], ap2, R=(b, bo), W=(Buf("d"),))
                    b.r[tk[0]] = tk
                if dbg_on:
                    bo = Buf("s5p_out")
                    dump(Ere[:].rearrange("p k g -> p (k g)"), 64, 9 * G, 0)
                    dump(Eim[:].rearrange("p k g -> p (k g)"), 64, 9 * G, 9 * G)
                    dump(zre[:], 64, G, 18 * G); dump(zim[:], 64, G, 19 * G)
                    dump(Bbre[:, 0:8, :].rearrange("p g h -> p (g h)"), 64, 128, 20 * G)
                    dump(Nre[:].rearrange("p k g -> p (k g)"), 64, 9 * G, 20 * G + 128)
                    dump(Nim[:].rearrange("p k g -> p (k g)"), 64, 9 * G, 29 * G + 128)
                if not dbg_on:
                    bo = Buf("s5p_out")
                Mbf = t("Mbf", [128, 8, 128], BF16); Bbf = t("Bbf", [128, 8, 128], BF16); Cbf = t("Cbf", [64, 2, 8, 128], BF16)
                Xre = t("Xre", [64, 8, 8, 16]); Xim = t("Xim", [64, 8, 8, 16]); BTre = t("BTre", [64, 8, 8, 16]); BTim = t("BTim", [64, 8, 8, 16])
                CRre = t("CRre", [64, 8, 8, 16]); CRim = t("CRim", [64, 8, 8, 16]); tb = t("tb", [64, 8, 16])
                ident = self.cst["ident"]
                for gb in range(cfg.GB):
                    gs = slice(gb * 8, gb * 8 + 8)
                    for s_ in range(8):
                        cmul(Xre[:, :, s_, :], Xim[:, :, s_, :], bcG(Nre[:, s_ + 1, gs], 8), bcG(Nim[:, s_ + 1, gs], 8), Bbre[:, gs, :], Bbim[:, gs, :], tb[:])
                        cmul(BTre[:, :, s_, :], BTim[:, :, s_, :], bcG(Ere[:, 7 - s_, gs], 8), bcG(Eim[:, 7 - s_, gs], 8), Bbre[:, gs, :], Bbim[:, gs, :], tb[:])
                        cmul(CRre[:, :, s_, :], CRim[:, :, s_, :], bcG(Ere[:, s_ + 1, gs], 8), bcG(Eim[:, s_ + 1, gs], 8), cre[:, gs, :], cim[:, gs, :], tb[:])
                    S.op("dve", lambda e: e.tensor_scalar(out=CRim[:], in0=CRim[:], scalar1=-1.0, scalar2=None, op0=ALU.mult), R=(b,), W=(b,))
                    if dbg_on and gb == 0:
                        dump(Xre[:].rearrange("p g s h -> p (g s h)"), 64, 1024, 1024)
                        dump(CRre[:].rearrange("p g s h -> p (g s h)"), 64, 1024, 2048)
                        dump(CRim[:].rearrange("p g s h -> p (g s h)"), 64, 1024, 3072)
                        dump(BTre[:].rearrange("p g s h -> p (g s h)"), 64, 1024, 4096)
                    S.op("dve", lambda e: e.tensor_copy(out=Cbf[:, 0], in_=CRre[:].rearrange("p g s h -> p g (s h)")), R=(b,), W=(bo,))
                    S.op("dve", lambda e: e.tensor_copy(out=Cbf[:, 1], in_=CRim[:].rearrange("p g s h -> p g (s h)")), R=(b,), W=(bo,))
                    for gi in range(8):
                        pm = self.bank(gi % 2); pb = self.bank(2 + gi % 2)
                        fl = lambda ap: ap.rearrange("p s h -> p (s h)")
                        S.op("pe", lambda e, gi=gi, pm=pm: e.matmul(pm[0][:, 0:128], fl(Xre[:, gi]), fl(CRre[:, gi]), start=True, stop=False), R=(b,), W=(pm[1],), sig=False)
                        S.op("pe", lambda e, gi=gi, pm=pm: e.matmul(pm[0][:, 0:128], fl(Xim[:, gi]), fl(CRim[:, gi]), start=False, stop=True), R=(b,), W=(pm[1],))
                        S.op("dve", lambda e, gi=gi, pm=pm: e.tensor_tensor(out=Mbf[:, gi, :], in0=pm[0][:, 0:128], in1=self.cst["mmask"][:], op=ALU.mult), R=(pm[1], self.b_const), W=(bo,))
                        if dbg_on and gb == 0 and gi == 0:
                            mdb = t("mdb", [128, 256])
                            S.op("dve", lambda e, pm=pm: e.tensor_tensor(out=mdb[:, 0:128], in0=pm[0][:, 0:128], in1=self.cst["mmask"][:], op=ALU.mult), R=(pm[1], self.b_const), W=(b,))
                            dump(mdb[:, 0:128], 128, 128, 5120)
                        S.op("pe", lambda e, gi=gi, pb=pb: e.matmul(pb[0][:, 0:64], fl(BTre[:, gi]), ident[0:64, 0:64], start=True, stop=True), R=(b, self.b_const), W=(pb[1],), sig=False)
                        S.op("pe", lambda e, gi=gi, pb=pb: e.matmul(pb[0][:, 64:128], fl(BTim[:, gi]), ident[0:64, 0:64], start=True, stop=True), R=(b, self.b_const), W=(pb[1],))
                        S.op("act", lambda e, gi=gi, pb=pb: e.activation(out=Bbf[:, gi, :], in_=pb[0][:, 0:128], func=AF.Copy), R=(pb[1],), W=(bo,))
                    sc = self.scr[l]
                    xb = Buf("x")
                    tk1 = S.dma("qs", sc["M"][:, gs, :], Mbf[:], R=(bo,), W=(xb,))
                    tk2 = S.dma("qs", sc["B"][:, gs, :], Bbf[:], R=(bo,), W=(xb,))
                    tk3 = S.dma("qs", sc["C"][:, :, gs, :], Cbf[:], R=(bo,), W=(xb,))
                    for nm, tk in (("M", tk1), ("B", tk2), ("C", tk3)):
                        rb_ = self.s5_ready[(nm, l)]
                        rb_.r[tk[0]] = tk
                S.barrier("dve", (b, bo, br8) + tuple(self.s5_ready[(nm, l)] for nm in ("M", "B", "C")))

    def emit_main(self):
        cfg = self.cfg; nc = self.nc; S = self.S; sb = self.sb; cst = self.cst
        L = cfg.L; KD = cfg.KD; HH = cfg.HH; KS = cfg.KS; KF = cfg.KF; G = cfg.G
        GH = min(16, G); NHALF = G // GH
        ones_bf = cst["ones_bf"]; bcn = self.b_const
        hT = sb("hT", [128, KD, NT]); b_h = [Buf(f"h{k}") for k in range(KD)]
        xn = sb("xn", [128, KD, NT], BF16); b_xn = [Buf(f"xn{k}") for k in range(KD)]
        mix = sb("mix", [128, KD, NT], BF16); b_mix = [Buf(f"mix{k}") for k in range(KD)]
        NS = 3
        wsl = [sb(f"wsl{i}", [128, 8192], BF16) for i in range(NS)]; b_wsl = [Buf(f"wsl{i}") for i in range(NS)]
        self.ws_i = 0
        hst = sb("hst", [128, L, 2, HH, 128]); b_hst = [[[Buf(f"hst{l}_{s}_{h}") for h in range(HH)] for s in range(2)] for l in range(L)]
        s5c = sb("s5c", [64, L, 2, G, 2]); b_s5c = [Buf(f"s5c{l}") for l in range(L)]
        cvc = sb("cvc", [128, L, KF, 2, 2]); b_cvc = [Buf(f"cvc{l}") for l in range(L)]
        std = sb("std", [128, NT]); rstd = sb("rstd", [128, NT]); b_std = Buf("std")
        S.op("dve", lambda e: e.memset(hst[:].rearrange("p a b c d -> p (a b c d)"), 0.0), W=tuple(b for a in b_hst for c in a for b in c))
        S.op("dve", lambda e: e.memset(s5c[:].rearrange("p a b c d -> p (a b c d)"), 0.0), W=tuple(b_s5c))
        S.op("dve", lambda e: e.memset(cvc[:].rearrange("p a b c d -> p (a b c d)"), 0.0), W=tuple(b_cvc))
        reg_bufs = []

        def RB(name):
            b_ = Buf(name); reg_bufs.append(b_); return b_
        sizes = {
            "io": cfg.D,
            "hg": 4 * NT + 4 * NT + NT + NT + (3 * 128 + 2 * NT + 3 * 128 + 3 * 128 + 2 * 128 + NT) // 2 + 8,
            "s5": KS * NT + 2 * GH * NW + 2 * 2 * GH * 2 + KS * NT + 2 * NT + (2 * NT + GH * NW + 2 * GH * NW + GH * NW + KS * NT) // 2 + 8,
            "ff": 2 * 2 * (TT + 2) + 2 * 2 * TT + 2 * 2 * TT + (KF * NT) // 2 + 8,
        }
        RW = max(sizes.values())
        reg = sb("reg", [128, RW])

        class RA:
            def __init__(s_): s_.off = 0
            def _shape(s_, v, shape):
                if len(shape) == 2: return v
                names = "abcd"[:len(shape) - 1]
                pat = "p (" + " ".join(names) + ") -> p " + " ".join(names)
                return v.rearrange(pat, **{n_: d_ for n_, d_ in zip(names[1:], shape[2:])})
            def f32(s_, shape, parts=128):
                n = int(np.prod(shape[1:])); v = reg[0:parts, s_.off:s_.off + n]; s_.off += n
                assert s_.off <= RW
                return s_._shape(v, shape)
            def bf(s_, shape, parts=128):
                n = int(np.prod(shape[1:])); nw = (n + 1) // 2
                v = reg[0:parts, s_.off:s_.off + nw].bitcast(BF16)[:, 0:n]; s_.off += nw
                assert s_.off <= RW
                return s_._shape(v, shape)
        ra = RA()
        xin = ra.f32([128, cfg.D]); b_xin = RB("xin")
        ra = RA()
        hq = [ra.f32([128, NT])]; hk = [ra.f32([128, NT])]; hg_ = [ra.f32([128, NT])]; hf = [ra.f32([128, NT])]
        hv = [ra.bf([128, 3, 128])]
        b_hp = [RB("hp0")]
        bT = ra.f32([128, NT]); eb = ra.f32([128, NT]); enb = ra.f32([128, NT]); erc = ra.f32([128, NT])
        qtil = ra.bf([128, NT]); ktil = ra.bf([128, NT]); khT = ra.f32([128, NT])
        khat = ra.bf([128, 3, 128]); scm = ra.bf([128, 3, 128])
        sbf = ra.bf([128, 2, 128]); osq = ra.bf([128, NT]); otmp = ra.f32([128, NT])
        b_hr = RB("hrest"); b_sbf = [RB("sbf0"), RB("sbf1")]; b_scm = RB("scm"); b_khat = RB("khat")
        ra = RA()
        uT = ra.f32([128, KS, NT]); b_uT = [RB(f"uT{k}") for k in range(KS)]
        ubf = [ra.bf([128, NT]) for i in range(2)]; b_ubf = [RB("ubf0"), RB("ubf1")]
        Uall = ra.bf([128, GH, NW]); b_U = RB("Uall")
        Wall = ra.f32([64, 2, GH, NW], parts=64); b_W = RB("Wall")
        Sbf = ra.bf([64, 2, GH, NW], parts=64); b_S = RB("Sbf")
        Yw = ra.bf([128, GH, NW]); b_Yw = RB("Yw")
        sA = ra.f32([64, 2, GH, 2], parts=64); sB = ra.f32([64, 2, GH, 2], parts=64); b_sc = RB("scan")
        yact = ra.f32([128, KS, NT]); b_ya = [RB(f"ya{k}") for k in range(KS)]
        yabf = ra.bf([128, KS, NT]); b_yb = [RB(f"yb{k}") for k in range(KS)]
        ypre = ra.f32([128, NT]); sgl = ra.f32([128, NT]); b_yt = RB("ytmp")
        ra = RA()
        hid = ra.bf([128, KF, NT]); b_hid = [RB(f"hid{k}") for k in range(KF)]
        aext = [ra.f32([128, 2, TT + 2]) for i in range(2)]; cacc = [ra.f32([128, 2, TT]) for i in range(2)]
        csg = [ra.f32([128, 2, TT]) for i in range(2)]; b_ff = [RB("ff0"), RB("ff1")]

        def phase():
            S.barrier("dve", tuple(reg_bufs))

        def V(fn, R=(), W=()): return S.op("dve", fn, R=R, W=W)
        def A(fn, R=(), W=()): return S.op("act", fn, R=R, W=W)
        def Pl(fn, R=(), W=()): return S.op("pool", fn, R=R, W=W)
        def MM(fn, R=(), W=(), sig=False): return S.op("pe", fn, R=R, W=W, sig=sig)

        def load_slab(src_ap, nk, cw, ready, parts=128):
            i = self.ws_i % NS; self.ws_i += 1
            view = wsl[i][0:parts, 0:nk * cw].rearrange("p (k c) -> p k c", c=cw)
            S.dma("qs", view, src_ap, R=tuple(ready), W=(b_wsl[i],))
            return view, b_wsl[i]

        def rmsnorm_to_xn(gam):
            for k in range(KD):
                Pl(lambda e, k=k: e.tensor_tensor(out=xn[:, k, :], in0=hT[:, k, :], in1=hT[:, k, :], op=ALU.mult), R=(b_h[k],), W=(b_xn[k],))
            ps, pb = self.bank(0)
            for k in range(KD):
                MM(lambda e, k=k: e.matmul(ps[:, 0:NT], ones_bf[:], xn[:, k, :], start=(k == 0), stop=(k == KD - 1)),
                   R=(b_xn[k], bcn), W=(pb,), sig=(k == KD - 1))
            A(lambda e: e.activation(out=std[:], in_=ps[:, 0:NT], func=AF.Sqrt, scale=1.0 / cfg.D, bias=cst["eps"][:]), R=(pb, bcn), W=(b_std,))
            V(lambda e: e.reciprocal(out=rstd[:], in_=std[:]), R=(b_std,), W=(b_std,))
            for k in range(KD):
                V(lambda e, k=k: e.scalar_tensor_tensor(out=xn[:, k, :], in0=hT[:, k, :], scalar=gam(k), in1=rstd[:], op0=ALU.mult, op1=ALU.mult),
                  R=(b_h[k], b_std, bcn), W=(b_xn[k],))

        lbv, omlv, nomlv, b_lb = self.lb
        idf = cst["ident"]

        for ti in range(cfg.NTILE):
            phase()
            for tg in range(3):
                c0 = tg * 128
                segs = []
                c = c0
                while c < c0 + 128:
                    s_ = c // TT; j = c % TT; n = min(TT - j, c0 + 128 - c)
                    segs.append((c - c0, s_, ti * TT + j, n)); c += n
                for (po, s_, tok, n) in segs:
                    S.dma("qs", xin[po:po + n, :], self.dram["xp"][s_, tok:tok + n, :], W=(b_xin,))
                for k in range(KD):
                    ps, pb = self.bank(k % 4)
                    MM(lambda e, k=k, ps=ps: e.matmul(ps[:, 0:128], xin[:, k * 128:(k + 1) * 128], idf[:], start=True, stop=True),
                       R=(b_xin, bcn), W=(pb,), sig=True)
                    A(lambda e, k=k, ps=ps, c0=c0: e.activation(out=hT[:, k, c0:c0 + 128], in_=ps[:, 0:128], func=AF.Copy), R=(pb,), W=(b_h[k],))
            if self.stop_after == "stage0":
                self.write_out(hT, b_h, xin, b_xin, ti); continue
            stopped = False
            for l in range(L):
                sc = self.scr[l]
                rmsnorm_to_xn(lambda k: cst["g_mix"][:, l, k:k + 1])
                if self.stop_after == "norm1" and l == 0:
                    self.debug_dump_bf(xn, b_xn, ti); stopped = True; break
                phase()
                for hd in range(HH):
                    pp = 0
                    wv, wb = load_slab(sc["head"][hd], KD, 512, (self.slab_ready[("head", l, hd)],))
                    pq, bq = self.bank(0); pz, bz = self.bank(1); pg, bg = self.bank(2); pv, bv = self.bank(3)
                    for (pst, bst, c0) in ((pq, bq, 0), (pz, bz, 128), (pg, bg, 384)):
                        for k in range(KD):
                            MM(lambda e, k=k, pst=pst, c0=c0: e.matmul(pst[:, 0:NT], wv[:, k, c0:c0 + 128], xn[:, k, :], start=(k == 0), stop=(k == KD - 1)),
                               R=(wb, b_xn[k]), W=(bst,), sig=(k == KD - 1))
                    for tg in range(3):
                        for k in range(KD):
                            MM(lambda e, k=k, tg=tg: e.matmul(pv[:, tg * 128:(tg + 1) * 128], xn[:, k, tg * 128:(tg + 1) * 128], wv[:, k, 256:384], start=(k == 0), stop=(k == KD - 1)),
                               R=(wb, b_xn[k]), W=(bv,), sig=(k == KD - 1 and tg == 2))
                    bp = b_hp[pp]
                    A(lambda e: e.activation(out=hq[pp][:], in_=pq[:, 0:NT], func=AF.Silu), R=(bq,), W=(bp,))
                    A(lambda e: e.activation(out=hg_[pp][:], in_=pg[:, 0:NT], func=AF.Silu), R=(bg,), W=(bp,))
                    A(lambda e: e.activation(out=hk[pp][:], in_=pz[:, 0:NT], func=AF.Sigmoid), R=(bz,), W=(bp,))
                    V(lambda e: e.tensor_copy(out=hv[pp][:], in_=pv[:, 0:NT].rearrange("p (a b) -> p a b", b=128)), R=(bv,), W=(bp,))
                    V(lambda e: e.tensor_scalar(out=hf[pp][:], in0=hk[pp][:], scalar1=omlv[:, l, hd:hd + 1], scalar2=lbv[:, l, hd:hd + 1], op0=ALU.mult, op1=ALU.add), R=(bp, b_lb), W=(bp,))
                    V(lambda e: e.tensor_scalar(out=hf[pp][:], in0=hf[pp][:], scalar1=F_FLOOR, scalar2=None, op0=ALU.max), R=(bp,), W=(bp,))
                    V(lambda e: e.tensor_scalar(out=hk[pp][:], in0=hk[pp][:], scalar1=nomlv[:, l, hd:hd + 1], scalar2=omlv[:, l, hd:hd + 1], op0=ALU.mult, op1=ALU.add), R=(bp, b_lb), W=(bp,))
                    A(lambda e: e.activation(out=hf[pp][:], in_=hf[pp][:], func=AF.Ln), R=(bp,), W=(bp,))
                    V(lambda e: e.tensor_tensor_scan(out=bT[:], data0=cst["cmask"][:], data1=hf[pp][:], initial=0.0, op0=ALU.mult, op1=ALU.add), R=(bp, bcn), W=(b_hr,))
                    A(lambda e: e.activation(out=eb[:], in_=bT[:], func=AF.Exp), R=(b_hr,), W=(b_hr,))
                    A(lambda e: e.activation(out=enb[:], in_=bT[:], func=AF.Exp, scale=-1.0), R=(b_hr,), W=(b_hr,))
                    for c in range(6):
                        A(lambda e, c=c: e.activation(out=erc[:, c * 64:(c + 1) * 64], in_=bT[:, c * 64:(c + 1) * 64], func=AF.Exp, scale=-1.0, bias=bT[:, c * 64 + 63:c * 64 + 64]), R=(b_hr,), W=(b_hr,))
                    V(lambda e: e.tensor_tensor(out=qtil[:], in0=hq[pp][:], in1=eb[:], op=ALU.mult), R=(bp, b_hr), W=(b_hr,))
                    V(lambda e: e.tensor_tensor(out=ktil[:], in0=hk[pp][:], in1=enb[:], op=ALU.mult), R=(bp, b_hr), W=(b_hr,))
                    V(lambda e: e.tensor_tensor(out=khT[:], in0=hk[pp][:], in1=erc[:], op=ALU.mult), R=(bp, b_hr), W=(b_hr,))
                    pk, bk = self.bank(4); psc, bsc = self.bank(5); pkv, bkv = self.bank(6); po, bo_ = self.bank(7)
                    for tg in range(3):
                        MM(lambda e, tg=tg: e.matmul(pk[:, tg * 128:(tg + 1) * 128], khT[:, tg * 128:(tg + 1) * 128], idf[:], start=True, stop=True), R=(b_hr, bcn), W=(bk,), sig=(tg == 2))
                    A(lambda e: e.activation(out=khat[:], in_=pk[:, 0:NT].rearrange("p (a b) -> p a b", b=128), func=AF.Copy), R=(bk,), W=(b_khat,))
                    for tg in range(3):
                        MM(lambda e, tg=tg: e.matmul(psc[:, tg * 128:(tg + 1) * 128], ktil[:, tg * 128:(tg + 1) * 128], qtil[:, tg * 128:(tg + 1) * 128], start=True, stop=True), R=(b_hr,), W=(bsc,), sig=(tg == 2))
                    for tg in range(3):
                        V(lambda e, tg=tg: e.tensor_tensor(out=scm[:, tg, :], in0=psc[:, tg * 128:(tg + 1) * 128], in1=cst["tri2"][:], op=ALU.mult), R=(bsc, bcn), W=(b_scm,))
                    for c in range(6):
                        s_ = c // 3; tg = c // 2; j = c % 2; r0 = 64 * j
                        bs_ = b_hst[l][s_][hd]
                        if c % 3 == 0:
                            A(lambda e, s_=s_: e.activation(out=sbf[:, s_, :], in_=hst[:, l, s_, hd, :], func=AF.Copy), R=(bs_,), W=(b_sbf[s_],))
                        MM(lambda e, c=c, s_=s_: e.matmul(po[:, c * 64:(c + 1) * 64], sbf[:, s_, :], qtil[:, c * 64:(c + 1) * 64], start=True, stop=False), R=(b_sbf[s_], b_hr), W=(bo_,))
                        MM(lambda e, c=c, tg=tg, r0=r0: e.matmul(po[:, c * 64:(c + 1) * 64], hv[pp][r0:r0 + 64, tg, :], scm[r0:r0 + 64, tg, r0:r0 + 64], start=False, stop=True), R=(bp, b_scm), W=(bo_,), sig=(c == 5))
                        MM(lambda e, tg=tg, r0=r0, j=j: e.matmul(pkv[:, j * 128:(j + 1) * 128], khat[r0:r0 + 64, tg, :], hv[pp][r0:r0 + 64, tg, :], start=True, stop=True), R=(b_khat, bp), W=(bkv,), sig=True)
                        V(lambda e, c=c, s_=s_, j=j: e.scalar_tensor_tensor(out=hst[:, l, s_, hd, :], in0=hst[:, l, s_, hd, :], scalar=eb[:, c * 64 + 63:c * 64 + 64], in1=pkv[:, j * 128:(j + 1) * 128], op0=ALU.mult, op1=ALU.add), R=(bkv, b_hr, bs_), W=(bs_,))
                        if c % 3 != 2:
                            A(lambda e, s_=s_: e.activation(out=sbf[:, s_, :], in_=hst[:, l, s_, hd, :], func=AF.Copy), R=(bs_,), W=(b_sbf[s_],))
                    A(lambda e: e.activation(out=osq[:], in_=po[:, 0:NT], func=AF.Square), R=(bo_,), W=(b_hr,))
                    MM(lambda e: e.matmul(pk[:, 0:NT], ones_bf[:], osq[:], start=True, stop=True), R=(b_hr, bcn), W=(bk,), sig=True)
                    A(lambda e: e.activation(out=otmp[:], in_=pk[:, 0:NT], func=AF.Sqrt, scale=1.0 / 128.0, bias=cst["eps"][:]), R=(bk, bcn), W=(b_hr,))
                    V(lambda e: e.reciprocal(out=otmp[:], in_=otmp[:]), R=(b_hr,), W=(b_hr,))
                    V(lambda e: e.tensor_tensor(out=otmp[:], in0=po[:, 0:NT], in1=otmp[:], op=ALU.mult), R=(bo_, b_hr), W=(b_hr,))
                    V(lambda e: e.scalar_tensor_tensor(out=mix[:, hd, :], in0=otmp[:], scalar=cst["hg_gain"][:, l, hd:hd + 1], in1=hg_[pp][:], op0=ALU.mult, op1=ALU.mult), R=(b_hr, bp, bcn), W=(b_mix[hd],))
                if self.stop_after == "hgrn" and l == 0:
                    self.debug_dump_bf(mix, b_mix, ti); stopped = True; break
                phase()
                for j in range(cfg.NU):
                    wv, wb = load_slab(sc["u"][j], KD, cfg.UW, (self.slab_ready[("u", l, j)],))
                    for bi in range(cfg.UW // 128):
                        blk = j * (cfg.UW // 128) + bi
                        ps, pb = self.bank(blk % 4)
                        for k in range(KD):
                            MM(lambda e, k=k, bi=bi, ps=ps: e.matmul(ps[:, 0:NT], wv[:, k, bi * 128:(bi + 1) * 128], xn[:, k, :], start=(k == 0), stop=(k == KD - 1)),
                               R=(wb, b_xn[k]), W=(pb,), sig=(k == KD - 1))
                        A(lambda e, blk=blk, ps=ps: e.activation(out=uT[:, blk, :], in_=ps[:, 0:NT], func=AF.Copy), R=(pb,), W=(b_uT[blk],))
                if self.stop_after == "s5a" and l == 0:
                    self.debug_dump_bf(mix, b_mix, ti); stopped = True; break
                r8re, r8im, r8imn, br8 = self.r8[l]
                selbf = cst["sel_bf"]
                lvl = {"s5b": 1, "s5c": 2, "s5d": 3, "s5e": 4, "s5f": 5}.get(self.stop_after, 99)
                for hf_ in range(NHALF):
                    g0 = hf_ * GH
                    Bv, Bb_ = load_slab(sc["B"][:, g0:g0 + GH, :], GH, 128, ())
                    for gb in range(GH // 8):
                        blk = (g0 // 8) + gb
                        ub = ubf[blk % 2]; bub = b_ubf[blk % 2]
                        V(lambda e, blk=blk, ub=ub: e.tensor_copy(out=ub[:], in_=uT[:, blk, :]), R=(b_uT[blk],), W=(bub,))
                        ps, pb = self.bank(blk % 2)
                        for jg in range(8):
                            for s_ in range(8):
                                MM(lambda e, jg=jg, s_=s_, ps=ps, ub=ub: e.matmul(ps[:, jg * NW:(jg + 1) * NW], selbf[:, jg, s_, :], ub[:, s_:NT:8], start=(s_ == 0), stop=(s_ == 7)),
                                   R=(bub, bcn), W=(pb,), sig=(s_ == 7 and jg == 7))
                        A(lambda e, gb=gb, ps=ps: e.activation(out=Uall[:, gb * 8:(gb + 1) * 8, :], in_=ps[:, 0:8 * NW].rearrange("p (g c) -> p g c", c=NW), func=AF.Copy), R=(pb,), W=(b_U,))
                    if lvl < 2:
                        continue
                    for g4 in range(GH // 4):
                        ps, pb = self.bank(2 + g4 % 2)
                        for gi in range(4):
                            g = g4 * 4 + gi
                            for ri in range(2):
                                MM(lambda e, g=g, gi=gi, ri=ri, ps=ps: e.matmul(ps[0:64, (gi * 2 + ri) * NW:(gi * 2 + ri + 1) * NW], Bv[:, g, ri * 64:(ri + 1) * 64], Uall[:, g, :], start=True, stop=True),
                                   R=(Bb_, b_U), W=(pb,), sig=(gi == 3 and ri == 1))
                        V(lambda e, g4=g4, ps=ps: e.tensor_copy(out=Wall[:, :, g4 * 4:(g4 + 1) * 4, :].rearrange("p r g c -> p g r c"), in_=ps[0:64, 0:8 * NW].rearrange("p (g r c) -> p g r c", r=2, c=NW)), R=(pb,), W=(b_W,))
                    if lvl < 3:
                        continue
                    cs_ = s5c[:, l, :, g0:g0 + GH, :]
                    Wv = Wall[:].rearrange("p r g (s m) -> p r g s m", s=2)
                    Sv = Sbf[:].rearrange("p r g (s m) -> p r g s m", s=2)
                    Pl(lambda e: e.tensor_copy(out=Sv[:, :, :, :, 0], in_=cs_), R=(b_s5c[l],), W=(b_S,))
                    for m in range(NWS):
                        prev = cs_ if m == 0 else Wv[:, :, :, :, m - 1]
                        cur = Wv[:, :, :, :, m]
                        rb = (b_W, br8, b_s5c[l])
                        Pl(lambda e, prev=prev: e.tensor_tensor(out=sA[:], in0=prev, in1=r8re[:, g0:g0 + GH, :].unsqueeze(1).broadcast_to([64, 2, GH, 2]), op=ALU.mult), R=rb, W=(b_sc,))
                        Pl(lambda e, prev=prev: e.tensor_tensor(out=sB[:, 0], in0=prev[:, 1], in1=r8imn[:, g0:g0 + GH, :], op=ALU.mult), R=rb, W=(b_sc,))
                        Pl(lambda e, prev=prev: e.tensor_tensor(out=sB[:, 1], in0=prev[:, 0], in1=r8im[:, g0:g0 + GH, :], op=ALU.mult), R=rb, W=(b_sc,))
                        Pl(lambda e: e.tensor_tensor(out=sA[:], in0=sA[:], in1=sB[:], op=ALU.add), R=(b_sc,), W=(b_sc,))
                        Pl(lambda e, cur=cur: e.tensor_tensor(out=cur, in0=cur, in1=sA[:], op=ALU.add), R=(b_sc, b_W), W=(b_W,))
                    for s_ in range(2):
                        Pl(lambda e, s_=s_: e.tensor_copy(out=Sv[:, :, :, s_, 1:NWS], in_=Wv[:, :, :, s_, 0:NWS - 1]), R=(b_W,), W=(b_S,))
                    Pl(lambda e: e.tensor_copy(out=cs_, in_=Wv[:, :, :, :, NWS - 1]), R=(b_W, b_S), W=(b_s5c[l],))
                    if lvl < 4:
                        continue
                    Mv, Mb = load_slab(sc["M"][:, g0:g0 + GH, :], GH, 128, ())
                    i_ = self.ws_i % NS; self.ws_i += 1
                    Cv = wsl[i_][0:64, 0:2 * GH * 128].rearrange("p (r g c) -> p r g c", r=2, c=128)
                    S.dma("qs", Cv, sc["C"][:, :, g0:g0 + GH, :], W=(b_wsl[i_],))
                    Cre_v = Cv[:, 0]; Cim_v = Cv[:, 1]; Cre_b = b_wsl[i_]; Cim_b = b_wsl[i_]
                    for gb in range(GH // 8):
                        ps, pb = self.bank(4 + gb % 2)
                        for gi in range(8):
                            g = gb * 8 + gi
                            osl = ps[:, gi * NW:(gi + 1) * NW]
                            MM(lambda e, g=g, osl=osl: e.matmul(osl, Mv[:, g, :], Uall[:, g, :], start=True, stop=False), R=(Mb, b_U), W=(pb,))
                            MM(lambda e, g=g, osl=osl: e.matmul(osl, Cre_v[0:64, g, :], Sbf[:, 0, g, :], start=False, stop=False), R=(Cre_b, b_S), W=(pb,))
                            MM(lambda e, g=g, osl=osl: e.matmul(osl, Cim_v[0:64, g, :], Sbf[:, 1, g, :], start=False, stop=True), R=(Cim_b, b_S), W=(pb,), sig=(gi == 7))
                        A(lambda e, gb=gb, ps=ps: e.activation(out=Yw[:, gb * 8:(gb + 1) * 8, :], in_=ps[:, 0:8 * NW].rearrange("p (g c) -> p g c", c=NW), func=AF.Copy), R=(pb,), W=(b_Yw,))
                    if lvl < 5:
                        continue
                    for gb in range(GH // 8):
                        blk = (g0 // 8) + gb
                        ps, pb = self.bank(6 + gb % 2)
                        for t_ in range(8):
                            for jg in range(8):
                                MM(lambda e, t_=t_, jg=jg, gb=gb, ps=ps: e.matmul(ps[:, t_:NT:8], selbf[:, t_, jg, :], Yw[:, gb * 8 + jg, :], start=(jg == 0), stop=(jg == 7)),
                                   R=(b_Yw, bcn), W=(pb,), sig=(jg == 7 and t_ == 7))
                        V(lambda e, blk=blk, ps=ps: e.scalar_tensor_tensor(out=ypre[:], in0=uT[:, blk, :], scalar=cst["s5_d"][:, l, blk:blk + 1], in1=ps[:, 0:NT], op0=ALU.mult, op1=ALU.add), R=(pb, b_uT[blk], bcn), W=(b_yt,))
                        V(lambda e: e.tensor_tensor(out=sgl[:], in0=ypre[:], in1=ypre[:], op=ALU.mult), R=(b_yt,), W=(b_yt,))
                        V(lambda e: e.tensor_scalar(out=sgl[:], in0=sgl[:], scalar1=0.044715, scalar2=1.0, op0=ALU.mult, op1=ALU.add), R=(b_yt,), W=(b_yt,))
                        V(lambda e: e.tensor_tensor(out=sgl[:], in0=sgl[:], in1=ypre[:], op=ALU.mult), R=(b_yt,), W=(b_yt,))
                        A(lambda e: e.activation(out=sgl[:], in_=sgl[:], func=AF.Sigmoid, scale=1.5957691216057308), R=(b_yt,), W=(b_yt,))
                        V(lambda e, blk=blk: e.tensor_tensor(out=yact[:, blk, :], in0=ypre[:], in1=sgl[:], op=ALU.mult), R=(b_yt,), W=(b_ya[blk],))
                        V(lambda e, blk=blk: e.tensor_copy(out=yabf[:, blk, :], in_=yact[:, blk, :]), R=(b_ya[blk],), W=(b_yb[blk],))
                if lvl < 99:
                    if self.stop_after == "s5f" and ti == 0:
                        dt_ = self.sb("dbgtmp", [128, 4096])
                        bd = Buf("dbgtmp")
                        def dd(ap, parts, n, off, rb):
                            S.op("dve", lambda e: e.tensor_copy(out=dt_[0:parts, 0:n], in_=ap), R=rb + (bd,), W=(bd,))
                            S.dma("qs", self.dram["dbg2"][0:parts, off:off + n], dt_[0:parts, 0:n], R=(bd,), W=(Buf("z"),))
                            tk_ = bd.r[("qs", (S.qs["qs"].i - 1) % 8)] if False else None
                        dd(Uall[:].rearrange("p g c -> p (g c)"), 128, GH * NW, 0, (b_U,))
                        dd(Wall[:].rearrange("p r g c -> p (r g c)"), 64, 2 * GH * NW, 2048, (b_W,))
                        dd(Sbf[:].rearrange("p r g c -> p (r g c)"), 64, 2 * GH * NW, 6144, (b_S,))
                        dd(Yw[:].rearrange("p g c -> p (g c)"), 128, GH * NW, 10240, (b_Yw,))
                        dd(yact[:].rearrange("p k c -> p (k c)"), 128, KS * NT, 12288, tuple(b_ya))
                        dd(uT[:].rearrange("p k c -> p (k c)"), 128, KS * NT, 16384, tuple(b_uT))
                    self.debug_dump_bf(mix, b_mix, ti); stopped = True; break
                for j in range(cfg.NU):
                    wv, wb = load_slab(sc["glu"][j], KS, cfg.UW, (self.slab_ready[("glu", l, j)],))
                    for bi in range(cfg.UW // 128):
                        blk = j * (cfg.UW // 128) + bi
                        ps, pb = self.bank(blk % 4)
                        for k in range(KS):
                            MM(lambda e, k=k, bi=bi, ps=ps: e.matmul(ps[:, 0:NT], wv[:, k, bi * 128:(bi + 1) * 128], yabf[:, k, :], start=(k == 0), stop=(k == KS - 1)),
                               R=(wb, b_yb[k]), W=(pb,), sig=(k == KS - 1))
                        A(lambda e, blk=blk, ps=ps: e.activation(out=sgl[:], in_=ps[:, 0:NT], func=AF.Sigmoid, bias=cst["b_glu"][:, l, blk:blk + 1]), R=(pb, bcn), W=(b_yt,))
                        V(lambda e, blk=blk: e.tensor_tensor(out=yact[:, blk, :], in0=yact[:, blk, :], in1=sgl[:], op=ALU.mult), R=(b_yt, b_ya[blk]), W=(b_ya[blk],))
                for k in range(KS):
                    Pl(lambda e, k=k: e.tensor_tensor(out=yabf[:, k, :], in0=yact[:, k, :], in1=yact[:, k, :], op=ALU.mult), R=(b_ya[k], b_yb[k]), W=(b_yb[k],))
                ps, pb = self.bank(4)
                for k in range(KS):
                    MM(lambda e, k=k: e.matmul(ps[:, 0:NT], ones_bf[:], yabf[:, k, :], start=(k == 0), stop=(k == KS - 1)), R=(b_yb[k], bcn), W=(pb,), sig=(k == KS - 1))
                A(lambda e: e.activation(out=std[:], in_=ps[:, 0:NT], func=AF.Sqrt, scale=1.0 / cfg.DS5, bias=cst["eps"][:]), R=(pb, bcn), W=(b_std,))
                V(lambda e: e.reciprocal(out=rstd[:], in_=std[:]), R=(b_std,), W=(b_std,))
                for k in range(KS):
                    V(lambda e, k=k: e.scalar_tensor_tensor(out=mix[:, HH + k, :], in0=yact[:, k, :], scalar=cst["s5_gain"][:, l, k:k + 1], in1=rstd[:], op0=ALU.mult, op1=ALU.mult), R=(b_ya[k], b_std, bcn), W=(b_mix[HH + k],))
                if self.stop_after == "mix" and l == 0:
                    self.debug_dump_bf(mix, b_mix, ti); stopped = True; break
                nob = cfg.OW // 128
                for j in range(cfg.NO):
                    wv, wb = load_slab(sc["out"][j], KD, cfg.OW, (self.slab_ready[("out", l, j)],))
                    for bi in range(nob):
                        ob = j * nob + bi
                        ps, pb = self.bank(ob % 4)
                        for k in range(KD):
                            MM(lambda e, k=k, bi=bi, ps=ps: e.matmul(ps[:, 0:NT], wv[:, k, bi * 128:(bi + 1) * 128], mix[:, k, :], start=(k == 0), stop=(k == KD - 1)),
                               R=(wb, b_mix[k]), W=(pb,), sig=(k == KD - 1))
                        V(lambda e, ob=ob, ps=ps: e.tensor_tensor(out=hT[:, ob, :], in0=hT[:, ob, :], in1=ps[:, 0:NT], op=ALU.add), R=(pb, b_h[ob]), W=(b_h[ob],))
                if self.stop_after == "attn" and l == 0:
                    self.write_out(hT, b_h, xin, b_xin, ti); stopped = True; break
                phase()
                rmsnorm_to_xn(lambda k: cst["g_ffn"][:, l, k:k + 1])
                nfb = cfg.FW // 128
                for j in range(cfg.NF):
                    gv, gbuf = load_slab(sc["gate"][j], KD, cfg.FW, (self.slab_ready[("gate", l, j)],))
                    uv, ubuf = load_slab(sc["up"][j], KD, cfg.FW, (self.slab_ready[("up", l, j)],))
                    for bi in range(nfb):
                        fb = j * nfb + bi; pp = fb % 2
                        pa, ba = self.bank(2 * pp); pu, bu = self.bank(2 * pp + 1)
                        for k in range(KD):
                            MM(lambda e, k=k, bi=bi, pa=pa: e.matmul(pa[:, 0:NT], gv[:, k, bi * 128:(bi + 1) * 128], xn[:, k, :], start=(k == 0), stop=(k == KD - 1)), R=(gbuf, b_xn[k]), W=(ba,), sig=(k == KD - 1))
                        for k in range(KD):
                            MM(lambda e, k=k, bi=bi, pu=pu: e.matmul(pu[:, 0:NT], uv[:, k, bi * 128:(bi + 1) * 128], xn[:, k, :], start=(k == 0), stop=(k == KD - 1)), R=(ubuf, b_xn[k]), W=(bu,), sig=(k == KD - 1))
                        ae = aext[pp]; ca = cacc[pp]; cg = csg[pp]; bf_ = b_ff[pp]
                        cw = cst["conv_w"]
                        Pl(lambda e, fb=fb, ae=ae: e.tensor_copy(out=ae[:, :, 0:2], in_=cvc[:, l, fb, :, :]), R=(b_cvc[l],), W=(bf_,))
                        A(lambda e, ae=ae, pa=pa: e.activation(out=ae[:, :, 2:TT + 2], in_=pa[:, 0:NT].rearrange("p (s t) -> p s t", s=2), func=AF.Copy), R=(ba,), W=(bf_,))
                        Pl(lambda e, fb=fb, ae=ae: e.tensor_copy(out=cvc[:, l, fb, :, :], in_=ae[:, :, TT:TT + 2]), R=(bf_,), W=(b_cvc[l],))
                        V(lambda e, fb=fb, ae=ae, ca=ca: e.tensor_scalar(out=ca[:], in0=ae[:, :, 2:TT + 2], scalar1=cw[:, l, 2, fb:fb + 1], scalar2=cst["conv_b"][:, l, fb:fb + 1], op0=ALU.mult, op1=ALU.add), R=(bf_, bcn), W=(bf_,))
                        V(lambda e, fb=fb, ae=ae, ca=ca: e.scalar_tensor_tensor(out=ca[:], in0=ae[:, :, 1:TT + 1], scalar=cw[:, l, 1, fb:fb + 1], in1=ca[:], op0=ALU.mult, op1=ALU.add), R=(bf_, bcn), W=(bf_,))
                        V(lambda e, fb=fb, ae=ae, ca=ca: e.scalar_tensor_tensor(out=ca[:], in0=ae[:, :, 0:TT], scalar=cw[:, l, 0, fb:fb + 1], in1=ca[:], op0=ALU.mult, op1=ALU.add), R=(bf_, bcn), W=(bf_,))
                        A(lambda e, ca=ca, cg=cg: e.activation(out=cg[:], in_=ca[:], func=AF.Silu), R=(bf_,), W=(bf_,))
                        V(lambda e, fb=fb, cg=cg, pu=pu: e.tensor_tensor(out=hid[:, fb, :], in0=cg[:].rearrange("p s t -> p (s t)"), in1=pu[:, 0:NT], op=ALU.mult), R=(bf_, bu), W=(b_hid[fb],))
                for j in range(cfg.NO):
                    banks = [self.bank(4 + bi) for bi in range(nob)]
                    for pi, (k0, nk) in enumerate(cfg.KPIECES):
                        wv, wb = load_slab(sc["down"][j, pi, :, 0:nk, :], nk, cfg.OW, (self.slab_ready[("down", l, j, pi)],))
                        for bi in range(nob):
                            ps, pb = banks[bi]
                            for kk in range(nk):
                                kf_ = k0 + kk
                                MM(lambda e, kk=kk, kf_=kf_, bi=bi, ps=ps: e.matmul(ps[:, 0:NT], wv[:, kk, bi * 128:(bi + 1) * 128], hid[:, kf_, :], start=(kf_ == 0), stop=(kf_ == KF - 1)),
                                   R=(wb, b_hid[kf_]), W=(pb,), sig=(kk == nk - 1))
                    for bi in range(nob):
                        ob = j * nob + bi; ps, pb = banks[bi]
                        V(lambda e, ob=ob, ps=ps: e.tensor_tensor(out=hT[:, ob, :], in0=hT[:, ob, :], in1=ps[:, 0:NT], op=ALU.add), R=(pb, b_h[ob]), W=(b_h[ob],))
                if self.stop_after == "l0" and l == 0:
                    self.write_out(hT, b_h, xin, b_xin, ti); stopped = True; break
            if stopped:
                continue
            if True:
                phase()
                for k in range(KD):
                    Pl(lambda e, k=k: e.tensor_tensor(out=xn[:, k, :], in0=hT[:, k, :], in1=hT[:, k, :], op=ALU.mult), R=(b_h[k],), W=(b_xn[k],))
                ps, pb = self.bank(0)
                for k in range(KD):
                    MM(lambda e, k=k: e.matmul(ps[:, 0:NT], ones_bf[:], xn[:, k, :], start=(k == 0), stop=(k == KD - 1)), R=(b_xn[k], bcn), W=(pb,), sig=(k == KD - 1))
                A(lambda e: e.activation(out=std[:], in_=ps[:, 0:NT], func=AF.Sqrt, scale=1.0 / cfg.D, bias=cst["eps"][:]), R=(pb, bcn), W=(b_std,))
                V(lambda e: e.reciprocal(out=rstd[:], in_=std[:]), R=(b_std,), W=(b_std,))
                for k in range(KD):
                    V(lambda e, k=k: e.scalar_tensor_tensor(out=hT[:, k, :], in0=hT[:, k, :], scalar=cst["g_fin"][:, k:k + 1], in1=rstd[:], op0=ALU.mult, op1=ALU.mult), R=(b_h[k], b_std, bcn), W=(b_h[k],))
                self.write_out(hT, b_h, xin, b_xin, ti)

    def write_out(self, src, b_src, xin, b_xin, ti, nblk=None):
        cfg = self.cfg; S = self.S; idf = self.cst["ident"]; bcn = self.b_const
        nblk = cfg.KD if nblk is None else nblk
        for tg in range(3):
            c0 = tg * 128
            for k in range(nblk):
                ps, pb = self.bank(k % 4)
                S.op("pe", lambda e, k=k, ps=ps: e.matmul(ps[:, 0:128], src[:, k, c0:c0 + 128], idf[:], start=True, stop=True), R=(b_src[k], bcn), W=(pb,), sig=True)
                S.op("act", lambda e, k=k, ps=ps: e.activation(out=xin[:, k * 128:(k + 1) * 128], in_=ps[:, 0:128], func=AF.Copy), R=(pb,), W=(b_xin,))
            c = c0
            while c < c0 + 128:
                s_ = c // TT; j = c % TT; n = min(TT - j, c0 + 128 - c)
                tk = S.dma("qs", self.dram["out"][s_, ti * TT + j:ti * TT + j + n, 0:nblk * 128], xin[c - c0:c - c0 + n, 0:nblk * 128], R=(b_xin,), W=(Buf("o"),))
                self.out_tks.append(tk)
                c += n

    def debug_dump_bf(self, src, b_src, ti):
        cfg = self.cfg; S = self.S
        if not hasattr(self, "dbg_t"):
            self.dbg_t = self.sb("dbg_t", [128, cfg.KD, NT]); self.dbg_b = [Buf(f"dbg{k}") for k in range(cfg.KD)]
        for k in range(cfg.KD):
            S.op("dve", lambda e, k=k: e.tensor_copy(out=self.dbg_t[:, k, :], in_=src[:, k, :]), R=(b_src[k],), W=(self.dbg_b[k],))
        xin = self.xin_dbg if hasattr(self, "xin_dbg") else None
        if xin is None:
            self.xin_dbg = self.sb("xin_dbg", [128, cfg.D]); self.xin_dbg_b = Buf("xind")
        self.write_out(self.dbg_t, self.dbg_b, self.xin_dbg, self.xin_dbg_b, ti)


_CACHE = {}


def run_cfg(inputs, cfg, stop_after=None):
    maps = host_prep(inputs, cfg)
    key = (cfg.D, cfg.DFF, cfg.SEQ, cfg.L, stop_after)
    if key not in _CACHE:
        _CACHE[key] = Builder(cfg, stop_after).build()
    nc = _CACHE[key]
    ncores = cfg.NB // 2
    res = run_bass_kernel_spmd(nc, maps, core_ids=list(range(ncores)))
    if stop_after == "s5p":
        return np.asarray(res.results[0]["dbg"])
    if stop_after == "s5f":
        return np.asarray(res.results[0]["dbg2"])
    outs = [np.asarray(r["out"]) for r in res.results]
    full = np.concatenate([o[:, 64:, :] for o in outs], axis=0)
    return np.ascontiguousarray(full.astype(np.float32))


def kernel(**inputs):
    cfg = Cfg()
    return run_cfg(inputs, cfg)
```

```python
import math
from contextlib import ExitStack
import numpy as np
import concourse.bass as bass
import concourse.mybir as mybir
from concourse.bass_utils import run_bass_kernel_spmd

F32 = mybir.dt.float32
BF16 = mybir.dt.bfloat16
I32 = mybir.dt.int32
AF = mybir.ActivationFunctionType
ALU = mybir.AluOpType
P = 128
EPS = 1e-6
F_FLOOR = 1e-6
TT = 192
NT = 2 * TT
NW = NT // 8
NWS = TT // 8


class Cfg:
    def __init__(self, D=2048, DFF=5632, SEQ=2048, L=2, NB=16):
        self.D = D; self.DFF = DFF; self.SEQ = SEQ; self.L = L; self.NB = NB
        self.DHG = D // 2; self.HH = self.DHG // 128
        self.DS5 = D - self.DHG; self.G = self.DS5 // 16; self.GB = self.G // 8
        self.KS = self.DS5 // 128
        self.DIN = 4 * self.DHG + self.DS5
        self.KD = D // 128; self.KF = DFF // 128
        self.LP = SEQ + 64
        assert self.LP % TT == 0
        self.NTILE = self.LP // TT
        self.UW = min(512, self.DS5); self.NU = self.DS5 // self.UW
        self.OW = min(512, D); self.NO = D // self.OW
        self.FW = 512 if DFF % 512 == 0 else (384 if DFF % 384 == 0 else 128)
        self.NF = DFF // self.FW
        self.KPIECES = [(k0, min(16, self.KF - k0)) for k0 in range(0, self.KF, 16)]


class Buf:
    __slots__ = ("name", "w", "r")

    def __init__(self, name):
        self.name = name; self.w = None; self.r = {}


class Eng:
    def __init__(self, name, h, kind):
        self.name = name; self.h = h; self.kind = kind; self.n = 0; self.waited = {}


class DQ:
    def __init__(self, name, eng, sems):
        self.name = name; self.eng = eng; self.sems = sems; self.i = 0


CAP = 20000


class Sync:
    def __init__(self, nc, es):
        self.nc = nc; self.es = es
        self.engs = {}
        self.qs = {}
        self.semlists = {}
        self.trace = {}

    def add_eng(self, name, h, kind="c"):
        self.engs[name] = Eng(name, h, kind); self.semlists[name] = []

    def add_q(self, name, engname, K=8):
        sems = [self.es.enter_context(self.nc.semaphore(f"q_{name}_{i}")) for i in range(K)]
        self.qs[name] = DQ(name, self.engs[engname], sems)

    def _sem(self, name, idx):
        lst = self.semlists[name]
        while len(lst) <= idx:
            lst.append(self.es.enter_context(self.nc.semaphore(f"e_{name}_{len(lst)}")))
        return lst[idx]

    def _wait(self, E, k, v):
        if v <= 0 or E.waited.get(k, 0) >= v:
            return
        if k[0] in self.qs:
            E.h.wait_ge(self.qs[k[0]].sems[k[1]], 16 * v)
        else:
            E.h.wait_ge(self._sem(k[0], k[1]), v)
        E.waited[k] = v
        self.trace.setdefault(E.name, []).append(("w", k, v))

    def _deps(self, E, R, W):
        deps = {}
        for b in R:
            t = b.w
            if t is not None and deps.get(t[0], 0) < t[1]:
                deps[t[0]] = t[1]
        for b in W:
            t = b.w
            if t is not None and deps.get(t[0], 0) < t[1]:
                deps[t[0]] = t[1]
            for t in b.r.values():
                if deps.get(t[0], 0) < t[1]:
                    deps[t[0]] = t[1]
        for k, v in deps.items():
            if k[0] == E.name and E.kind == "pe":
                continue
            self._wait(E, k, v)

    def _mark(self, tk, R, W):
        for b in R:
            o = b.r.get(tk[0])
            if o is None or o[1] < tk[1]:
                b.r[tk[0]] = tk
        for b in W:
            b.w = tk; b.r = {}

    def _tk(self, en, n):
        return ((en, (n - 1) // CAP), (n - 1) % CAP + 1)

    def op(self, en, fn, R=(), W=(), sig=True):
        E = self.engs[en]
        self._deps(E, R, W)
        ins = fn(E.h)
        if sig:
            E.n += 1
            tk = self._tk(en, E.n)
            ins.then_inc(self._sem(en, tk[0][1]), 1)
            self.trace.setdefault(en, []).append(("s", tk[0], tk[1]))
        else:
            tk = self._tk(en, E.n + 1)
        self._mark(tk, R, W)
        return tk

    def dma(self, qn, out, in_, R=(), W=()):
        q = self.qs[qn]; E = q.eng
        self._deps(E, R, W)
        K = len(q.sems); i = q.i; q.i += 1
        lane = i % K; cnt = i // K + 1
        self._wait(E, (qn, lane), cnt - 1)
        ins = E.h.dma_start(out=out, in_=in_)
        ins.then_inc(q.sems[lane], 16)
        tk = ((qn, lane), cnt)
        self.trace.setdefault(E.name, []).append(("s", tk[0], tk[1]))
        self._mark(tk, R, W)
        return tk

    def check_deadlock(self):
        pc = {e: 0 for e in self.trace}
        sem = {}
        progress = True
        while progress:
            progress = False
            for e, tr in self.trace.items():
                while pc[e] < len(tr):
                    kind, k, v = tr[pc[e]]
                    if kind == "w":
                        if sem.get(k, 0) >= v:
                            pc[e] += 1; progress = True
                        else:
                            break
                    else:
                        assert sem.get(k, 0) == v - 1, (e, k, v, sem.get(k, 0))
                        sem[k] = v; pc[e] += 1; progress = True
        stuck = {e: (pc[e], len(tr), tr[pc[e]]) for e, tr in self.trace.items() if pc[e] < len(tr)}
        return stuck

    def final_wait(self, en, tks):
        E = self.engs[en]
        for tk in tks:
            self._wait(E, tk[0], tk[1])
        for qn, q in self.qs.items():
            K = len(q.sems)
            for lane in range(K):
                cnt = (q.i - lane + K - 1) // K
                self._wait(E, (qn, lane), cnt)

    def barrier(self, en, bufs):
        tk = self.op(en, lambda e: e.memset(self.bar_t[:], 0.0), R=tuple(bufs), W=tuple(bufs) + (self.bar_b,))
        for E in self.engs.values():
            self._wait(E, tk[0], tk[1])


def make_consts():
    c = {}
    c["ident"] = np.eye(128, dtype=np.float32)
    c["ones"] = np.ones((128, 128), np.float32)
    s = np.arange(128)
    same = (s[:, None] // 64) == (s[None, :] // 64)
    c["tri2"] = (same & (s[:, None] <= s[None, :])).astype(np.float32)
    sel = np.zeros((128, 8, 8, 128), np.float32)
    for a_ in range(8):
        for b_ in range(8):
            for hi in range(16):
                sel[16 * a_ + hi, a_, b_, 16 * b_ + hi] = 1.0
    c["sel"] = sel
    a = np.arange(128) // 16
    c["mmask"] = (a[None, :] >= a[:, None]).astype(np.float32)
    cm = np.ones((128, NT), np.float32); cm[:, ::64] = 0.0
    c["cmask"] = cm
    return c


def host_prep(inp, cfg):
    L = cfg.L; f = np.float32
    sh = {}
    perm = []
    for h in range(cfg.HH):
        for part in range(4):
            perm.extend(range(part * cfg.DHG + h * 128, part * cfg.DHG + (h + 1) * 128))
    perm.extend(range(4 * cfg.DHG, cfg.DIN))
    sh["w_in"] = np.ascontiguousarray(np.asarray(inp["w_in"], f)[:, :, perm])
    for k in ("w_glu", "w_out", "w_ffn_gate", "w_ffn_up", "w_ffn_down"):
        sh[k] = np.ascontiguousarray(np.asarray(inp[k], f))

    def fm(v, nblk):
        return np.ascontiguousarray(np.asarray(v, f).reshape(L, nblk, 128).transpose(2, 0, 1))
    sh["g_mix"] = fm(inp["norm_mix"], cfg.KD)
    sh["g_ffn"] = fm(inp["norm_ffn"], cfg.KD)
    sh["g_fin"] = np.ascontiguousarray(np.asarray(inp["final_norm"], f).reshape(cfg.KD, 128).T)
    sh["hg_gain"] = fm(inp["hg_norm"], cfg.HH)
    sh["s5_gain"] = fm(inp["s5_norm"], cfg.KS)
    sh["s5_d"] = fm(inp["s5_d"], cfg.KS)
    sh["b_glu"] = fm(inp["b_glu"], cfg.KS)
    sh["lbl_fm"] = fm(inp["lb_logits"], cfg.HH)
    cw = np.asarray(inp["ffn_conv_w"], f)
    sh["conv_w"] = np.ascontiguousarray(cw.reshape(L, 3, cfg.KF, 128).transpose(3, 0, 1, 2))
    sh["conv_b"] = fm(inp["ffn_conv_b"], cfg.KF)
    sh["lam_re"] = np.ascontiguousarray(np.asarray(inp["s5_lambda_re"], f).transpose(2, 0, 1))
    sh["lam_im"] = np.ascontiguousarray(np.asarray(inp["s5_lambda_im"], f).transpose(2, 0, 1))
    sh["lstep"] = np.ascontiguousarray(np.broadcast_to(np.asarray(inp["s5_log_step"], f)[None], (64, L, cfg.G)))
    sh["b_re"] = np.ascontiguousarray(np.asarray(inp["s5_b_re"], f).transpose(2, 0, 1, 3))
    sh["b_im"] = np.ascontiguousarray(np.asarray(inp["s5_b_im"], f).transpose(2, 0, 1, 3))
    sh["c_re"] = np.ascontiguousarray(np.asarray(inp["s5_c_re"], f).transpose(3, 0, 1, 2))
    sh["c_im"] = np.ascontiguousarray(np.asarray(inp["s5_c_im"], f).transpose(3, 0, 1, 2))
    sh.update(make_consts())
    x = np.asarray(inp["x"], f); meta = np.asarray(inp["meta_tokens"], f)
    ncores = cfg.NB // 2
    maps = []
    for c in range(ncores):
        xp = np.zeros((2, cfg.LP, cfg.D), f)
        xp[:, 48:64] = meta[None]
        xp[:, 64:] = x[2 * c:2 * c + 2]
        m = dict(sh); m["xp"] = xp
        maps.append(m)
    return maps


def input_shapes(cfg):
    L = cfg.L
    return {
        "xp": [2, cfg.LP, cfg.D], "w_in": [L, cfg.D, cfg.DIN], "w_glu": [L, cfg.DS5, cfg.DS5],
        "w_out": [L, cfg.D, cfg.D], "w_ffn_gate": [L, cfg.D, cfg.DFF], "w_ffn_up": [L, cfg.D, cfg.DFF],
        "w_ffn_down": [L, cfg.DFF, cfg.D],
        "g_mix": [128, L, cfg.KD], "g_ffn": [128, L, cfg.KD], "g_fin": [128, cfg.KD],
        "hg_gain": [128, L, cfg.HH], "s5_gain": [128, L, cfg.KS], "s5_d": [128, L, cfg.KS],
        "b_glu": [128, L, cfg.KS], "lbl_fm": [128, L, cfg.HH],
        "conv_w": [128, L, 3, cfg.KF], "conv_b": [128, L, cfg.KF],
        "lam_re": [64, L, cfg.G], "lam_im": [64, L, cfg.G], "lstep": [64, L, cfg.G],
        "b_re": [64, L, cfg.G, 16], "b_im": [64, L, cfg.G, 16], "c_re": [64, L, cfg.G, 16], "c_im": [64, L, cfg.G, 16],
        "ident": [128, 128], "ones": [128, 128], "tri2": [128, 128],
        "sel": [128, 8, 8, 128], "mmask": [128, 128], "cmask": [128, NT],
    }


class Builder:
    def __init__(self, cfg, stop_after=None):
        self.cfg = cfg
        self.stop_after = stop_after

    def sb(self, name, shape, dt=F32):
        return self.es.enter_context(self.nc.sbuf_tensor(name, list(shape), dt))

    def build(self):
        cfg = self.cfg
        nc = bass.Bass("TRN2", target_bir_lowering=False)
        self.nc = nc
        with ExitStack() as es:
            self.es = es
            S = Sync(nc, es); self.S = S
            S.add_eng("pe", nc.tensor, "pe"); S.add_eng("act", nc.scalar); S.add_eng("dve", nc.vector)
            S.add_eng("pool", nc.gpsimd); S.add_eng("sp", nc.sync)
            S.add_q("qs", "sp", 8); S.add_q("qg", "pool", 8)
            S.bar_t = self.sb("bar_t", [128, 1]); S.bar_b = Buf("bar")
            self.dram = {}
            for k, shp in input_shapes(cfg).items():
                self.dram[k] = nc.dram_tensor(k, shp, F32, kind="ExternalInput").ap()
            self.dram["out"] = nc.dram_tensor("out", [2, cfg.LP, cfg.D], F32, kind="ExternalOutput").ap()
            if self.stop_after == "s5p":
                self.dram["dbg"] = nc.dram_tensor("dbg", [128, 8192], F32, kind="ExternalOutput").ap()
            if self.stop_after == "s5f":
                self.dram["dbg2"] = nc.dram_tensor("dbg2", [128, 32768], F32, kind="ExternalOutput").ap()
            self.psum = [es.enter_context(nc.psum_tensor(f"ps{i}", [128, 512], F32)) for i in range(8)]
            self.pbuf = [Buf(f"ps{i}") for i in range(8)]
            self.emit_all()
            stuck = S.check_deadlock()
            if stuck:
                raise RuntimeError(f"sync deadlock: {stuck}")
        return nc

    def bank(self, i):
        return (self.psum[i], self.pbuf[i])

    def emit_all(self):
        cfg = self.cfg; nc = self.nc; S = self.S; sb = self.sb
        L = cfg.L; KD = cfg.KD
        self.out_tks = []
        cst = {}
        self.cst = cst
        self.b_const = Buf("const")
        bc_ = self.b_const
        for key in ("g_mix", "g_ffn", "g_fin", "hg_gain", "s5_gain", "s5_d", "b_glu", "lbl_fm", "conv_w", "conv_b",
                    "ident", "ones", "tri2", "mmask", "cmask"):
            t = sb("c_" + key, input_shapes(cfg)[key], F32)
            S.dma("qs", t[:], self.dram[key], W=(bc_,))
            cst[key] = t
        cst["ones_bf"] = sb("ones_bf", [128, 128], BF16)
        cst["tri_bf"] = sb("tri_bf", [128, 128], BF16)
        cst["sel_bf"] = sb("sel_bf", [128, 8, 8, 128], BF16)
        cst["eps"] = sb("eps_t", [128, 1], F32)
        S.op("dve", lambda e: e.tensor_copy(out=cst["ones_bf"][:], in_=cst["ones"][:]), R=(bc_,), W=(bc_,))
        S.op("dve", lambda e: e.tensor_copy(out=cst["tri_bf"][:], in_=cst["tri2"][:]), R=(bc_,), W=(bc_,))
        S.op("dve", lambda e: e.memset(cst["eps"][:], EPS), W=(bc_,))
        with nc.sbuf_tensor("sel_tmp", [128, 8, 8, 128], F32) as selt:
            bt = Buf("selt")
            S.dma("qs", selt[:], self.dram["sel"], W=(bt,))
            S.op("dve", lambda e: e.tensor_copy(out=cst["sel_bf"][:], in_=selt[:]), R=(bt,), W=(bc_,))
            S.barrier("dve", (bt, bc_))
        self.scr = {}
        for l in range(L):
            d = {}
            d["head"] = nc.dram_tensor(f"s_head{l}", [cfg.HH, 128, KD, 512], BF16, kind="Internal").ap()
            d["u"] = nc.dram_tensor(f"s_u{l}", [cfg.NU, 128, KD, cfg.UW], BF16, kind="Internal").ap()
            d["glu"] = nc.dram_tensor(f"s_glu{l}", [cfg.NU, 128, cfg.KS, cfg.UW], BF16, kind="Internal").ap()
            d["out"] = nc.dram_tensor(f"s_out{l}", [cfg.NO, 128, KD, cfg.OW], BF16, kind="Internal").ap()
            d["gate"] = nc.dram_tensor(f"s_gate{l}", [cfg.NF, 128, KD, cfg.FW], BF16, kind="Internal").ap()
            d["up"] = nc.dram_tensor(f"s_up{l}", [cfg.NF, 128, KD, cfg.FW], BF16, kind="Internal").ap()
            d["down"] = nc.dram_tensor(f"s_down{l}", [cfg.NO, len(cfg.KPIECES), 128, 16, cfg.OW], BF16, kind="Internal").ap()
            d["M"] = nc.dram_tensor(f"s_M{l}", [128, cfg.G, 128], BF16, kind="Internal").ap()
            d["B"] = nc.dram_tensor(f"s_B{l}", [128, cfg.G, 128], BF16, kind="Internal").ap()
            d["C"] = nc.dram_tensor(f"s_C{l}", [64, 2, cfg.G, 128], BF16, kind="Internal").ap()
            self.scr[l] = d
        self.slab_ready = {}
        def fin():
            for b in list(self.slab_ready.values()) + [self.b_const]:
                if b.w is not None:
                    self.out_tks.append(b.w)
            S.barrier("dve", (self.b_const,))
            S.final_wait("sp", self.out_tks)
        if self.stop_after == "consts":
            return fin()
        import os
        if not os.environ.get("SKIP_PRE"):
            self.emit_precast()
        if self.stop_after == "precast":
            return fin()
        self.emit_lb()
        if self.stop_after == "lb":
            return fin()
        if not os.environ.get("SKIP_PRE"):
            self.emit_s5_prologue()
        if self.stop_after == "s5p":
            return fin()
        self.emit_main()
        S.final_wait("sp", self.out_tks)

    def emit_precast(self):
        cfg = self.cfg; S = self.S; D = self.dram; nc = self.nc
        jobs = []

        def v(ap):
            return ap.rearrange("(k p) c -> p k c", p=128)
        for l in range(cfg.L):
            sc = self.scr[l]
            for h in range(cfg.HH):
                jobs.append((("head", l, h), sc["head"][h], v(D["w_in"][l, :, h * 512:(h + 1) * 512]), cfg.KD, 512))
            for j in range(cfg.NU):
                c0 = cfg.HH * 512 + j * cfg.UW
                jobs.append((("u", l, j), sc["u"][j], v(D["w_in"][l, :, c0:c0 + cfg.UW]), cfg.KD, cfg.UW))
            for j in range(cfg.NU):
                jobs.append((("glu", l, j), sc["glu"][j], v(D["w_glu"][l, :, j * cfg.UW:(j + 1) * cfg.UW]), cfg.KS, cfg.UW))
            for j in range(cfg.NO):
                jobs.append((("out", l, j), sc["out"][j], v(D["w_out"][l, :, j * cfg.OW:(j + 1) * cfg.OW]), cfg.KD, cfg.OW))
            for j in range(cfg.NF):
                jobs.append((("gate", l, j), sc["gate"][j], v(D["w_ffn_gate"][l, :, j * cfg.FW:(j + 1) * cfg.FW]), cfg.KD, cfg.FW))
                jobs.append((("up", l, j), sc["up"][j], v(D["w_ffn_up"][l, :, j * cfg.FW:(j + 1) * cfg.FW]), cfg.KD, cfg.FW))
            for j in range(cfg.NO):
                for pi, (k0, nk) in enumerate(cfg.KPIECES):
                    jobs.append((("down", l, j, pi), sc["down"][j, pi, :, 0:nk, :],
                                 v(D["w_ffn_down"][l, k0 * 128:(k0 + nk) * 128, j * cfg.OW:(j + 1) * cfg.OW]), nk, cfg.OW))
        NB_ = 3
        with ExitStack() as es2:
            s32 = [es2.enter_context(nc.sbuf_tensor(f"pc32_{i}", [128, 8192], F32)) for i in range(NB_)]
            s16 = [es2.enter_context(nc.sbuf_tensor(f"pc16_{i}", [128, 8192], BF16)) for i in range(NB_)]
            b32 = [Buf(f"pc32_{i}") for i in range(NB_)]; b16 = [Buf(f"pc16_{i}") for i in range(NB_)]
            engs = ["dve", "act", "pool"]
            for n, (key, dst, src, nk, cw) in enumerate(jobs):
                i = n % NB_
                v32 = s32[i][:, 0:nk * cw].rearrange("p (k c) -> p k c", c=cw)
                v16 = s16[i][:, 0:nk * cw].rearrange("p (k c) -> p k c", c=cw)
                S.dma("qs", v32, src, W=(b32[i],))
                en = engs[n % 3]
                if en == "act":
                    S.op("act", lambda e, i=i, nk=nk, cw=cw: e.activation(out=s16[i][:, 0:nk * cw], in_=s32[i][:, 0:nk * cw], func=AF.Copy), R=(b32[i],), W=(b16[i],))
                else:
                    S.op(en, lambda e, i=i, nk=nk, cw=cw: e.tensor_copy(out=s16[i][:, 0:nk * cw], in_=s32[i][:, 0:nk * cw]), R=(b32[i],), W=(b16[i],))
                rb = Buf(str(key)); self.slab_ready[key] = rb
                S.dma("qs", dst, v16, R=(b16[i],), W=(rb,))
            S.barrier("dve", tuple(b32) + tuple(b16) + tuple(self.slab_ready.values()))

    def emit_lb(self):
        cfg = self.cfg; S = self.S; sb = self.sb; L = cfg.L; W_ = cfg.HH
        lg = self.cst["lbl_fm"]; b = Buf("lb"); rb = (self.b_const, b)
        mx = sb("lb_mx", [128, W_]); ex = sb("lb_ex", [128, L, W_]); sm = sb("lb_sm", [128, W_])
        lb = sb("lb_fm", [128, L, W_]); oml = sb("oml_fm", [128, L, W_]); noml = sb("noml_fm", [128, L, W_])
        V = lambda fn: S.op("dve", fn, R=rb, W=(b,))
        V(lambda e: e.tensor_copy(out=mx[:], in_=lg[:, 0, :]))
        for l in range(1, L):
            V(lambda e, l=l: e.tensor_tensor(out=mx[:], in0=mx[:], in1=lg[:, l, :], op=ALU.max))
        for l in range(L):
            V(lambda e, l=l: e.tensor_tensor(out=ex[:, l, :], in0=lg[:, l, :], in1=mx[:], op=ALU.subtract))
        S.op("act", lambda e: e.activation(out=ex[:], in_=ex[:], func=AF.Exp), R=(b,), W=(b,))
        V(lambda e: e.tensor_copy(out=sm[:], in_=ex[:, 0, :]))
        for l in range(1, L):
            V(lambda e, l=l: e.tensor_tensor(out=sm[:], in0=sm[:], in1=ex[:, l, :], op=ALU.add))
        V(lambda e: e.reciprocal(out=sm[:], in_=sm[:]))
        V(lambda e: e.memset(lb[:, 0, :], 0.0))
        for l in range(1, L):
            V(lambda e, l=l: e.tensor_tensor(out=ex[:, l, :], in0=ex[:, l, :], in1=sm[:], op=ALU.mult))
            V(lambda e, l=l: e.tensor_tensor(out=lb[:, l, :], in0=lb[:, l - 1, :], in1=ex[:, l, :], op=ALU.add))
        V(lambda e: e.tensor_scalar(out=oml[:], in0=lb[:], scalar1=-1.0, scalar2=1.0, op0=ALU.mult, op1=ALU.add))
        V(lambda e: e.tensor_scalar(out=noml[:], in0=oml[:], scalar1=-1.0, scalar2=None, op0=ALU.mult))
        self.lb = (lb, oml, noml, b)

    def emit_s5_prologue(self):
        cfg = self.cfg; S = self.S; nc = self.nc; G = cfg.G
        TWO_PI = 2.0 * math.pi
        self.r8 = {}
        self.s5_ready = {}
        for l in range(cfg.L):
            r8re = self.sb(f"r8re{l}", [64, G, 2]); r8im = self.sb(f"r8im{l}", [64, G, 2]); r8imn = self.sb(f"r8imn{l}", [64, G, 2])
            br8 = Buf(f"r8_{l}")
            self.r8[l] = (r8re, r8im, r8imn, br8)
            for nm in ("M", "B", "C"):
                self.s5_ready[(nm, l)] = Buf(f"s5r{nm}{l}")
            with ExitStack() as es2:
                def t(name, shape, dt=F32):
                    return es2.enter_context(nc.sbuf_tensor(f"p{l}_{name}", list(shape), dt))
                b = Buf("s5p")
                lre = t("lre", [64, G]); lim = t("lim", [64, G]); lst = t("lst", [64, G])
                S.dma("qs", lre[:], self.dram["lam_re"][:, l, :], W=(b,))
                S.dma("qs", lim[:], self.dram["lam_im"][:, l, :], W=(b,))
                S.dma("qs", lst[:], self.dram["lstep"][:, l, :], W=(b,))
                bre = t("bre", [64, G, 16]); bim = t("bim", [64, G, 16]); cre = t("cre", [64, G, 16]); cim = t("cim", [64, G, 16])
                for tt_, key in ((bre, "b_re"), (bim, "b_im"), (cre, "c_re"), (cim, "c_im")):
                    S.dma("qs", tt_[:], self.dram[key][:, l], W=(b,))
                are = t("are", [64, G]); dt_ = t("dt", [64, G]); xr = t("xr", [64, G]); th = t("th", [64, G])
                V = lambda fn: S.op("dve", fn, R=(b,), W=(b,))
                A = lambda fn: S.op("act", fn, R=(b,), W=(b,))
                V(lambda e: e.tensor_scalar(out=are[:], in0=lre[:], scalar1=-1e-4, scalar2=None, op0=ALU.min))
                A(lambda e: e.activation(out=dt_[:], in_=lst[:], func=AF.Exp))
                V(lambda e: e.tensor_tensor(out=xr[:], in0=are[:], in1=dt_[:], op=ALU.mult))
                V(lambda e: e.tensor_tensor(out=th[:], in0=lim[:], in1=dt_[:], op=ALU.mult))
                Ere = t("Ere", [64, 9, G]); Eim = t("Eim", [64, 9, G]); Nre = t("Nre", [64, 9, G]); Nim = t("Nim", [64, 9, G])
                mp = t("mp", [64, G]); mn = t("mn", [64, G]); ang = t("ang", [64, G]); kf = t("kf", [64, G]); ki = t("ki", [64, G], I32)
                sn = t("sn", [64, G]); cs = t("cs", [64, G])
                V(lambda e: e.memset(Ere[:, 0, :], 1.0)); V(lambda e: e.memset(Eim[:, 0, :], 0.0))
                V(lambda e: e.memset(Nre[:, 0, :], 1.0)); V(lambda e: e.memset(Nim[:, 0, :], 0.0))

                def sin_of(dst, k, shift):
                    V(lambda e: e.tensor_scalar(out=ang[:], in0=th[:], scalar1=float(k), scalar2=float(shift), op0=ALU.mult, op1=ALU.add))
                    V(lambda e: e.tensor_scalar(out=kf[:], in0=ang[:], scalar1=1.0 / TWO_PI, scalar2=None, op0=ALU.mult))
                    V(lambda e: e.tensor_copy(out=ki[:], in_=kf[:]))
                    V(lambda e: e.tensor_copy(out=kf[:], in_=ki[:]))
                    V(lambda e: e.scalar_tensor_tensor(out=ang[:], in0=kf[:], scalar=-TWO_PI, in1=ang[:], op0=ALU.mult, op1=ALU.add))
                    V(lambda e: e.tensor_scalar(out=ang[:], in0=ang[:], scalar1=math.pi, scalar2=-math.pi, op0=ALU.min, op1=ALU.max))
                    A(lambda e: e.activation(out=dst[:], in_=ang[:], func=AF.Sin))
                for k in range(1, 9):
                    A(lambda e, k=k: e.activation(out=mp[:], in_=xr[:], func=AF.Exp, scale=float(k)))
                    A(lambda e, k=k: e.activation(out=mn[:], in_=xr[:], func=AF.Exp, scale=float(-k)))
                    sin_of(sn, k, 0.0); sin_of(cs, k, math.pi / 2)
                    V(lambda e, k=k: e.tensor_tensor(out=Ere[:, k, :], in0=mp[:], in1=cs[:], op=ALU.mult))
                    V(lambda e, k=k: e.tensor_tensor(out=Eim[:, k, :], in0=mp[:], in1=sn[:], op=ALU.mult))
                    V(lambda e, k=k: e.tensor_tensor(out=Nre[:, k, :], in0=mn[:], in1=cs[:], op=ALU.mult))
                    V(lambda e, k=k: e.scalar_tensor_tensor(out=Nim[:, k, :], in0=mn[:], scalar=-1.0, in1=sn[:], op0=ALU.mult, op1=ALU.mult))
                for s_ in range(2):
                    S.op("dve", lambda e, s_=s_: e.tensor_copy(out=r8re[:, :, s_], in_=Ere[:, 8, :]), R=(b,), W=(br8,))
                    S.op("dve", lambda e, s_=s_: e.tensor_copy(out=r8im[:, :, s_], in_=Eim[:, 8, :]), R=(b,), W=(br8,))
                    S.op("dve", lambda e, s_=s_: e.tensor_scalar(out=r8imn[:, :, s_], in0=Eim[:, 8, :], scalar1=-1.0, scalar2=None, op0=ALU.mult), R=(b,), W=(br8,))
                den = t("den", [64, G]); zre = t("zre", [64, G]); zim = t("zim", [64, G]); t1 = t("t1", [64, G]); x1 = t("x1", [64, G])
                V(lambda e: e.tensor_tensor(out=den[:], in0=are[:], in1=are[:], op=ALU.mult))
                V(lambda e: e.tensor_tensor(out=t1[:], in0=lim[:], in1=lim[:], op=ALU.mult))
                V(lambda e: e.tensor_tensor(out=den[:], in0=den[:], in1=t1[:], op=ALU.add))
                V(lambda e: e.reciprocal(out=den[:], in_=den[:]))
                V(lambda e: e.tensor_scalar(out=x1[:], in0=Ere[:, 1, :], scalar1=-1.0, scalar2=None, op0=ALU.add))
                V(lambda e: e.tensor_tensor(out=zre[:], in0=x1[:], in1=are[:], op=ALU.mult))
                V(lambda e: e.tensor_tensor(out=t1[:], in0=Eim[:, 1, :], in1=lim[:], op=ALU.mult))
                V(lambda e: e.tensor_tensor(out=zre[:], in0=zre[:], in1=t1[:], op=ALU.add))
                V(lambda e: e.tensor_tensor(out=zre[:], in0=zre[:], in1=den[:], op=ALU.mult))
                V(lambda e: e.tensor_tensor(out=zim[:], in0=Eim[:, 1, :], in1=are[:], op=ALU.mult))
                V(lambda e: e.tensor_tensor(out=t1[:], in0=x1[:], in1=lim[:], op=ALU.mult))
                V(lambda e: e.tensor_tensor(out=zim[:], in0=zim[:], in1=t1[:], op=ALU.subtract))
                V(lambda e: e.tensor_tensor(out=zim[:], in0=zim[:], in1=den[:], op=ALU.mult))
                Bbre = t("Bbre", [64, G, 16]); Bbim = t("Bbim", [64, G, 16]); tg16 = t("tg16", [64, G, 16])

                def bcG(ap2, n):
                    return ap2.unsqueeze(2).broadcast_to([64, n, 16])

                def cmul(ore, oim, are_, aim_, bre_, bim_, tmp):
                    V(lambda e: e.tensor_tensor(out=ore, in0=bre_, in1=are_, op=ALU.mult))
                    V(lambda e: e.tensor_tensor(out=tmp, in0=bim_, in1=aim_, op=ALU.mult))
                    V(lambda e: e.tensor_tensor(out=ore, in0=ore, in1=tmp, op=ALU.subtract))
                    V(lambda e: e.tensor_tensor(out=oim, in0=bim_, in1=are_, op=ALU.mult))
                    V(lambda e: e.tensor_tensor(out=tmp, in0=bre_, in1=aim_, op=ALU.mult))
                    V(lambda e: e.tensor_tensor(out=oim, in0=oim, in1=tmp, op=ALU.add))
                cmul(Bbre[:], Bbim[:], bcG(zre[:], G), bcG(zim[:], G), bre[:], bim[:], tg16[:])
                dbg_on = (self.stop_after == "s5p" and l == 0)

                def dump(ap2, parts, n, off):
                    tk = S.dma("qs", self.dram["dbg"][0:parts, off:off + n], ap2, R=(b, bo), W=(Buf("d"),))
                    b.r[tk[0]] = tk
                if dbg_on:
                    bo = Buf("s5p_out")
                    dump(Ere[:].rearrange("p k g -> p (k g)"), 64, 9 * G, 0)
                    dump(Eim[:].rearrange("p k g -> p (k g)"), 64, 9 * G, 9 * G)
                    dump(zre[:], 64, G, 18 * G); dump(zim[:], 64, G, 19 * G)
                    dump(Bbre[:, 0:8, :].rearrange("p g h -> p (g h)"), 64, 128, 20 * G)
                    dump(Nre[:].rearrange("p k g -> p (k g)"), 64, 9 * G, 20 * G + 128)
                    dump(Nim[:].rearrange("p k g -> p (k g)"), 64, 9 * G, 29 * G + 128)
                if not dbg_on:
                    bo = Buf("s5p_out")
                Mbf = t("Mbf", [128, 8, 128], BF16); Bbf = t("Bbf", [128, 8, 128], BF16); Cbf = t("Cbf", [64, 2, 8, 128], BF16)
                Xre = t("Xre", [64, 8, 8, 16]); Xim = t("Xim", [64, 8, 8, 16]); BTre = t("BTre", [64, 8, 8, 16]); BTim = t("BTim", [64, 8, 8, 16])
                CRre = t("CRre", [64, 8, 8, 16]); CRim = t("CRim", [64, 8, 8, 16]); tb = t("tb", [64, 8, 16])
                ident = self.cst["ident"]
                for gb in range(cfg.GB):
                    gs = slice(gb * 8, gb * 8 + 8)
                    for s_ in range(8):
                        cmul(Xre[:, :, s_, :], Xim[:, :, s_, :], bcG(Nre[:, s_ + 1, gs], 8), bcG(Nim[:, s_ + 1, gs], 8), Bbre[:, gs, :], Bbim[:, gs, :], tb[:])
                        cmul(BTre[:, :, s_, :], BTim[:, :, s_, :], bcG(Ere[:, 7 - s_, gs], 8), bcG(Eim[:, 7 - s_, gs], 8), Bbre[:, gs, :], Bbim[:, gs, :], tb[:])
                        cmul(CRre[:, :, s_, :], CRim[:, :, s_, :], bcG(Ere[:, s_ + 1, gs], 8), bcG(Eim[:, s_ + 1, gs], 8), cre[:, gs, :], cim[:, gs, :], tb[:])
                    S.op("dve", lambda e: e.tensor_scalar(out=CRim[:], in0=CRim[:], scalar1=-1.0, scalar2=None, op0=ALU.mult), R=(b,), W=(b,))
                    if dbg_on and gb == 0:
                        dump(Xre[:].rearrange("p g s h -> p (g s h)"), 64, 1024, 1024)
                        dump(CRre[:].rearrange("p g s h -> p (g s h)"), 64, 1024, 2048)
                        dump(CRim[:].rearrange("p g s h -> p (g s h)"), 64, 1024, 3072)
                        dump(BTre[:].rearrange("p g s h -> p (g s h)"), 64, 1024, 4096)
                    S.op("dve", lambda e: e.tensor_copy(out=Cbf[:, 0], in_=CRre[:].rearrange("p g s h -> p g (s h)")), R=(b,), W=(bo,))
                    S.op("dve", lambda e: e.tensor_copy(out=Cbf[:, 1], in_=CRim[:].rearrange("p g s h -> p g (s h)")), R=(b,), W=(bo,))
                    for gi in range(8):
                        pm = self.bank(gi % 2); pb = self.bank(2 + gi % 2)
                        fl = lambda ap: ap.rearrange("p s h -> p (s h)")
                        S.op("pe", lambda e, gi=gi, pm=pm: e.matmul(pm[0][:, 0:128], fl(Xre[:, gi]), fl(CRre[:, gi]), start=True, stop=False), R=(b,), W=(pm[1],), sig=False)
                        S.op("pe", lambda e, gi=gi, pm=pm: e.matmul(pm[0][:, 0:128], fl(Xim[:, gi]), fl(CRim[:, gi]), start=False, stop=True), R=(b,), W=(pm[1],))
                        S.op("dve", lambda e, gi=gi, pm=pm: e.tensor_tensor(out=Mbf[:, gi, :], in0=pm[0][:, 0:128], in1=self.cst["mmask"][:], op=ALU.mult), R=(pm[1], self.b_const), W=(bo,))
                        if dbg_on and gb == 0 and gi == 0:
                            mdb = t("mdb", [128, 256])
                            S.op("dve", lambda e, pm=pm: e.tensor_tensor(out=mdb[:, 0:128], in0=pm[0][:, 0:128], in1=self.cst["mmask"][:], op=ALU.mult), R=(pm[1], self.b_const), W=(b,))
                            dump(mdb[:, 0:128], 128, 128, 5120)
                        S.op("pe", lambda e, gi=gi, pb=pb: e.matmul(pb[0][:, 0:64], fl(BTre[:, gi]), ident[0:64, 0:64], start=True, stop=True), R=(b, self.b_const), W=(pb[1],), sig=False)
                        S.op("pe", lambda e, gi=gi, pb=pb: e.matmul(pb[0][:, 64:128], fl(BTim[:, gi]), ident[0:64, 0:64], start=True, stop=True), R=(b, self.b_const), W=(pb[1],))
                        S.op("act", lambda e, gi=gi, pb=pb: e.activation(out=Bbf[:, gi, :], in_=pb[0][:, 0:128], func=AF.Copy), R=(pb[1],), W=(bo,))
                    sc = self.scr[l]
                    xb = Buf("x")
                    tk1 = S.dma("qs", sc["M"][:, gs, :], Mbf[:], R=(bo,), W=(xb,))
                    tk2 = S.dma("qs", sc["B"][:, gs, :], Bbf[:], R=(bo,), W=(xb,))
                    tk3 = S.dma("qs", sc["C"][:, :, gs, :], Cbf[:], R=(bo,), W=(xb,))
                    for nm, tk in (("M", tk1), ("B", tk2), ("C", tk3)):
                        rb_ = self.s5_ready[(nm, l)]
                        rb_.r[tk[0]] = tk
                S.barrier("dve", (b, bo, br8) + tuple(self.s5_ready[(nm, l)] for nm in ("M", "B", "C")))

    def emit_main(self):
        cfg = self.cfg; nc = self.nc; S = self.S; sb = self.sb; cst = self.cst
        L = cfg.L; KD = cfg.KD; HH = cfg.HH; KS = cfg.KS; KF = cfg.KF; G = cfg.G
        GH = min(16, G); NHALF = G // GH
        ones_bf = cst["ones_bf"]; bcn = self.b_const
        hT = sb("hT", [128, KD, NT]); b_h = [Buf(f"h{k}") for k in range(KD)]
        xn = sb("xn", [128, KD, NT], BF16); b_xn = [Buf(f"xn{k}") for k in range(KD)]
        mix = sb("mix", [128, KD, NT], BF16); b_mix = [Buf(f"mix{k}") for k in range(KD)]
        NS = 3
        wsl = [sb(f"wsl{i}", [128, 8192], BF16) for i in range(NS)]; b_wsl = [Buf(f"wsl{i}") for i in range(NS)]
        self.ws_i = 0
        hst = sb("hst", [128, L, 2, HH, 128]); b_hst = [[[Buf(f"hst{l}_{s}_{h}") for h in range(HH)] for s in range(2)] for l in range(L)]
        s5c = sb("s5c", [64, L, 2, G, 2]); b_s5c = [Buf(f"s5c{l}") for l in range(L)]
        cvc = sb("cvc", [128, L, KF, 2, 2]); b_cvc = [Buf(f"cvc{l}") for l in range(L)]
        std = sb("std", [128, NT]); rstd = sb("rstd", [128, NT]); b_std = Buf("std")
        S.op("dve", lambda e: e.memset(hst[:].rearrange("p a b c d -> p (a b c d)"), 0.0), W=tuple(b for a in b_hst for c in a for b in c))
        S.op("dve", lambda e: e.memset(s5c[:].rearrange("p a b c d -> p (a b c d)"), 0.0), W=tuple(b_s5c))
        S.op("dve", lambda e: e.memset(cvc[:].rearrange("p a b c d -> p (a b c d)"), 0.0), W=tuple(b_cvc))
        reg_bufs = []

        def RB(name):
            b_ = Buf(name); reg_bufs.append(b_); return b_
        sizes = {
            "io": cfg.D,
            "hg": 3 * NT + 2 * NT + 2 * NT + 2 * NT + 3 * NT + NT + (2 * 3 * 128 + 4 * NT + 2 * 3 * 128 + 3 * 128 + 2 * 128 + NT) // 2 + 16,
            "s5": KS * NT + 2 * GH * NW + 2 * 2 * GH * 2 + KS * NT + 2 * NT + (2 * NT + GH * NW + 2 * GH * NW + GH * NW + KS * NT) // 2 + 8,
            "ff": 2 * 2 * (TT + 2) + 2 * 2 * TT + 2 * 2 * TT + (KF * NT) // 2 + 8,
        }
        RW = max(sizes.values())
        reg = sb("reg", [128, RW])

        class RA:
            def __init__(s_): s_.off = 0
            def _shape(s_, v, shape):
                if len(shape) == 2: return v
                names = "abcd"[:len(shape) - 1]
                pat = "p (" + " ".join(names) + ") -> p " + " ".join(names)
                return v.rearrange(pat, **{n_: d_ for n_, d_ in zip(names[1:], shape[2:])})
            def f32(s_, shape, parts=128):
                n = int(np.prod(shape[1:])); v = reg[0:parts, s_.off:s_.off + n]; s_.off += n
                assert s_.off <= RW
                return s_._shape(v, shape)
            def bf(s_, shape, parts=128):
                n = int(np.prod(shape[1:])); nw = (n + 1) // 2
                v = reg[0:parts, s_.off:s_.off + nw].bitcast(BF16)[:, 0:n]; s_.off += nw
                assert s_.off <= RW
                return s_._shape(v, shape)
        ra = RA()
        xin = ra.f32([128, cfg.D]); b_xin = RB("xin")
        ra = RA()
        hq = [ra.f32([128, NT])]; hk = [ra.f32([128, NT])]; hf = [ra.f32([128, NT])]
        hg2 = [ra.f32([128, NT]) for i in range(2)]; hv2 = [ra.bf([128, 3, 128]) for i in range(2)]
        eb2 = [ra.f32([128, NT]) for i in range(2)]; qtil2 = [ra.bf([128, NT]) for i in range(2)]
        ktil2 = [ra.bf([128, NT]) for i in range(2)]; khT2 = [ra.f32([128, NT]) for i in range(2)]
        b_hp = [RB("hp0"), RB("hp1")]; b_t1 = RB("ht1")
        bT = ra.f32([128, NT]); enb = ra.f32([128, NT]); erc = ra.f32([128, NT])
        khat = ra.bf([128, 2, 3, 128]); scm = ra.bf([128, 3, 128])
        sbf = ra.bf([128, 2, 128]); osq = ra.bf([128, NT]); otmp = ra.f32([128, NT])
        b_hr = RB("hrest"); b_sbf = [RB("sbf0"), RB("sbf1")]; b_scm = RB("scm"); b_khat = RB("khat")
        ra = RA()
        uT = ra.f32([128, KS, NT]); b_uT = [RB(f"uT{k}") for k in range(KS)]
        ubf = [ra.bf([128, NT]) for i in range(2)]; b_ubf = [RB("ubf0"), RB("ubf1")]
        Uall = ra.bf([128, GH, NW]); b_U = RB("Uall")
        Wall = ra.f32([64, 2, GH, NW], parts=64); b_W = RB("Wall")
        Sbf = ra.bf([64, 2, GH, NW], parts=64); b_S = RB("Sbf")
        Yw = ra.bf([128, GH, NW]); b_Yw = RB("Yw")
        sA = ra.f32([64, 2, GH, 2], parts=64); sB = ra.f32([64, 2, GH, 2], parts=64); b_sc = RB("scan")
        yact = ra.f32([128, KS, NT]); b_ya = [RB(f"ya{k}") for k in range(KS)]
        yabf = ra.bf([128, KS, NT]); b_yb = [RB(f"yb{k}") for k in range(KS)]
        ypre = ra.f32([128, NT]); sgl = ra.f32([128, NT]); b_yt = RB("ytmp")
        ra = RA()
        hid = ra.bf([128, KF, NT]); b_hid = [RB(f"hid{k}") for k in range(KF)]
        aext = [ra.f32([128, 2, TT + 2]) for i in range(2)]; cacc = [ra.f32([128, 2, TT]) for i in range(2)]
        csg = [ra.f32([128, 2, TT]) for i in range(2)]; b_ff = [RB("ff0"), RB("ff1")]

        def phase():
            S.barrier("dve", tuple(reg_bufs))

        def V(fn, R=(), W=()): return S.op("dve", fn, R=R, W=W)
        def A(fn, R=(), W=()): return S.op("act", fn, R=R, W=W)
        def Pl(fn, R=(), W=()): return S.op("pool", fn, R=R, W=W)
        def MM(fn, R=(), W=(), sig=False): return S.op("pe", fn, R=R, W=W, sig=sig)

        def load_slab(src_ap, nk, cw, ready, parts=128):
            i = self.ws_i % NS; self.ws_i += 1
            view = wsl[i][0:parts, 0:nk * cw].rearrange("p (k c) -> p k c", c=cw)
            S.dma("qs", view, src_ap, R=tuple(ready), W=(b_wsl[i],))
            return view, b_wsl[i]

        def rmsnorm_to_xn(gam):
            import os
            for k in range(KD):
                if k % 2 == 0 or os.environ.get("NOSQ"):
                    Pl(lambda e, k=k: e.tensor_tensor(out=xn[:, k, :], in0=hT[:, k, :], in1=hT[:, k, :], op=ALU.mult), R=(b_h[k],), W=(b_xn[k],))
                else:
                    A(lambda e, k=k: e.activation(out=xn[:, k, :], in_=hT[:, k, :], func=AF.Square), R=(b_h[k],), W=(b_xn[k],))
            ps, pb = self.bank(0)
            for k in range(KD):
                MM(lambda e, k=k: e.matmul(ps[:, 0:NT], ones_bf[:], xn[:, k, :], start=(k == 0), stop=(k == KD - 1)),
                   R=(b_xn[k], bcn), W=(pb,), sig=(k == KD - 1))
            A(lambda e: e.activation(out=std[:], in_=ps[:, 0:NT], func=AF.Sqrt, scale=1.0 / cfg.D, bias=cst["eps"][:]), R=(pb, bcn), W=(b_std,))
            V(lambda e: e.reciprocal(out=rstd[:], in_=std[:]), R=(b_std,), W=(b_std,))
            for k in range(KD):
                V(lambda e, k=k: e.scalar_tensor_tensor(out=xn[:, k, :], in0=hT[:, k, :], scalar=gam(k), in1=rstd[:], op0=ALU.mult, op1=ALU.mult),
                  R=(b_h[k], b_std, bcn), W=(b_xn[k],))

        lbv, omlv, nomlv, b_lb = self.lb
        rmask = sb("rmask", [128, 2])
        S.op("dve", lambda e: e.memset(rmask[:], 0.0), W=(bcn,))
        S.op("dve", lambda e: e.memset(rmask[0:64, 0:1], 1.0), R=(bcn,), W=(bcn,))
        S.op("dve", lambda e: e.memset(rmask[64:128, 1:2], 1.0), R=(bcn,), W=(bcn,))
        self.kvb = [self.pbuf[6], self.pbuf[6]]
        idf = cst["ident"]

        for ti in range(cfg.NTILE):
            phase()
            for tg in range(3):
                c0 = tg * 128
                segs = []
                c = c0
                while c < c0 + 128:
                    s_ = c // TT; j = c % TT; n = min(TT - j, c0 + 128 - c)
                    segs.append((c - c0, s_, ti * TT + j, n)); c += n
                for (po, s_, tok, n) in segs:
                    S.dma("qs", xin[po:po + n, :], self.dram["xp"][s_, tok:tok + n, :], W=(b_xin,))
                for k in range(KD):
                    ps, pb = self.bank(k % 4)
                    MM(lambda e, k=k, ps=ps: e.matmul(ps[:, 0:128], xin[:, k * 128:(k + 1) * 128], idf[:], start=True, stop=True),
                       R=(b_xin, bcn), W=(pb,), sig=True)
                    A(lambda e, k=k, ps=ps, c0=c0: e.activation(out=hT[:, k, c0:c0 + 128], in_=ps[:, 0:128], func=AF.Copy), R=(pb,), W=(b_h[k],))
            if self.stop_after == "stage0":
                self.write_out(hT, b_h, xin, b_xin, ti); continue
            stopped = False
            for l in range(L):
                sc = self.scr[l]
                rmsnorm_to_xn(lambda k: cst["g_mix"][:, l, k:k + 1])
                if self.stop_after == "norm1" and l == 0:
                    self.debug_dump_bf(xn, b_xn, ti); stopped = True; break
                phase()
                def hg_s1(hd):
                    pp = hd % 2
                    wv, wb = load_slab(sc["head"][hd], KD, 512, (self.slab_ready[("head", l, hd)],))
                    pq, bq = self.bank(0); pz, bz = self.bank(1); pg, bg = self.bank(2); pv, bv = self.bank(3)
                    for (pst, bst, c0) in ((pz, bz, 128), (pq, bq, 0), (pg, bg, 384)):
                        for k in range(KD):
                            MM(lambda e, k=k, pst=pst, c0=c0: e.matmul(pst[:, 0:NT], wv[:, k, c0:c0 + 128], xn[:, k, :], start=(k == 0), stop=(k == KD - 1)),
                               R=(wb, b_xn[k]), W=(bst,), sig=(k == KD - 1))
                    for tg in range(3):
                        for k in range(KD):
                            MM(lambda e, k=k, tg=tg: e.matmul(pv[:, tg * 128:(tg + 1) * 128], xn[:, k, tg * 128:(tg + 1) * 128], wv[:, k, 256:384], start=(k == 0), stop=(k == KD - 1)),
                               R=(wb, b_xn[k]), W=(bv,), sig=(k == KD - 1 and tg == 2))
                    bp = b_hp[pp]; bt1 = b_t1
                    A(lambda e: e.activation(out=hk[0][:], in_=pz[:, 0:NT], func=AF.Sigmoid), R=(bz,), W=(bt1,))
                    A(lambda e: e.activation(out=hq[0][:], in_=pq[:, 0:NT], func=AF.Silu), R=(bq,), W=(bt1,))
                    A(lambda e: e.activation(out=hg2[pp][:], in_=pg[:, 0:NT], func=AF.Silu), R=(bg,), W=(bp,))
                    Pl(lambda e: e.tensor_copy(out=hv2[pp][:], in_=pv[:, 0:NT].rearrange("p (a b) -> p a b", b=128)), R=(bv,), W=(bp,)) if False else \
                        V(lambda e: e.tensor_copy(out=hv2[pp][:], in_=pv[:, 0:NT].rearrange("p (a b) -> p a b", b=128)), R=(bv,), W=(bp,))
                    V(lambda e: e.tensor_scalar(out=hf[0][:], in0=hk[0][:], scalar1=omlv[:, l, hd:hd + 1], scalar2=lbv[:, l, hd:hd + 1], op0=ALU.mult, op1=ALU.add), R=(bt1, b_lb), W=(bt1,))
                    V(lambda e: e.tensor_scalar(out=hf[0][:], in0=hf[0][:], scalar1=F_FLOOR, scalar2=None, op0=ALU.max), R=(bt1,), W=(bt1,))
                    V(lambda e: e.tensor_scalar(out=hk[0][:], in0=hk[0][:], scalar1=nomlv[:, l, hd:hd + 1], scalar2=omlv[:, l, hd:hd + 1], op0=ALU.mult, op1=ALU.add), R=(bt1, b_lb), W=(bt1,))
                    A(lambda e: e.activation(out=hf[0][:], in_=hf[0][:], func=AF.Ln), R=(bt1,), W=(bt1,))
                    V(lambda e: e.tensor_tensor_scan(out=bT[:], data0=cst["cmask"][:], data1=hf[0][:], initial=0.0, op0=ALU.mult, op1=ALU.add), R=(bt1, bcn), W=(bt1,))
                    A(lambda e: e.activation(out=eb2[pp][:], in_=bT[:], func=AF.Exp), R=(bt1,), W=(bp,))
                    A(lambda e: e.activation(out=enb[:], in_=bT[:], func=AF.Exp, scale=-1.0), R=(bt1,), W=(bt1,))
                    for c in range(6):
                        A(lambda e, c=c: e.activation(out=erc[:, c * 64:(c + 1) * 64], in_=bT[:, c * 64:(c + 1) * 64], func=AF.Exp, scale=-1.0, bias=bT[:, c * 64 + 63:c * 64 + 64]), R=(bt1,), W=(bt1,))
                    V(lambda e: e.tensor_tensor(out=qtil2[pp][:], in0=hq[0][:], in1=eb2[pp][:], op=ALU.mult), R=(bt1, bp), W=(bp,))
                    V(lambda e: e.tensor_tensor(out=ktil2[pp][:], in0=hk[0][:], in1=enb[:], op=ALU.mult), R=(bt1,), W=(bp,))
                    V(lambda e: e.tensor_tensor(out=khT2[pp][:], in0=hk[0][:], in1=erc[:], op=ALU.mult), R=(bt1,), W=(bp,))

                def hg_s2(hd):
                    pp = hd % 2; bp = b_hp[pp]
                    qtil_ = qtil2[pp]; ktil_ = ktil2[pp]; khT_ = khT2[pp]; hv_ = hv2[pp]; eb_ = eb2[pp]; hgp = hg2[pp]
                    pk, bk = self.bank(4); psc, bsc = self.bank(5); pkv = self.psum[6]; bkv = self.kvb; po, bo_ = self.bank(7)
                    for tg in range(3):
                        MM(lambda e, tg=tg: e.matmul(pk[:, tg * 128:(tg + 1) * 128], khT_[:, tg * 128:(tg + 1) * 128], idf[:], start=True, stop=True), R=(bp, bcn), W=(bk,), sig=(tg == 2))
                    A(lambda e: e.activation(out=khat[:, 0], in_=pk[:, 0:NT].rearrange("p (a b) -> p a b", b=128), func=AF.Copy, scale=rmask[:, 0:1]), R=(bk, bcn), W=(b_khat,))
                    V(lambda e: e.tensor_scalar(out=khat[:, 1], in0=pk[:, 0:NT].rearrange("p (a b) -> p a b", b=128), scalar1=rmask[:, 1:2], scalar2=None, op0=ALU.mult), R=(bk, bcn), W=(b_khat,))
                    for tg in range(3):
                        MM(lambda e, tg=tg: e.matmul(psc[:, tg * 128:(tg + 1) * 128], ktil_[:, tg * 128:(tg + 1) * 128], qtil_[:, tg * 128:(tg + 1) * 128], start=True, stop=True), R=(bp,), W=(bsc,), sig=(tg == 2))
                    for tg in range(3):
                        V(lambda e, tg=tg: e.tensor_tensor(out=scm[:, tg, :], in0=psc[:, tg * 128:(tg + 1) * 128], in1=cst["tri2"][:], op=ALU.mult), R=(bsc, bcn), W=(b_scm,))
                    order = (0, 1, 2, 3, 4, 5)
                    for ci, c in enumerate(order):
                        s_ = c // 3; tg = c // 2; j = c % 2; r0 = 64 * j
                        bs_ = b_hst[l][s_][hd]
                        if c % 3 == 0:
                            A(lambda e, s_=s_: e.activation(out=sbf[:, s_, :], in_=hst[:, l, s_, hd, :], func=AF.Copy), R=(bs_,), W=(b_sbf[s_],))
                        MM(lambda e, c=c, s_=s_: e.matmul(po[:, c * 64:(c + 1) * 64], sbf[:, s_, :], qtil_[:, c * 64:(c + 1) * 64], start=True, stop=False), R=(b_sbf[s_], bp), W=(bo_,))
                        MM(lambda e, c=c, tg=tg, r0=r0: e.matmul(po[:, c * 64:(c + 1) * 64], hv_[:, tg, :], scm[:, tg, r0:r0 + 64], start=False, stop=True), R=(bp, b_scm), W=(bo_,), sig=(ci == 5))
                        kvs = (ci % 2) * 128
                        MM(lambda e, tg=tg, j=j, kvs=kvs: e.matmul(pkv[:, kvs:kvs + 128], khat[:, j, tg, :], hv_[:, tg, :], start=True, stop=True), R=(b_khat, bp), W=(bkv[ci % 2],), sig=True)
                        V(lambda e, c=c, s_=s_, kvs=kvs: e.scalar_tensor_tensor(out=hst[:, l, s_, hd, :], in0=hst[:, l, s_, hd, :], scalar=eb_[:, c * 64 + 63:c * 64 + 64], in1=pkv[:, kvs:kvs + 128], op0=ALU.mult, op1=ALU.add), R=(bkv[ci % 2], bp, bs_), W=(bs_,))
                        if c % 3 != 2:
                            A(lambda e, s_=s_: e.activation(out=sbf[:, s_, :], in_=hst[:, l, s_, hd, :], func=AF.Copy), R=(bs_,), W=(b_sbf[s_],))
                    A(lambda e: e.activation(out=osq[:], in_=po[:, 0:NT], func=AF.Square), R=(bo_,), W=(b_hr,))
                    MM(lambda e: e.matmul(pk[:, 0:NT], ones_bf[:], osq[:], start=True, stop=True), R=(b_hr, bcn), W=(bk,), sig=True)
                    A(lambda e: e.activation(out=otmp[:], in_=pk[:, 0:NT], func=AF.Sqrt, scale=1.0 / 128.0, bias=cst["eps"][:]), R=(bk, bcn), W=(b_hr,))
                    V(lambda e: e.reciprocal(out=otmp[:], in_=otmp[:]), R=(b_hr,), W=(b_hr,))
                    V(lambda e: e.tensor_tensor(out=otmp[:], in0=po[:, 0:NT], in1=otmp[:], op=ALU.mult), R=(bo_, b_hr), W=(b_hr,))
                    V(lambda e: e.scalar_tensor_tensor(out=mix[:, hd, :], in0=otmp[:], scalar=cst["hg_gain"][:, l, hd:hd + 1], in1=hgp[:], op0=ALU.mult, op1=ALU.mult), R=(b_hr, bp, bcn), W=(b_mix[hd],))

                import os
                if os.environ.get("NOPIPE"):
                    for hd in range(HH):
                        hg_s1(hd); hg_s2(hd)
                else:
                    hg_s1(0)
                    for hd in range(HH):
                        if hd + 1 < HH:
                            hg_s1(hd + 1)
                        hg_s2(hd)
                if self.stop_after == "hgrn" and l == 0:
                    self.debug_dump_bf(mix, b_mix, ti); stopped = True; break
                phase()
                for j in range(cfg.NU):
                    wv, wb = load_slab(sc["u"][j], KD, cfg.UW, (self.slab_ready[("u", l, j)],))
                    for bi in range(cfg.UW // 128):
                        blk = j * (cfg.UW // 128) + bi
                        ps, pb = self.bank(blk % 4)
                        for k in range(KD):
                            MM(lambda e, k=k, bi=bi, ps=ps: e.matmul(ps[:, 0:NT], wv[:, k, bi * 128:(bi + 1) * 128], xn[:, k, :], start=(k == 0), stop=(k == KD - 1)),
                               R=(wb, b_xn[k]), W=(pb,), sig=(k == KD - 1))
                        A(lambda e, blk=blk, ps=ps: e.activation(out=uT[:, blk, :], in_=ps[:, 0:NT], func=AF.Copy), R=(pb,), W=(b_uT[blk],))
                if self.stop_after == "s5a" and l == 0:
                    self.debug_dump_bf(mix, b_mix, ti); stopped = True; break
                r8re, r8im, r8imn, br8 = self.r8[l]
                selbf = cst["sel_bf"]
                lvl = {"s5b": 1, "s5c": 2, "s5d": 3, "s5e": 4, "s5f": 5}.get(self.stop_after, 99)
                for hf_ in range(NHALF):
                    g0 = hf_ * GH
                    Bv, Bb_ = load_slab(sc["B"][:, g0:g0 + GH, :], GH, 128, ())
                    for gb in range(GH // 8):
                        blk = (g0 // 8) + gb
                        ub = ubf[blk % 2]; bub = b_ubf[blk % 2]
                        V(lambda e, blk=blk, ub=ub: e.tensor_copy(out=ub[:], in_=uT[:, blk, :]), R=(b_uT[blk],), W=(bub,))
                        ps, pb = self.bank(blk % 2)
                        for jg in range(8):
                            for s_ in range(8):
                                MM(lambda e, jg=jg, s_=s_, ps=ps, ub=ub: e.matmul(ps[:, jg * NW:(jg + 1) * NW], selbf[:, jg, s_, :], ub[:, s_:NT:8], start=(s_ == 0), stop=(s_ == 7)),
                                   R=(bub, bcn), W=(pb,), sig=(s_ == 7 and jg == 7))
                        A(lambda e, gb=gb, ps=ps: e.activation(out=Uall[:, gb * 8:(gb + 1) * 8, :], in_=ps[:, 0:8 * NW].rearrange("p (g c) -> p g c", c=NW), func=AF.Copy), R=(pb,), W=(b_U,))
                    if lvl < 2:
                        continue
                    for g4 in range(GH // 4):
                        ps, pb = self.bank(2 + g4 % 2)
                        for gi in range(4):
                            g = g4 * 4 + gi
                            for ri in range(2):
                                MM(lambda e, g=g, gi=gi, ri=ri, ps=ps: e.matmul(ps[0:64, (gi * 2 + ri) * NW:(gi * 2 + ri + 1) * NW], Bv[:, g, ri * 64:(ri + 1) * 64], Uall[:, g, :], start=True, stop=True),
                                   R=(Bb_, b_U), W=(pb,), sig=(gi == 3 and ri == 1))
                        V(lambda e, g4=g4, ps=ps: e.tensor_copy(out=Wall[:, :, g4 * 4:(g4 + 1) * 4, :].rearrange("p r g c -> p g r c"), in_=ps[0:64, 0:8 * NW].rearrange("p (g r c) -> p g r c", r=2, c=NW)), R=(pb,), W=(b_W,))
                    if lvl < 3:
                        continue
                    cs_ = s5c[:, l, :, g0:g0 + GH, :]
                    Wv = Wall[:].rearrange("p r g (s m) -> p r g s m", s=2)
                    Sv = Sbf[:].rearrange("p r g (s m) -> p r g s m", s=2)
                    Pl(lambda e: e.tensor_copy(out=Sv[:, :, :, :, 0], in_=cs_), R=(b_s5c[l],), W=(b_S,))
                    for m in range(NWS):
                        prev = cs_ if m == 0 else Wv[:, :, :, :, m - 1]
                        cur = Wv[:, :, :, :, m]
                        rb = (b_W, br8, b_s5c[l])
                        Pl(lambda e, prev=prev: e.tensor_tensor(out=sA[:], in0=prev, in1=r8re[:, g0:g0 + GH, :].unsqueeze(1).broadcast_to([64, 2, GH, 2]), op=ALU.mult), R=rb, W=(b_sc,))
                        Pl(lambda e, prev=prev: e.tensor_tensor(out=sB[:, 0], in0=prev[:, 1], in1=r8imn[:, g0:g0 + GH, :], op=ALU.mult), R=rb, W=(b_sc,))
                        Pl(lambda e, prev=prev: e.tensor_tensor(out=sB[:, 1], in0=prev[:, 0], in1=r8im[:, g0:g0 + GH, :], op=ALU.mult), R=rb, W=(b_sc,))
                        Pl(lambda e: e.tensor_tensor(out=sA[:], in0=sA[:], in1=sB[:], op=ALU.add), R=(b_sc,), W=(b_sc,))
                        Pl(lambda e, cur=cur: e.tensor_tensor(out=cur, in0=cur, in1=sA[:], op=ALU.add), R=(b_sc, b_W), W=(b_W,))
                    for s_ in range(2):
                        Pl(lambda e, s_=s_: e.tensor_copy(out=Sv[:, :, :, s_, 1:NWS], in_=Wv[:, :, :, s_, 0:NWS - 1]), R=(b_W,), W=(b_S,))
                    Pl(lambda e: e.tensor_copy(out=cs_, in_=Wv[:, :, :, :, NWS - 1]), R=(b_W, b_S), W=(b_s5c[l],))
                    if lvl < 4:
                        continue
                    Mv, Mb = load_slab(sc["M"][:, g0:g0 + GH, :], GH, 128, ())
                    i_ = self.ws_i % NS; self.ws_i += 1
                    Cv = wsl[i_][0:64, 0:2 * GH * 128].rearrange("p (r g c) -> p r g c", r=2, c=128)
                    S.dma("qs", Cv, sc["C"][:, :, g0:g0 + GH, :], W=(b_wsl[i_],))
                    Cre_v = Cv[:, 0]; Cim_v = Cv[:, 1]; Cre_b = b_wsl[i_]; Cim_b = b_wsl[i_]
                    for gb in range(GH // 8):
                        ps, pb = self.bank(4 + gb % 2)
                        for gi in range(8):
                            g = gb * 8 + gi
                            osl = ps[:, gi * NW:(gi + 1) * NW]
                            MM(lambda e, g=g, osl=osl: e.matmul(osl, Mv[:, g, :], Uall[:, g, :], start=True, stop=False), R=(Mb, b_U), W=(pb,))
                            MM(lambda e, g=g, osl=osl: e.matmul(osl, Cre_v[0:64, g, :], Sbf[:, 0, g, :], start=False, stop=False), R=(Cre_b, b_S), W=(pb,))
                            MM(lambda e, g=g, osl=osl: e.matmul(osl, Cim_v[0:64, g, :], Sbf[:, 1, g, :], start=False, stop=True), R=(Cim_b, b_S), W=(pb,), sig=(gi == 7))
                        A(lambda e, gb=gb, ps=ps: e.activation(out=Yw[:, gb * 8:(gb + 1) * 8, :], in_=ps[:, 0:8 * NW].rearrange("p (g c) -> p g c", c=NW), func=AF.Copy), R=(pb,), W=(b_Yw,))
                    if lvl < 5:
                        continue
                    for gb in range(GH // 8):
                        blk = (g0 // 8) + gb
                        ps, pb = self.bank(6 + gb % 2)
                        for t_ in range(8):
                            for jg in range(8):
                                MM(lambda e, t_=t_, jg=jg, gb=gb, ps=ps: e.matmul(ps[:, t_:NT:8], selbf[:, t_, jg, :], Yw[:, gb * 8 + jg, :], start=(jg == 0), stop=(jg == 7)),
                                   R=(b_Yw, bcn), W=(pb,), sig=(jg == 7 and t_ == 7))
                        V(lambda e, blk=blk, ps=ps: e.scalar_tensor_tensor(out=ypre[:], in0=uT[:, blk, :], scalar=cst["s5_d"][:, l, blk:blk + 1], in1=ps[:, 0:NT], op0=ALU.mult, op1=ALU.add), R=(pb, b_uT[blk], bcn), W=(b_yt,))
                        V(lambda e: e.tensor_tensor(out=sgl[:], in0=ypre[:], in1=ypre[:], op=ALU.mult), R=(b_yt,), W=(b_yt,))
                        V(lambda e: e.tensor_scalar(out=sgl[:], in0=sgl[:], scalar1=0.044715, scalar2=1.0, op0=ALU.mult, op1=ALU.add), R=(b_yt,), W=(b_yt,))
                        V(lambda e: e.tensor_tensor(out=sgl[:], in0=sgl[:], in1=ypre[:], op=ALU.mult), R=(b_yt,), W=(b_yt,))
                        A(lambda e: e.activation(out=sgl[:], in_=sgl[:], func=AF.Sigmoid, scale=1.5957691216057308), R=(b_yt,), W=(b_yt,))
                        V(lambda e, blk=blk: e.tensor_tensor(out=yact[:, blk, :], in0=ypre[:], in1=sgl[:], op=ALU.mult), R=(b_yt,), W=(b_ya[blk],))
                        V(lambda e, blk=blk: e.tensor_copy(out=yabf[:, blk, :], in_=yact[:, blk, :]), R=(b_ya[blk],), W=(b_yb[blk],))
                if lvl < 99:
                    if self.stop_after == "s5f" and ti == 0:
                        dt_ = self.sb("dbgtmp", [128, 4096])
                        bd = Buf("dbgtmp")
                        def dd(ap, parts, n, off, rb):
                            S.op("dve", lambda e: e.tensor_copy(out=dt_[0:parts, 0:n], in_=ap), R=rb + (bd,), W=(bd,))
                            S.dma("qs", self.dram["dbg2"][0:parts, off:off + n], dt_[0:parts, 0:n], R=(bd,), W=(Buf("z"),))
                            tk_ = bd.r[("qs", (S.qs["qs"].i - 1) % 8)] if False else None
                        dd(Uall[:].rearrange("p g c -> p (g c)"), 128, GH * NW, 0, (b_U,))
                        dd(Wall[:].rearrange("p r g c -> p (r g c)"), 64, 2 * GH * NW, 2048, (b_W,))
                        dd(Sbf[:].rearrange("p r g c -> p (r g c)"), 64, 2 * GH * NW, 6144, (b_S,))
                        dd(Yw[:].rearrange("p g c -> p (g c)"), 128, GH * NW, 10240, (b_Yw,))
                        dd(yact[:].rearrange("p k c -> p (k c)"), 128, KS * NT, 12288, tuple(b_ya))
                        dd(uT[:].rearrange("p k c -> p (k c)"), 128, KS * NT, 16384, tuple(b_uT))
                    self.debug_dump_bf(mix, b_mix, ti); stopped = True; break
                for j in range(cfg.NU):
                    wv, wb = load_slab(sc["glu"][j], KS, cfg.UW, (self.slab_ready[("glu", l, j)],))
                    for bi in range(cfg.UW // 128):
                        blk = j * (cfg.UW // 128) + bi
                        ps, pb = self.bank(blk % 4)
                        for k in range(KS):
                            MM(lambda e, k=k, bi=bi, ps=ps: e.matmul(ps[:, 0:NT], wv[:, k, bi * 128:(bi + 1) * 128], yabf[:, k, :], start=(k == 0), stop=(k == KS - 1)),
                               R=(wb, b_yb[k]), W=(pb,), sig=(k == KS - 1))
                        A(lambda e, blk=blk, ps=ps: e.activation(out=sgl[:], in_=ps[:, 0:NT], func=AF.Sigmoid, bias=cst["b_glu"][:, l, blk:blk + 1]), R=(pb, bcn), W=(b_yt,))
                        V(lambda e, blk=blk: e.tensor_tensor(out=yact[:, blk, :], in0=yact[:, blk, :], in1=sgl[:], op=ALU.mult), R=(b_yt, b_ya[blk]), W=(b_ya[blk],))
                for k in range(KS):
                    Pl(lambda e, k=k: e.tensor_tensor(out=yabf[:, k, :], in0=yact[:, k, :], in1=yact[:, k, :], op=ALU.mult), R=(b_ya[k], b_yb[k]), W=(b_yb[k],))
                ps, pb = self.bank(4)
                for k in range(KS):
                    MM(lambda e, k=k: e.matmul(ps[:, 0:NT], ones_bf[:], yabf[:, k, :], start=(k == 0), stop=(k == KS - 1)), R=(b_yb[k], bcn), W=(pb,), sig=(k == KS - 1))
                A(lambda e: e.activation(out=std[:], in_=ps[:, 0:NT], func=AF.Sqrt, scale=1.0 / cfg.DS5, bias=cst["eps"][:]), R=(pb, bcn), W=(b_std,))
                V(lambda e: e.reciprocal(out=rstd[:], in_=std[:]), R=(b_std,), W=(b_std,))
                for k in range(KS):
                    V(lambda e, k=k: e.scalar_tensor_tensor(out=mix[:, HH + k, :], in0=yact[:, k, :], scalar=cst["s5_gain"][:, l, k:k + 1], in1=rstd[:], op0=ALU.mult, op1=ALU.mult), R=(b_ya[k], b_std, bcn), W=(b_mix[HH + k],))
                if self.stop_after == "mix" and l == 0:
                    self.debug_dump_bf(mix, b_mix, ti); stopped = True; break
                nob = cfg.OW // 128
                for j in range(cfg.NO):
                    wv, wb = load_slab(sc["out"][j], KD, cfg.OW, (self.slab_ready[("out", l, j)],))
                    for bi in range(nob):
                        ob = j * nob + bi
                        ps, pb = self.bank(ob % 4)
                        for k in range(KD):
                            MM(lambda e, k=k, bi=bi, ps=ps: e.matmul(ps[:, 0:NT], wv[:, k, bi * 128:(bi + 1) * 128], mix[:, k, :], start=(k == 0), stop=(k == KD - 1)),
                               R=(wb, b_mix[k]), W=(pb,), sig=(k == KD - 1))
                        V(lambda e, ob=ob, ps=ps: e.tensor_tensor(out=hT[:, ob, :], in0=hT[:, ob, :], in1=ps[:, 0:NT], op=ALU.add), R=(pb, b_h[ob]), W=(b_h[ob],))
                if self.stop_after == "attn" and l == 0:
                    self.write_out(hT, b_h, xin, b_xin, ti); stopped = True; break
                phase()
                rmsnorm_to_xn(lambda k: cst["g_ffn"][:, l, k:k + 1])
                nfb = cfg.FW // 128
                for j in range(cfg.NF):
                    gv, gbuf = load_slab(sc["gate"][j], KD, cfg.FW, (self.slab_ready[("gate", l, j)],))
                    uv, ubuf = load_slab(sc["up"][j], KD, cfg.FW, (self.slab_ready[("up", l, j)],))
                    for bi in range(nfb):
                        fb = j * nfb + bi; pp = fb % 2
                        pa, ba = self.bank(2 * pp); pu, bu = self.bank(2 * pp + 1)
                        for k in range(KD):
                            MM(lambda e, k=k, bi=bi, pa=pa: e.matmul(pa[:, 0:NT], gv[:, k, bi * 128:(bi + 1) * 128], xn[:, k, :], start=(k == 0), stop=(k == KD - 1)), R=(gbuf, b_xn[k]), W=(ba,), sig=(k == KD - 1))
                        for k in range(KD):
                            MM(lambda e, k=k, bi=bi, pu=pu: e.matmul(pu[:, 0:NT], uv[:, k, bi * 128:(bi + 1) * 128], xn[:, k, :], start=(k == 0), stop=(k == KD - 1)), R=(ubuf, b_xn[k]), W=(bu,), sig=(k == KD - 1))
                        ae = aext[pp]; ca = cacc[pp]; cg = csg[pp]; bf_ = b_ff[pp]
                        cw = cst["conv_w"]
                        Pl(lambda e, fb=fb, ae=ae: e.tensor_copy(out=ae[:, :, 0:2], in_=cvc[:, l, fb, :, :]), R=(b_cvc[l],), W=(bf_,))
                        A(lambda e, ae=ae, pa=pa: e.activation(out=ae[:, :, 2:TT + 2], in_=pa[:, 0:NT].rearrange("p (s t) -> p s t", s=2), func=AF.Copy), R=(ba,), W=(bf_,))
                        Pl(lambda e, fb=fb, ae=ae: e.tensor_copy(out=cvc[:, l, fb, :, :], in_=ae[:, :, TT:TT + 2]), R=(bf_,), W=(b_cvc[l],))
                        V(lambda e, fb=fb, ae=ae, ca=ca: e.tensor_scalar(out=ca[:], in0=ae[:, :, 2:TT + 2], scalar1=cw[:, l, 2, fb:fb + 1], scalar2=cst["conv_b"][:, l, fb:fb + 1], op0=ALU.mult, op1=ALU.add), R=(bf_, bcn), W=(bf_,))
                        V(lambda e, fb=fb, ae=ae, ca=ca: e.scalar_tensor_tensor(out=ca[:], in0=ae[:, :, 1:TT + 1], scalar=cw[:, l, 1, fb:fb + 1], in1=ca[:], op0=ALU.mult, op1=ALU.add), R=(bf_, bcn), W=(bf_,))
                        V(lambda e, fb=fb, ae=ae, ca=ca: e.scalar_tensor_tensor(out=ca[:], in0=ae[:, :, 0:TT], scalar=cw[:, l, 0, fb:fb + 1], in1=ca[:], op0=ALU.mult, op1=ALU.add), R=(bf_, bcn), W=(bf_,))
                        A(lambda e, ca=ca, cg=cg: e.activation(out=cg[:], in_=ca[:], func=AF.Silu), R=(bf_,), W=(bf_,))
                        V(lambda e, fb=fb, cg=cg, pu=pu: e.tensor_tensor(out=hid[:, fb, :], in0=cg[:].rearrange("p s t -> p (s t)"), in1=pu[:, 0:NT], op=ALU.mult), R=(bf_, bu), W=(b_hid[fb],))
                for j in range(cfg.NO):
                    banks = [self.bank(4 + bi) for bi in range(nob)]
                    for pi, (k0, nk) in enumerate(cfg.KPIECES):
                        wv, wb = load_slab(sc["down"][j, pi, :, 0:nk, :], nk, cfg.OW, (self.slab_ready[("down", l, j, pi)],))
                        for bi in range(nob):
                            ps, pb = banks[bi]
                            for kk in range(nk):
                                kf_ = k0 + kk
                                MM(lambda e, kk=kk, kf_=kf_, bi=bi, ps=ps: e.matmul(ps[:, 0:NT], wv[:, kk, bi * 128:(bi + 1) * 128], hid[:, kf_, :], start=(kf_ == 0), stop=(kf_ == KF - 1)),
                                   R=(wb, b_hid[kf_]), W=(pb,), sig=(kk == nk - 1))
                    for bi in range(nob):
                        ob = j * nob + bi; ps, pb = banks[bi]
                        V(lambda e, ob=ob, ps=ps: e.tensor_tensor(out=hT[:, ob, :], in0=hT[:, ob, :], in1=ps[:, 0:NT], op=ALU.add), R=(pb, b_h[ob]), W=(b_h[ob],))
                if self.stop_after == "l0" and l == 0:
                    self.write_out(hT, b_h, xin, b_xin, ti); stopped = True; break
            if stopped:
                continue
            if True:
                phase()
                for k in range(KD):
                    Pl(lambda e, k=k: e.tensor_tensor(out=xn[:, k, :], in0=hT[:, k, :], in1=hT[:, k, :], op=ALU.mult), R=(b_h[k],), W=(b_xn[k],))
                ps, pb = self.bank(0)
                for k in range(KD):
                    MM(lambda e, k=k: e.matmul(ps[:, 0:NT], ones_bf[:], xn[:, k, :], start=(k == 0), stop=(k == KD - 1)), R=(b_xn[k], bcn), W=(pb,), sig=(k == KD - 1))
                A(lambda e: e.activation(out=std[:], in_=ps[:, 0:NT], func=AF.Sqrt, scale=1.0 / cfg.D, bias=cst["eps"][:]), R=(pb, bcn), W=(b_std,))
                V(lambda e: e.reciprocal(out=rstd[:], in_=std[:]), R=(b_std,), W=(b_std,))
                for k in range(KD):
                    V(lambda e, k=k: e.scalar_tensor_tensor(out=hT[:, k, :], in0=hT[:, k, :], scalar=cst["g_fin"][:, k:k + 1], in1=rstd[:], op0=ALU.mult, op1=ALU.mult), R=(b_h[k], b_std, bcn), W=(b_h[k],))
                self.write_out(hT, b_h, xin, b_xin, ti)

    def write_out(self, src, b_src, xin, b_xin, ti, nblk=None):
        cfg = self.cfg; S = self.S; idf = self.cst["ident"]; bcn = self.b_const
        nblk = cfg.KD if nblk is None else nblk
        for tg in range(3):
            c0 = tg * 128
            for k in range(nblk):
                ps, pb = self.bank(k % 4)
                S.op("pe", lambda e, k=k, ps=ps: e.matmul(ps[:, 0:128], src[:, k, c0:c0 + 128], idf[:], start=True, stop=True), R=(b_src[k], bcn), W=(pb,), sig=True)
                S.op("act", lambda e, k=k, ps=ps: e.activation(out=xin[:, k * 128:(k + 1) * 128], in_=ps[:, 0:128], func=AF.Copy), R=(pb,), W=(b_xin,))
            c = c0
            while c < c0 + 128:
                s_ = c // TT; j = c % TT; n = min(TT - j, c0 + 128 - c)
                tk = S.dma("qs", self.dram["out"][s_, ti * TT + j:ti * TT + j + n, 0:nblk * 128], xin[c - c0:c - c0 + n, 0:nblk * 128], R=(b_xin,), W=(Buf("o"),))
                self.out_tks.append(tk)
                c += n

    def debug_dump_bf(self, src, b_src, ti):
        cfg = self.cfg; S = self.S
        if not hasattr(self, "dbg_t"):
            self.dbg_t = self.sb("dbg_t", [128, cfg.KD, NT]); self.dbg_b = [Buf(f"dbg{k}") for k in range(cfg.KD)]
        for k in range(cfg.KD):
            S.op("dve", lambda e, k=k: e.tensor_copy(out=self.dbg_t[:, k, :], in_=src[:, k, :]), R=(b_src[k],), W=(self.dbg_b[k],))
        xin = self.xin_dbg if hasattr(self, "xin_dbg") else None
        if xin is None:
            self.xin_dbg = self.sb("xin_dbg", [128, cfg.D]); self.xin_dbg_b = Buf("xind")
        self.write_out(self.dbg_t, self.dbg_b, self.xin_dbg, self.xin_dbg_b, ti)


_CACHE = {}


def run_cfg(inputs, cfg, stop_after=None):
    maps = host_prep(inputs, cfg)
    key = (cfg.D, cfg.DFF, cfg.SEQ, cfg.L, stop_after)
    if key not in _CACHE:
        _CACHE[key] = Builder(cfg, stop_after).build()
    nc = _CACHE[key]
    ncores = cfg.NB // 2
    res = run_bass_kernel_spmd(nc, maps, core_ids=list(range(ncores)))
    if stop_after == "s5p":
        return np.asarray(res.results[0]["dbg"])
    if stop_after == "s5f":
        return np.asarray(res.results[0]["dbg2"])
    outs = [np.asarray(r["out"]) for r in res.results]
    full = np.concatenate([o[:, 64:, :] for o in outs], axis=0)
    return np.ascontiguousarray(full.astype(np.float32))


def kernel(**inputs):
    cfg = Cfg()
    return run_cfg(inputs, cfg)
```

```python
import math
from contextlib import ExitStack
import numpy as np
import concourse.bass as bass
import concourse.mybir as mybir
from concourse.bass_utils import run_bass_kernel_spmd

F32 = mybir.dt.float32
BF16 = mybir.dt.bfloat16
I32 = mybir.dt.int32
AF = mybir.ActivationFunctionType
ALU = mybir.AluOpType
P = 128
EPS = 1e-6
F_FLOOR = 1e-6
TT = 192
NT = 2 * TT
NW = NT // 8
NWS = TT // 8


class Cfg:
    def __init__(self, D=2048, DFF=5632, SEQ=2048, L=2, NB=16):
        self.D = D; self.DFF = DFF; self.SEQ = SEQ; self.L = L; self.NB = NB
        self.DHG = D // 2; self.HH = self.DHG // 128
        self.DS5 = D - self.DHG; self.G = self.DS5 // 16; self.GB = self.G // 8
        self.KS = self.DS5 // 128
        self.DIN = 4 * self.DHG + self.DS5
        self.KD = D // 128; self.KF = DFF // 128
        self.LP = SEQ + 64
        assert self.LP % TT == 0
        self.NTILE = self.LP // TT
        self.UW = min(512, self.DS5); self.NU = self.DS5 // self.UW
        self.OW = min(512, D); self.NO = D // self.OW
        self.FW = 512 if DFF % 512 == 0 else (384 if DFF % 384 == 0 else 128)
        self.NF = DFF // self.FW
        self.KPIECES = [(k0, min(16, self.KF - k0)) for k0 in range(0, self.KF, 16)]


class Buf:
    __slots__ = ("name", "w", "r")

    def __init__(self, name):
        self.name = name; self.w = None; self.r = {}


class Eng:
    def __init__(self, name, h, kind):
        self.name = name; self.h = h; self.kind = kind; self.n = 0; self.waited = {}


class DQ:
    def __init__(self, name, eng, sems):
        self.name = name; self.eng = eng; self.sems = sems; self.i = 0


CAP = 20000


class Sync:
    def __init__(self, nc, es):
        self.nc = nc; self.es = es
        self.engs = {}
        self.qs = {}
        self.semlists = {}
        self.trace = {}

    def add_eng(self, name, h, kind="c"):
        self.engs[name] = Eng(name, h, kind); self.semlists[name] = []

    def add_q(self, name, engname, K=8):
        sems = [self.es.enter_context(self.nc.semaphore(f"q_{name}_{i}")) for i in range(K)]
        self.qs[name] = DQ(name, self.engs[engname], sems)

    def _sem(self, name, idx):
        lst = self.semlists[name]
        while len(lst) <= idx:
            lst.append(self.es.enter_context(self.nc.semaphore(f"e_{name}_{len(lst)}")))
        return lst[idx]

    def _wait(self, E, k, v):
        if v <= 0 or E.waited.get(k, 0) >= v:
            return
        if k[0] in self.qs:
            E.h.wait_ge(self.qs[k[0]].sems[k[1]], 16 * v)
        else:
            E.h.wait_ge(self._sem(k[0], k[1]), v)
        E.waited[k] = v
        self.trace.setdefault(E.name, []).append(("w", k, v))

    def _deps(self, E, R, W):
        deps = {}
        for b in R:
            t = b.w
            if t is not None and deps.get(t[0], 0) < t[1]:
                deps[t[0]] = t[1]
        for b in W:
            t = b.w
            if t is not None and deps.get(t[0], 0) < t[1]:
                deps[t[0]] = t[1]
            for t in b.r.values():
                if deps.get(t[0], 0) < t[1]:
                    deps[t[0]] = t[1]
        for k, v in deps.items():
            if k[0] == E.name and E.kind == "pe":
                continue
            self._wait(E, k, v)

    def _mark(self, tk, R, W):
        for b in R:
            o = b.r.get(tk[0])
            if o is None or o[1] < tk[1]:
                b.r[tk[0]] = tk
        for b in W:
            b.w = tk; b.r = {}

    def _tk(self, en, n):
        return ((en, (n - 1) // CAP), (n - 1) % CAP + 1)

    def op(self, en, fn, R=(), W=(), sig=True):
        E = self.engs[en]
        self._deps(E, R, W)
        ins = fn(E.h)
        if sig:
            E.n += 1
            tk = self._tk(en, E.n)
            ins.then_inc(self._sem(en, tk[0][1]), 1)
            self.trace.setdefault(en, []).append(("s", tk[0], tk[1]))
        else:
            tk = self._tk(en, E.n + 1)
        self._mark(tk, R, W)
        return tk

    def dma(self, qn, out, in_, R=(), W=()):
        q = self.qs[qn]; E = q.eng
        self._deps(E, R, W)
        K = len(q.sems); i = q.i; q.i += 1
        lane = i % K; cnt = i // K + 1
        self._wait(E, (qn, lane), cnt - 1)
        ins = E.h.dma_start(out=out, in_=in_)
        ins.then_inc(q.sems[lane], 16)
        tk = ((qn, lane), cnt)
        self.trace.setdefault(E.name, []).append(("s", tk[0], tk[1]))
        self._mark(tk, R, W)
        return tk

    def check_deadlock(self):
        pc = {e: 0 for e in self.trace}
        sem = {}
        progress = True
        while progress:
            progress = False
            for e, tr in self.trace.items():
                while pc[e] < len(tr):
                    kind, k, v = tr[pc[e]]
                    if kind == "w":
                        if sem.get(k, 0) >= v:
                            pc[e] += 1; progress = True
                        else:
                            break
                    else:
                        assert sem.get(k, 0) == v - 1, (e, k, v, sem.get(k, 0))
                        sem[k] = v; pc[e] += 1; progress = True
        stuck = {e: (pc[e], len(tr), tr[pc[e]]) for e, tr in self.trace.items() if pc[e] < len(tr)}
        return stuck

    def final_wait(self, en, tks):
        E = self.engs[en]
        for tk in tks:
            self._wait(E, tk[0], tk[1])
        for qn, q in self.qs.items():
            K = len(q.sems)
            for lane in range(K):
                cnt = (q.i - lane + K - 1) // K
                self._wait(E, (qn, lane), cnt)

    def barrier(self, en, bufs):
        tk = self.op(en, lambda e: e.memset(self.bar_t[:], 0.0), R=tuple(bufs), W=tuple(bufs) + (self.bar_b,))
        for E in self.engs.values():
            self._wait(E, tk[0], tk[1])


def make_consts():
    c = {}
    c["ident"] = np.eye(128, dtype=np.float32)
    c["ones"] = np.ones((128, 128), np.float32)
    s = np.arange(128)
    same = (s[:, None] // 64) == (s[None, :] // 64)
    c["tri2"] = (same & (s[:, None] <= s[None, :])).astype(np.float32)
    sel = np.zeros((128, 8, 8, 128), np.float32)
    for a_ in range(8):
        for b_ in range(8):
            for hi in range(16):
                sel[16 * a_ + hi, a_, b_, 16 * b_ + hi] = 1.0
    c["sel"] = sel
    a = np.arange(128) // 16
    c["mmask"] = (a[None, :] >= a[:, None]).astype(np.float32)
    cm = np.ones((128, NT), np.float32); cm[:, ::64] = 0.0
    c["cmask"] = cm
    return c


def host_prep(inp, cfg):
    L = cfg.L; f = np.float32
    sh = {}
    perm = []
    for h in range(cfg.HH):
        for part in range(4):
            perm.extend(range(part * cfg.DHG + h * 128, part * cfg.DHG + (h + 1) * 128))
    perm.extend(range(4 * cfg.DHG, cfg.DIN))
    sh["w_in"] = np.ascontiguousarray(np.asarray(inp["w_in"], f)[:, :, perm])
    for k in ("w_glu", "w_out", "w_ffn_gate", "w_ffn_up", "w_ffn_down"):
        sh[k] = np.ascontiguousarray(np.asarray(inp[k], f))

    def fm(v, nblk):
        return np.ascontiguousarray(np.asarray(v, f).reshape(L, nblk, 128).transpose(2, 0, 1))
    sh["g_mix"] = fm(inp["norm_mix"], cfg.KD)
    sh["g_ffn"] = fm(inp["norm_ffn"], cfg.KD)
    sh["g_fin"] = np.ascontiguousarray(np.asarray(inp["final_norm"], f).reshape(cfg.KD, 128).T)
    sh["hg_gain"] = fm(inp["hg_norm"], cfg.HH)
    sh["s5_gain"] = fm(inp["s5_norm"], cfg.KS)
    sh["s5_d"] = fm(inp["s5_d"], cfg.KS)
    sh["b_glu"] = fm(inp["b_glu"], cfg.KS)
    sh["lbl_fm"] = fm(inp["lb_logits"], cfg.HH)
    cw = np.asarray(inp["ffn_conv_w"], f)
    sh["conv_w"] = np.ascontiguousarray(cw.reshape(L, 3, cfg.KF, 128).transpose(3, 0, 1, 2))
    sh["conv_b"] = fm(inp["ffn_conv_b"], cfg.KF)
    sh["lam_re"] = np.ascontiguousarray(np.asarray(inp["s5_lambda_re"], f).transpose(2, 0, 1))
    sh["lam_im"] = np.ascontiguousarray(np.asarray(inp["s5_lambda_im"], f).transpose(2, 0, 1))
    sh["lstep"] = np.ascontiguousarray(np.broadcast_to(np.asarray(inp["s5_log_step"], f)[None], (64, L, cfg.G)))
    sh["b_re"] = np.ascontiguousarray(np.asarray(inp["s5_b_re"], f).transpose(2, 0, 1, 3))
    sh["b_im"] = np.ascontiguousarray(np.asarray(inp["s5_b_im"], f).transpose(2, 0, 1, 3))
    sh["c_re"] = np.ascontiguousarray(np.asarray(inp["s5_c_re"], f).transpose(3, 0, 1, 2))
    sh["c_im"] = np.ascontiguousarray(np.asarray(inp["s5_c_im"], f).transpose(3, 0, 1, 2))
    sh.update(make_consts())
    x = np.asarray(inp["x"], f); meta = np.asarray(inp["meta_tokens"], f)
    ncores = cfg.NB // 2
    maps = []
    for c in range(ncores):
        xp = np.zeros((2, cfg.LP, cfg.D), f)
        xp[:, 48:64] = meta[None]
        xp[:, 64:] = x[2 * c:2 * c + 2]
        m = dict(sh); m["xp"] = xp
        maps.append(m)
    return maps


def input_shapes(cfg):
    L = cfg.L
    return {
        "xp": [2, cfg.LP, cfg.D], "w_in": [L, cfg.D, cfg.DIN], "w_glu": [L, cfg.DS5, cfg.DS5],
        "w_out": [L, cfg.D, cfg.D], "w_ffn_gate": [L, cfg.D, cfg.DFF], "w_ffn_up": [L, cfg.D, cfg.DFF],
        "w_ffn_down": [L, cfg.DFF, cfg.D],
        "g_mix": [128, L, cfg.KD], "g_ffn": [128, L, cfg.KD], "g_fin": [128, cfg.KD],
        "hg_gain": [128, L, cfg.HH], "s5_gain": [128, L, cfg.KS], "s5_d": [128, L, cfg.KS],
        "b_glu": [128, L, cfg.KS], "lbl_fm": [128, L, cfg.HH],
        "conv_w": [128, L, 3, cfg.KF], "conv_b": [128, L, cfg.KF],
        "lam_re": [64, L, cfg.G], "lam_im": [64, L, cfg.G], "lstep": [64, L, cfg.G],
        "b_re": [64, L, cfg.G, 16], "b_im": [64, L, cfg.G, 16], "c_re": [64, L, cfg.G, 16], "c_im": [64, L, cfg.G, 16],
        "ident": [128, 128], "ones": [128, 128], "tri2": [128, 128],
        "sel": [128, 8, 8, 128], "mmask": [128, 128], "cmask": [128, NT],
    }


class Builder:
    def __init__(self, cfg, stop_after=None):
        self.cfg = cfg
        self.stop_after = stop_after

    def sb(self, name, shape, dt=F32):
        return self.es.enter_context(self.nc.sbuf_tensor(name, list(shape), dt))

    def build(self):
        cfg = self.cfg
        nc = bass.Bass("TRN2", target_bir_lowering=False)
        self.nc = nc
        with ExitStack() as es:
            self.es = es
            S = Sync(nc, es); self.S = S
            S.add_eng("pe", nc.tensor, "pe"); S.add_eng("act", nc.scalar); S.add_eng("dve", nc.vector)
            S.add_eng("pool", nc.gpsimd); S.add_eng("sp", nc.sync)
            S.add_q("qs", "sp", 8); S.add_q("qg", "pool", 8)
            S.bar_t = self.sb("bar_t", [128, 1]); S.bar_b = Buf("bar")
            self.dram = {}
            for k, shp in input_shapes(cfg).items():
                self.dram[k] = nc.dram_tensor(k, shp, F32, kind="ExternalInput").ap()
            self.dram["out"] = nc.dram_tensor("out", [2, cfg.LP, cfg.D], F32, kind="ExternalOutput").ap()
            if self.stop_after == "s5p":
                self.dram["dbg"] = nc.dram_tensor("dbg", [128, 8192], F32, kind="ExternalOutput").ap()
            if self.stop_after == "s5f":
                self.dram["dbg2"] = nc.dram_tensor("dbg2", [128, 32768], F32, kind="ExternalOutput").ap()
            self.psum = [es.enter_context(nc.psum_tensor(f"ps{i}", [128, 512], F32)) for i in range(8)]
            self.pbuf = [Buf(f"ps{i}") for i in range(8)]
            self.emit_all()
            stuck = S.check_deadlock()
            if stuck:
                raise RuntimeError(f"sync deadlock: {stuck}")
        return nc

    def bank(self, i):
        return (self.psum[i], self.pbuf[i])

    def emit_all(self):
        cfg = self.cfg; nc = self.nc; S = self.S; sb = self.sb
        L = cfg.L; KD = cfg.KD
        self.out_tks = []
        cst = {}
        self.cst = cst
        self.b_const = Buf("const")
        bc_ = self.b_const
        for key in ("g_mix", "g_ffn", "g_fin", "hg_gain", "s5_gain", "s5_d", "b_glu", "lbl_fm", "conv_w", "conv_b",
                    "ident", "ones", "tri2", "mmask", "cmask"):
            t = sb("c_" + key, input_shapes(cfg)[key], F32)
            S.dma("qs", t[:], self.dram[key], W=(bc_,))
            cst[key] = t
        cst["ones_bf"] = sb("ones_bf", [128, 128], BF16)
        cst["tri_bf"] = sb("tri_bf", [128, 128], BF16)
        cst["sel_bf"] = sb("sel_bf", [128, 8, 8, 128], BF16)
        cst["eps"] = sb("eps_t", [128, 1], F32)
        S.op("dve", lambda e: e.tensor_copy(out=cst["ones_bf"][:], in_=cst["ones"][:]), R=(bc_,), W=(bc_,))
        S.op("dve", lambda e: e.tensor_copy(out=cst["tri_bf"][:], in_=cst["tri2"][:]), R=(bc_,), W=(bc_,))
        S.op("dve", lambda e: e.memset(cst["eps"][:], EPS), W=(bc_,))
        with nc.sbuf_tensor("sel_tmp", [128, 8, 8, 128], F32) as selt:
            bt = Buf("selt")
            S.dma("qs", selt[:], self.dram["sel"], W=(bt,))
            S.op("dve", lambda e: e.tensor_copy(out=cst["sel_bf"][:], in_=selt[:]), R=(bt,), W=(bc_,))
            S.barrier("dve", (bt, bc_))
        self.scr = {}
        for l in range(L):
            d = {}
            d["head"] = nc.dram_tensor(f"s_head{l}", [cfg.HH, 128, KD, 512], BF16, kind="Internal").ap()
            d["u"] = nc.dram_tensor(f"s_u{l}", [cfg.NU, 128, KD, cfg.UW], BF16, kind="Internal").ap()
            d["glu"] = nc.dram_tensor(f"s_glu{l}", [cfg.NU, 128, cfg.KS, cfg.UW], BF16, kind="Internal").ap()
            d["out"] = nc.dram_tensor(f"s_out{l}", [cfg.NO, 128, KD, cfg.OW], BF16, kind="Internal").ap()
            d["gate"] = nc.dram_tensor(f"s_gate{l}", [cfg.NF, 128, KD, cfg.FW], BF16, kind="Internal").ap()
            d["up"] = nc.dram_tensor(f"s_up{l}", [cfg.NF, 128, KD, cfg.FW], BF16, kind="Internal").ap()
            d["down"] = nc.dram_tensor(f"s_down{l}", [cfg.NO, len(cfg.KPIECES), 128, 16, cfg.OW], BF16, kind="Internal").ap()
            d["M"] = nc.dram_tensor(f"s_M{l}", [128, cfg.G, 128], BF16, kind="Internal").ap()
            d["B"] = nc.dram_tensor(f"s_B{l}", [128, cfg.G, 128], BF16, kind="Internal").ap()
            d["C"] = nc.dram_tensor(f"s_C{l}", [64, 2, cfg.G, 128], BF16, kind="Internal").ap()
            self.scr[l] = d
        self.slab_ready = {}
        def fin():
            for b in list(self.slab_ready.values()) + [self.b_const]:
                if b.w is not None:
                    self.out_tks.append(b.w)
            S.barrier("dve", (self.b_const,))
            S.final_wait("sp", self.out_tks)
        if self.stop_after == "consts":
            return fin()
        import os
        if not os.environ.get("SKIP_PRE"):
            self.emit_precast()
        if self.stop_after == "precast":
            return fin()
        self.emit_lb()
        if self.stop_after == "lb":
            return fin()
        if not os.environ.get("SKIP_PRE"):
            self.emit_s5_prologue()
        if self.stop_after == "s5p":
            return fin()
        self.emit_main()
        S.final_wait("sp", self.out_tks)

    def emit_precast(self):
        cfg = self.cfg; S = self.S; D = self.dram; nc = self.nc
        jobs = []

        def v(ap):
            return ap.rearrange("(k p) c -> p k c", p=128)
        for l in range(cfg.L):
            sc = self.scr[l]
            for h in range(cfg.HH):
                jobs.append((("head", l, h), sc["head"][h], v(D["w_in"][l, :, h * 512:(h + 1) * 512]), cfg.KD, 512))
            for j in range(cfg.NU):
                c0 = cfg.HH * 512 + j * cfg.UW
                jobs.append((("u", l, j), sc["u"][j], v(D["w_in"][l, :, c0:c0 + cfg.UW]), cfg.KD, cfg.UW))
            for j in range(cfg.NU):
                jobs.append((("glu", l, j), sc["glu"][j], v(D["w_glu"][l, :, j * cfg.UW:(j + 1) * cfg.UW]), cfg.KS, cfg.UW))
            for j in range(cfg.NO):
                jobs.append((("out", l, j), sc["out"][j], v(D["w_out"][l, :, j * cfg.OW:(j + 1) * cfg.OW]), cfg.KD, cfg.OW))
            for j in range(cfg.NF):
                jobs.append((("gate", l, j), sc["gate"][j], v(D["w_ffn_gate"][l, :, j * cfg.FW:(j + 1) * cfg.FW]), cfg.KD, cfg.FW))
                jobs.append((("up", l, j), sc["up"][j], v(D["w_ffn_up"][l, :, j * cfg.FW:(j + 1) * cfg.FW]), cfg.KD, cfg.FW))
            for j in range(cfg.NO):
                for pi, (k0, nk) in enumerate(cfg.KPIECES):
                    jobs.append((("down", l, j, pi), sc["down"][j, pi, :, 0:nk, :],
                                 v(D["w_ffn_down"][l, k0 * 128:(k0 + nk) * 128, j * cfg.OW:(j + 1) * cfg.OW]), nk, cfg.OW))
        NB_ = 3
        with ExitStack() as es2:
            s32 = [es2.enter_context(nc.sbuf_tensor(f"pc32_{i}", [128, 8192], F32)) for i in range(NB_)]
            s16 = [es2.enter_context(nc.sbuf_tensor(f"pc16_{i}", [128, 8192], BF16)) for i in range(NB_)]
            b32 = [Buf(f"pc32_{i}") for i in range(NB_)]; b16 = [Buf(f"pc16_{i}") for i in range(NB_)]
            engs = ["dve", "act", "pool"]
            for n, (key, dst, src, nk, cw) in enumerate(jobs):
                i = n % NB_
                v32 = s32[i][:, 0:nk * cw].rearrange("p (k c) -> p k c", c=cw)
                v16 = s16[i][:, 0:nk * cw].rearrange("p (k c) -> p k c", c=cw)
                S.dma("qs", v32, src, W=(b32[i],))
                en = engs[n % 3]
                if en == "act":
                    S.op("act", lambda e, i=i, nk=nk, cw=cw: e.activation(out=s16[i][:, 0:nk * cw], in_=s32[i][:, 0:nk * cw], func=AF.Copy), R=(b32[i],), W=(b16[i],))
                else:
                    S.op(en, lambda e, i=i, nk=nk, cw=cw: e.tensor_copy(out=s16[i][:, 0:nk * cw], in_=s32[i][:, 0:nk * cw]), R=(b32[i],), W=(b16[i],))
                rb = Buf(str(key)); self.slab_ready[key] = rb
                S.dma("qs", dst, v16, R=(b16[i],), W=(rb,))
            S.barrier("dve", tuple(b32) + tuple(b16) + tuple(self.slab_ready.values()))

    def emit_lb(self):
        cfg = self.cfg; S = self.S; sb = self.sb; L = cfg.L; W_ = cfg.HH
        lg = self.cst["lbl_fm"]; b = Buf("lb"); rb = (self.b_const, b)
        mx = sb("lb_mx", [128, W_]); ex = sb("lb_ex", [128, L, W_]); sm = sb("lb_sm", [128, W_])
        lb = sb("lb_fm", [128, L, W_]); oml = sb("oml_fm", [128, L, W_]); noml = sb("noml_fm", [128, L, W_])
        V = lambda fn: S.op("dve", fn, R=rb, W=(b,))
        V(lambda e: e.tensor_copy(out=mx[:], in_=lg[:, 0, :]))
        for l in range(1, L):
            V(lambda e, l=l: e.tensor_tensor(out=mx[:], in0=mx[:], in1=lg[:, l, :], op=ALU.max))
        for l in range(L):
            V(lambda e, l=l: e.tensor_tensor(out=ex[:, l, :], in0=lg[:, l, :], in1=mx[:], op=ALU.subtract))
        S.op("act", lambda e: e.activation(out=ex[:], in_=ex[:], func=AF.Exp), R=(b,), W=(b,))
        V(lambda e: e.tensor_copy(out=sm[:], in_=ex[:, 0, :]))
        for l in range(1, L):
            V(lambda e, l=l: e.tensor_tensor(out=sm[:], in0=sm[:], in1=ex[:, l, :], op=ALU.add))
        V(lambda e: e.reciprocal(out=sm[:], in_=sm[:]))
        V(lambda e: e.memset(lb[:, 0, :], 0.0))
        for l in range(1, L):
            V(lambda e, l=l: e.tensor_tensor(out=ex[:, l, :], in0=ex[:, l, :], in1=sm[:], op=ALU.mult))
            V(lambda e, l=l: e.tensor_tensor(out=lb[:, l, :], in0=lb[:, l - 1, :], in1=ex[:, l, :], op=ALU.add))
        V(lambda e: e.tensor_scalar(out=oml[:], in0=lb[:], scalar1=-1.0, scalar2=1.0, op0=ALU.mult, op1=ALU.add))
        V(lambda e: e.tensor_scalar(out=noml[:], in0=oml[:], scalar1=-1.0, scalar2=None, op0=ALU.mult))
        self.lb = (lb, oml, noml, b)

    def emit_s5_prologue(self):
        cfg = self.cfg; S = self.S; nc = self.nc; G = cfg.G
        TWO_PI = 2.0 * math.pi
        self.r8 = {}
        self.s5_ready = {}
        for l in range(cfg.L):
            r8re = self.sb(f"r8re{l}", [64, G, 2]); r8im = self.sb(f"r8im{l}", [64, G, 2]); r8imn = self.sb(f"r8imn{l}", [64, G, 2])
            br8 = Buf(f"r8_{l}")
            self.r8[l] = (r8re, r8im, r8imn, br8)
            for nm in ("M", "B", "C"):
                self.s5_ready[(nm, l)] = Buf(f"s5r{nm}{l}")
            with ExitStack() as es2:
                def t(name, shape, dt=F32):
                    return es2.enter_context(nc.sbuf_tensor(f"p{l}_{name}", list(shape), dt))
                b = Buf("s5p")
                lre = t("lre", [64, G]); lim = t("lim", [64, G]); lst = t("lst", [64, G])
                S.dma("qs", lre[:], self.dram["lam_re"][:, l, :], W=(b,))
                S.dma("qs", lim[:], self.dram["lam_im"][:, l, :], W=(b,))
                S.dma("qs", lst[:], self.dram["lstep"][:, l, :], W=(b,))
                bre = t("bre", [64, G, 16]); bim = t("bim", [64, G, 16]); cre = t("cre", [64, G, 16]); cim = t("cim", [64, G, 16])
                for tt_, key in ((bre, "b_re"), (bim, "b_im"), (cre, "c_re"), (cim, "c_im")):
                    S.dma("qs", tt_[:], self.dram[key][:, l], W=(b,))
                are = t("are", [64, G]); dt_ = t("dt", [64, G]); xr = t("xr", [64, G]); th = t("th", [64, G])
                V = lambda fn: S.op("dve", fn, R=(b,), W=(b,))
                A = lambda fn: S.op("act", fn, R=(b,), W=(b,))
                V(lambda e: e.tensor_scalar(out=are[:], in0=lre[:], scalar1=-1e-4, scalar2=None, op0=ALU.min))
                A(lambda e: e.activation(out=dt_[:], in_=lst[:], func=AF.Exp))
                V(lambda e: e.tensor_tensor(out=xr[:], in0=are[:], in1=dt_[:], op=ALU.mult))
                V(lambda e: e.tensor_tensor(out=th[:], in0=lim[:], in1=dt_[:], op=ALU.mult))
                Ere = t("Ere", [64, 9, G]); Eim = t("Eim", [64, 9, G]); Nre = t("Nre", [64, 9, G]); Nim = t("Nim", [64, 9, G])
                mp = t("mp", [64, G]); mn = t("mn", [64, G]); ang = t("ang", [64, G]); kf = t("kf", [64, G]); ki = t("ki", [64, G], I32)
                sn = t("sn", [64, G]); cs = t("cs", [64, G])
                V(lambda e: e.memset(Ere[:, 0, :], 1.0)); V(lambda e: e.memset(Eim[:, 0, :], 0.0))
                V(lambda e: e.memset(Nre[:, 0, :], 1.0)); V(lambda e: e.memset(Nim[:, 0, :], 0.0))

                def sin_of(dst, k, shift):
                    V(lambda e: e.tensor_scalar(out=ang[:], in0=th[:], scalar1=float(k), scalar2=float(shift), op0=ALU.mult, op1=ALU.add))
                    V(lambda e: e.tensor_scalar(out=kf[:], in0=ang[:], scalar1=1.0 / TWO_PI, scalar2=None, op0=ALU.mult))
                    V(lambda e: e.tensor_copy(out=ki[:], in_=kf[:]))
                    V(lambda e: e.tensor_copy(out=kf[:], in_=ki[:]))
                    V(lambda e: e.scalar_tensor_tensor(out=ang[:], in0=kf[:], scalar=-TWO_PI, in1=ang[:], op0=ALU.mult, op1=ALU.add))
                    V(lambda e: e.tensor_scalar(out=ang[:], in0=ang[:], scalar1=math.pi, scalar2=-math.pi, op0=ALU.min, op1=ALU.max))
                    A(lambda e: e.activation(out=dst[:], in_=ang[:], func=AF.Sin))
                for k in range(1, 9):
                    A(lambda e, k=k: e.activation(out=mp[:], in_=xr[:], func=AF.Exp, scale=float(k)))
                    A(lambda e, k=k: e.activation(out=mn[:], in_=xr[:], func=AF.Exp, scale=float(-k)))
                    sin_of(sn, k, 0.0); sin_of(cs, k, math.pi / 2)
                    V(lambda e, k=k: e.tensor_tensor(out=Ere[:, k, :], in0=mp[:], in1=cs[:], op=ALU.mult))
                    V(lambda e, k=k: e.tensor_tensor(out=Eim[:, k, :], in0=mp[:], in1=sn[:], op=ALU.mult))
                    V(lambda e, k=k: e.tensor_tensor(out=Nre[:, k, :], in0=mn[:], in1=cs[:], op=ALU.mult))
                    V(lambda e, k=k: e.scalar_tensor_tensor(out=Nim[:, k, :], in0=mn[:], scalar=-1.0, in1=sn[:], op0=ALU.mult, op1=ALU.mult))
                for s_ in range(2):
                    S.op("dve", lambda e, s_=s_: e.tensor_copy(out=r8re[:, :, s_], in_=Ere[:, 8, :]), R=(b,), W=(br8,))
                    S.op("dve", lambda e, s_=s_: e.tensor_copy(out=r8im[:, :, s_], in_=Eim[:, 8, :]), R=(b,), W=(br8,))
                    S.op("dve", lambda e, s_=s_: e.tensor_scalar(out=r8imn[:, :, s_], in0=Eim[:, 8, :], scalar1=-1.0, scalar2=None, op0=ALU.mult), R=(b,), W=(br8,))
                den = t("den", [64, G]); zre = t("zre", [64, G]); zim = t("zim", [64, G]); t1 = t("t1", [64, G]); x1 = t("x1", [64, G])
                V(lambda e: e.tensor_tensor(out=den[:], in0=are[:], in1=are[:], op=ALU.mult))
                V(lambda e: e.tensor_tensor(out=t1[:], in0=lim[:], in1=lim[:], op=ALU.mult))
                V(lambda e: e.tensor_tensor(out=den[:], in0=den[:], in1=t1[:], op=ALU.add))
                V(lambda e: e.reciprocal(out=den[:], in_=den[:]))
                V(lambda e: e.tensor_scalar(out=x1[:], in0=Ere[:, 1, :], scalar1=-1.0, scalar2=None, op0=ALU.add))
                V(lambda e: e.tensor_tensor(out=zre[:], in0=x1[:], in1=are[:], op=ALU.mult))
                V(lambda e: e.tensor_tensor(out=t1[:], in0=Eim[:, 1, :], in1=lim[:], op=ALU.mult))
                V(lambda e: e.tensor_tensor(out=zre[:], in0=zre[:], in1=t1[:], op=ALU.add))
                V(lambda e: e.tensor_tensor(out=zre[:], in0=zre[:], in1=den[:], op=ALU.mult))
                V(lambda e: e.tensor_tensor(out=zim[:], in0=Eim[:, 1, :], in1=are[:], op=ALU.mult))
                V(lambda e: e.tensor_tensor(out=t1[:], in0=x1[:], in1=lim[:], op=ALU.mult))
                V(lambda e: e.tensor_tensor(out=zim[:], in0=zim[:], in1=t1[:], op=ALU.subtract))
                V(lambda e: e.tensor_tensor(out=zim[:], in0=zim[:], in1=den[:], op=ALU.mult))
                Bbre = t("Bbre", [64, G, 16]); Bbim = t("Bbim", [64, G, 16]); tg16 = t("tg16", [64, G, 16])

                def bcG(ap2, n):
                    return ap2.unsqueeze(2).broadcast_to([64, n, 16])

                def cmul(ore, oim, are_, aim_, bre_, bim_, tmp):
                    V(lambda e: e.tensor_tensor(out=ore, in0=bre_, in1=are_, op=ALU.mult))
                    V(lambda e: e.tensor_tensor(out=tmp, in0=bim_, in1=aim_, op=ALU.mult))
                    V(lambda e: e.tensor_tensor(out=ore, in0=ore, in1=tmp, op=ALU.subtract))
                    V(lambda e: e.tensor_tensor(out=oim, in0=bim_, in1=are_, op=ALU.mult))
                    V(lambda e: e.tensor_tensor(out=tmp, in0=bre_, in1=aim_, op=ALU.mult))
                    V(lambda e: e.tensor_tensor(out=oim, in0=oim, in1=tmp, op=ALU.add))
                cmul(Bbre[:], Bbim[:], bcG(zre[:], G), bcG(zim[:], G), bre[:], bim[:], tg16[:])
                dbg_on = (self.stop_after == "s5p" and l == 0)

                def dump(ap2, parts, n, off):
                    tk = S.dma("qs", self.dram["dbg"][0:parts, off:off + n], ap2, R=(b, bo), W=(Buf("d"),))
                    b.r[tk[0]] = tk
                if dbg_on:
                    bo = Buf("s5p_out")
                    dump(Ere[:].rearrange("p k g -> p (k g)"), 64, 9 * G, 0)
                    dump(Eim[:].rearrange("p k g -> p (k g)"), 64, 9 * G, 9 * G)
                    dump(zre[:], 64, G, 18 * G); dump(zim[:], 64, G, 19 * G)
                    dump(Bbre[:, 0:8, :].rearrange("p g h -> p (g h)"), 64, 128, 20 * G)
                    dump(Nre[:].rearrange("p k g -> p (k g)"), 64, 9 * G, 20 * G + 128)
                    dump(Nim[:].rearrange("p k g -> p (k g)"), 64, 9 * G, 29 * G + 128)
                if not dbg_on:
                    bo = Buf("s5p_out")
                Mbf = t("Mbf", [128, 8, 128], BF16); Bbf = t("Bbf", [128, 8, 128], BF16); Cbf = t("Cbf", [64, 2, 8, 128], BF16)
                Xre = t("Xre", [64, 8, 8, 16]); Xim = t("Xim", [64, 8, 8, 16]); BTre = t("BTre", [64, 8, 8, 16]); BTim = t("BTim", [64, 8, 8, 16])
                CRre = t("CRre", [64, 8, 8, 16]); CRim = t("CRim", [64, 8, 8, 16]); tb = t("tb", [64, 8, 16])
                ident = self.cst["ident"]
                for gb in range(cfg.GB):
                    gs = slice(gb * 8, gb * 8 + 8)
                    for s_ in range(8):
                        cmul(Xre[:, :, s_, :], Xim[:, :, s_, :], bcG(Nre[:, s_ + 1, gs], 8), bcG(Nim[:, s_ + 1, gs], 8), Bbre[:, gs, :], Bbim[:, gs, :], tb[:])
                        cmul(BTre[:, :, s_, :], BTim[:, :, s_, :], bcG(Ere[:, 7 - s_, gs], 8), bcG(Eim[:, 7 - s_, gs], 8), Bbre[:, gs, :], Bbim[:, gs, :], tb[:])
                        cmul(CRre[:, :, s_, :], CRim[:, :, s_, :], bcG(Ere[:, s_ + 1, gs], 8), bcG(Eim[:, s_ + 1, gs], 8), cre[:, gs, :], cim[:, gs, :], tb[:])
                    S.op("dve", lambda e: e.tensor_scalar(out=CRim[:], in0=CRim[:], scalar1=-1.0, scalar2=None, op0=ALU.mult), R=(b,), W=(b,))
                    if dbg_on and gb == 0:
                        dump(Xre[:].rearrange("p g s h -> p (g s h)"), 64, 1024, 1024)
                        dump(CRre[:].rearrange("p g s h -> p (g s h)"), 64, 1024, 2048)
                        dump(CRim[:].rearrange("p g s h -> p (g s h)"), 64, 1024, 3072)
                        dump(BTre[:].rearrange("p g s h -> p (g s h)"), 64, 1024, 4096)
                    S.op("dve", lambda e: e.tensor_copy(out=Cbf[:, 0], in_=CRre[:].rearrange("p g s h -> p g (s h)")), R=(b,), W=(bo,))
                    S.op("dve", lambda e: e.tensor_copy(out=Cbf[:, 1], in_=CRim[:].rearrange("p g s h -> p g (s h)")), R=(b,), W=(bo,))
                    for gi in range(8):
                        pm = self.bank(gi % 2); pb = self.bank(2 + gi % 2)
                        fl = lambda ap: ap.rearrange("p s h -> p (s h)")
                        S.op("pe", lambda e, gi=gi, pm=pm: e.matmul(pm[0][:, 0:128], fl(Xre[:, gi]), fl(CRre[:, gi]), start=True, stop=False), R=(b,), W=(pm[1],), sig=False)
                        S.op("pe", lambda e, gi=gi, pm=pm: e.matmul(pm[0][:, 0:128], fl(Xim[:, gi]), fl(CRim[:, gi]), start=False, stop=True), R=(b,), W=(pm[1],))
                        S.op("dve", lambda e, gi=gi, pm=pm: e.tensor_tensor(out=Mbf[:, gi, :], in0=pm[0][:, 0:128], in1=self.cst["mmask"][:], op=ALU.mult), R=(pm[1], self.b_const), W=(bo,))
                        if dbg_on and gb == 0 and gi == 0:
                            mdb = t("mdb", [128, 256])
                            S.op("dve", lambda e, pm=pm: e.tensor_tensor(out=mdb[:, 0:128], in0=pm[0][:, 0:128], in1=self.cst["mmask"][:], op=ALU.mult), R=(pm[1], self.b_const), W=(b,))
                            dump(mdb[:, 0:128], 128, 128, 5120)
                        S.op("pe", lambda e, gi=gi, pb=pb: e.matmul(pb[0][:, 0:64], fl(BTre[:, gi]), ident[0:64, 0:64], start=True, stop=True), R=(b, self.b_const), W=(pb[1],), sig=False)
                        S.op("pe", lambda e, gi=gi, pb=pb: e.matmul(pb[0][:, 64:128], fl(BTim[:, gi]), ident[0:64, 0:64], start=True, stop=True), R=(b, self.b_const), W=(pb[1],))
                        S.op("act", lambda e, gi=gi, pb=pb: e.activation(out=Bbf[:, gi, :], in_=pb[0][:, 0:128], func=AF.Copy), R=(pb[1],), W=(bo,))
                    sc = self.scr[l]
                    xb = Buf("x")
                    tk1 = S.dma("qs", sc["M"][:, gs, :], Mbf[:], R=(bo,), W=(xb,))
                    tk2 = S.dma("qs", sc["B"][:, gs, :], Bbf[:], R=(bo,), W=(xb,))
                    tk3 = S.dma("qs", sc["C"][:, :, gs, :], Cbf[:], R=(bo,), W=(xb,))
                    for nm, tk in (("M", tk1), ("B", tk2), ("C", tk3)):
                        rb_ = self.s5_ready[(nm, l)]
                        rb_.r[tk[0]] = tk
                S.barrier("dve", (b, bo, br8) + tuple(self.s5_ready[(nm, l)] for nm in ("M", "B", "C")))

    def emit_main(self):
        cfg = self.cfg; nc = self.nc; S = self.S; sb = self.sb; cst = self.cst
        L = cfg.L; KD = cfg.KD; HH = cfg.HH; KS = cfg.KS; KF = cfg.KF; G = cfg.G
        GH = min(16, G); NHALF = G // GH
        ones_bf = cst["ones_bf"]; bcn = self.b_const
        hT = sb("hT", [128, KD, NT]); b_h = [Buf(f"h{k}") for k in range(KD)]
        xn = sb("xn", [128, KD, NT], BF16); b_xn = [Buf(f"xn{k}") for k in range(KD)]
        mix = sb("mix", [128, KD, NT], BF16); b_mix = [Buf(f"mix{k}") for k in range(KD)]
        NS = 3
        wsl = [sb(f"wsl{i}", [128, 8192], BF16) for i in range(NS)]; b_wsl = [Buf(f"wsl{i}") for i in range(NS)]
        self.ws_i = 0
        hst = sb("hst", [128, L, 2, HH, 128]); b_hst = [[[Buf(f"hst{l}_{s}_{h}") for h in range(HH)] for s in range(2)] for l in range(L)]
        s5c = sb("s5c", [64, L, 2, G, 2]); b_s5c = [Buf(f"s5c{l}") for l in range(L)]
        cvc = sb("cvc", [128, L, KF, 2, 2]); b_cvc = [Buf(f"cvc{l}") for l in range(L)]
        std = sb("std", [128, NT]); rstd = sb("rstd", [128, NT]); b_std = Buf("std")
        S.op("dve", lambda e: e.memset(hst[:].rearrange("p a b c d -> p (a b c d)"), 0.0), W=tuple(b for a in b_hst for c in a for b in c))
        S.op("dve", lambda e: e.memset(s5c[:].rearrange("p a b c d -> p (a b c d)"), 0.0), W=tuple(b_s5c))
        S.op("dve", lambda e: e.memset(cvc[:].rearrange("p a b c d -> p (a b c d)"), 0.0), W=tuple(b_cvc))
        reg_bufs = []

        def RB(name):
            b_ = Buf(name); reg_bufs.append(b_); return b_
        sizes = {
            "io": cfg.D,
            "hg": 3 * NT + 2 * NT + 2 * NT + 2 * NT + 3 * NT + NT + (2 * 3 * 128 + 4 * NT + 2 * 3 * 128 + 3 * 128 + 2 * 128 + NT) // 2 + 16,
            "s5": KS * NT + 2 * GH * NW + 2 * 2 * GH * 2 + KS * NT + 2 * NT + (2 * NT + GH * NW + 2 * GH * NW + GH * NW + KS * NT) // 2 + 8,
            "ff": 2 * 2 * (TT + 2) + 2 * 2 * TT + 2 * 2 * TT + (KF * NT) // 2 + 8,
        }
        RW = max(sizes.values())
        reg = sb("reg", [128, RW])

        class RA:
            def __init__(s_): s_.off = 0
            def _shape(s_, v, shape):
                if len(shape) == 2: return v
                names = "abcd"[:len(shape) - 1]
                pat = "p (" + " ".join(names) + ") -> p " + " ".join(names)
                return v.rearrange(pat, **{n_: d_ for n_, d_ in zip(names[1:], shape[2:])})
            def f32(s_, shape, parts=128):
                n = int(np.prod(shape[1:])); v = reg[0:parts, s_.off:s_.off + n]; s_.off += n
                assert s_.off <= RW
                return s_._shape(v, shape)
            def bf(s_, shape, parts=128):
                n = int(np.prod(shape[1:])); nw = (n + 1) // 2
                v = reg[0:parts, s_.off:s_.off + nw].bitcast(BF16)[:, 0:n]; s_.off += nw
                assert s_.off <= RW
                return s_._shape(v, shape)
        ra = RA()
        xin = ra.f32([128, cfg.D]); b_xin = RB("xin")
        ra = RA()
        hq = [ra.f32([128, NT])]; hk = [ra.f32([128, NT])]; hf = [ra.f32([128, NT])]
        hg2 = [ra.f32([128, NT]) for i in range(2)]; hv2 = [ra.bf([128, 3, 128]) for i in range(2)]
        eb2 = [ra.f32([128, NT]) for i in range(2)]; qtil2 = [ra.bf([128, NT]) for i in range(2)]
        ktil2 = [ra.bf([128, NT]) for i in range(2)]; khT2 = [ra.f32([128, NT]) for i in range(2)]
        b_hp = [RB("hp0"), RB("hp1")]; b_t1 = RB("ht1")
        bT = ra.f32([128, NT]); enb = ra.f32([128, NT]); erc = ra.f32([128, NT])
        khat = ra.bf([128, 2, 3, 128]); scm = ra.bf([128, 3, 128])
        sbf = ra.bf([128, 2, 128]); osq = ra.bf([128, NT]); otmp = ra.f32([128, NT])
        b_hr = RB("hrest"); b_sbf = [RB("sbf0"), RB("sbf1")]; b_scm = RB("scm"); b_khat = RB("khat")
        ra = RA()
        uT = ra.f32([128, KS, NT]); b_uT = [RB(f"uT{k}") for k in range(KS)]
        ubf = [ra.bf([128, NT]) for i in range(2)]; b_ubf = [RB("ubf0"), RB("ubf1")]
        Uall = ra.bf([128, GH, NW]); b_U = RB("Uall")
        Wall = ra.f32([64, 2, GH, NW], parts=64); b_W = RB("Wall")
        Sbf = ra.bf([64, 2, GH, NW], parts=64); b_S = RB("Sbf")
        Yw = ra.bf([128, GH, NW]); b_Yw = RB("Yw")
        sA = ra.f32([64, 2, GH, 2], parts=64); sB = ra.f32([64, 2, GH, 2], parts=64); b_sc = RB("scan")
        yact = ra.f32([128, KS, NT]); b_ya = [RB(f"ya{k}") for k in range(KS)]
        yabf = ra.bf([128, KS, NT]); b_yb = [RB(f"yb{k}") for k in range(KS)]
        ypre = ra.f32([128, NT]); sgl = ra.f32([128, NT]); b_yt = RB("ytmp")
        ra = RA()
        hid = ra.bf([128, KF, NT]); b_hid = [RB(f"hid{k}") for k in range(KF)]
        aext = [ra.f32([128, 2, TT + 2]) for i in range(2)]; cacc = [ra.f32([128, 2, TT]) for i in range(2)]
        csg = [ra.f32([128, 2, TT]) for i in range(2)]; b_ff = [RB("ff0"), RB("ff1")]

        def phase():
            S.barrier("dve", tuple(reg_bufs))

        def V(fn, R=(), W=()): return S.op("dve", fn, R=R, W=W)
        def A(fn, R=(), W=()): return S.op("act", fn, R=R, W=W)
        def Pl(fn, R=(), W=()): return S.op("pool", fn, R=R, W=W)
        def MM(fn, R=(), W=(), sig=False): return S.op("pe", fn, R=R, W=W, sig=sig)

        def load_slab(src_ap, nk, cw, ready, parts=128):
            i = self.ws_i % NS; self.ws_i += 1
            view = wsl[i][0:parts, 0:nk * cw].rearrange("p (k c) -> p k c", c=cw)
            S.dma("qs", view, src_ap, R=tuple(ready), W=(b_wsl[i],))
            return view, b_wsl[i]

        def rmsnorm_to_xn(gam):
            import os
            for k in range(KD):
                if k % 2 == 0 or os.environ.get("NOSQ"):
                    Pl(lambda e, k=k: e.tensor_tensor(out=xn[:, k, :], in0=hT[:, k, :], in1=hT[:, k, :], op=ALU.mult), R=(b_h[k],), W=(b_xn[k],))
                else:
                    A(lambda e, k=k: e.activation(out=xn[:, k, :], in_=hT[:, k, :], func=AF.Square), R=(b_h[k],), W=(b_xn[k],))
            ps, pb = self.bank(0)
            for k in range(KD):
                MM(lambda e, k=k: e.matmul(ps[:, 0:NT], ones_bf[:], xn[:, k, :], start=(k == 0), stop=(k == KD - 1)),
                   R=(b_xn[k], bcn), W=(pb,), sig=(k == KD - 1))
            A(lambda e: e.activation(out=std[:], in_=ps[:, 0:NT], func=AF.Sqrt, scale=1.0 / cfg.D, bias=cst["eps"][:]), R=(pb, bcn), W=(b_std,))
            V(lambda e: e.reciprocal(out=rstd[:], in_=std[:]), R=(b_std,), W=(b_std,))
            for k in range(KD):
                V(lambda e, k=k: e.scalar_tensor_tensor(out=xn[:, k, :], in0=hT[:, k, :], scalar=gam(k), in1=rstd[:], op0=ALU.mult, op1=ALU.mult),
                  R=(b_h[k], b_std, bcn), W=(b_xn[k],))

        lbv, omlv, nomlv, b_lb = self.lb
        rmask = sb("rmask", [128, 2])
        S.op("dve", lambda e: e.memset(rmask[:], 0.0), W=(bcn,))
        S.op("dve", lambda e: e.memset(rmask[0:64, 0:1], 1.0), R=(bcn,), W=(bcn,))
        S.op("dve", lambda e: e.memset(rmask[64:128, 1:2], 1.0), R=(bcn,), W=(bcn,))
        self.kvb = [self.pbuf[6], self.pbuf[6]]
        idf = cst["ident"]

        for ti in range(cfg.NTILE):
            phase()
            for tg in range(3):
                c0 = tg * 128
                segs = []
                c = c0
                while c < c0 + 128:
                    s_ = c // TT; j = c % TT; n = min(TT - j, c0 + 128 - c)
                    segs.append((c - c0, s_, ti * TT + j, n)); c += n
                for (po, s_, tok, n) in segs:
                    S.dma("qs", xin[po:po + n, :], self.dram["xp"][s_, tok:tok + n, :], W=(b_xin,))
                for k in range(KD):
                    ps, pb = self.bank(k % 4)
                    MM(lambda e, k=k, ps=ps: e.matmul(ps[:, 0:128], xin[:, k * 128:(k + 1) * 128], idf[:], start=True, stop=True),
                       R=(b_xin, bcn), W=(pb,), sig=True)
                    A(lambda e, k=k, ps=ps, c0=c0: e.activation(out=hT[:, k, c0:c0 + 128], in_=ps[:, 0:128], func=AF.Copy), R=(pb,), W=(b_h[k],))
            if self.stop_after == "stage0":
                self.write_out(hT, b_h, xin, b_xin, ti); continue
            stopped = False
            for l in range(L):
                sc = self.scr[l]
                rmsnorm_to_xn(lambda k: cst["g_mix"][:, l, k:k + 1])
                if self.stop_after == "norm1" and l == 0:
                    self.debug_dump_bf(xn, b_xn, ti); stopped = True; break
                phase()
                def hg_s1(hd):
                    pp = hd % 2
                    wv, wb = load_slab(sc["head"][hd], KD, 512, (self.slab_ready[("head", l, hd)],))
                    pq, bq = self.bank(0); pz, bz = self.bank(1); pg, bg = self.bank(2); pv, bv = self.bank(3)
                    for (pst, bst, c0) in ((pz, bz, 128), (pq, bq, 0), (pg, bg, 384)):
                        for k in range(KD):
                            MM(lambda e, k=k, pst=pst, c0=c0: e.matmul(pst[:, 0:NT], wv[:, k, c0:c0 + 128], xn[:, k, :], start=(k == 0), stop=(k == KD - 1)),
                               R=(wb, b_xn[k]), W=(bst,), sig=(k == KD - 1))
                    for tg in range(3):
                        for k in range(KD):
                            MM(lambda e, k=k, tg=tg: e.matmul(pv[:, tg * 128:(tg + 1) * 128], xn[:, k, tg * 128:(tg + 1) * 128], wv[:, k, 256:384], start=(k == 0), stop=(k == KD - 1)),
                               R=(wb, b_xn[k]), W=(bv,), sig=(k == KD - 1 and tg == 2))
                    bp = b_hp[pp]; bt1 = b_t1
                    A(lambda e: e.activation(out=hk[0][:], in_=pz[:, 0:NT], func=AF.Sigmoid), R=(bz,), W=(bt1,))
                    A(lambda e: e.activation(out=hq[0][:], in_=pq[:, 0:NT], func=AF.Silu), R=(bq,), W=(bt1,))
                    A(lambda e: e.activation(out=hg2[pp][:], in_=pg[:, 0:NT], func=AF.Silu), R=(bg,), W=(bp,))
                    Pl(lambda e: e.tensor_copy(out=hv2[pp][:], in_=pv[:, 0:NT].rearrange("p (a b) -> p a b", b=128)), R=(bv,), W=(bp,)) if False else \
                        V(lambda e: e.tensor_copy(out=hv2[pp][:], in_=pv[:, 0:NT].rearrange("p (a b) -> p a b", b=128)), R=(bv,), W=(bp,))
                    V(lambda e: e.tensor_scalar(out=hf[0][:], in0=hk[0][:], scalar1=omlv[:, l, hd:hd + 1], scalar2=lbv[:, l, hd:hd + 1], op0=ALU.mult, op1=ALU.add), R=(bt1, b_lb), W=(bt1,))
                    V(lambda e: e.tensor_scalar(out=hf[0][:], in0=hf[0][:], scalar1=F_FLOOR, scalar2=None, op0=ALU.max), R=(bt1,), W=(bt1,))
                    V(lambda e: e.tensor_scalar(out=hk[0][:], in0=hk[0][:], scalar1=nomlv[:, l, hd:hd + 1], scalar2=omlv[:, l, hd:hd + 1], op0=ALU.mult, op1=ALU.add), R=(bt1, b_lb), W=(bt1,))
                    A(lambda e: e.activation(out=hf[0][:], in_=hf[0][:], func=AF.Ln), R=(bt1,), W=(bt1,))
                    V(lambda e: e.tensor_tensor_scan(out=bT[:], data0=cst["cmask"][:], data1=hf[0][:], initial=0.0, op0=ALU.mult, op1=ALU.add), R=(bt1, bcn), W=(bt1,))
                    A(lambda e: e.activation(out=eb2[pp][:], in_=bT[:], func=AF.Exp), R=(bt1,), W=(bp,))
                    A(lambda e: e.activation(out=enb[:], in_=bT[:], func=AF.Exp, scale=-1.0), R=(bt1,), W=(bt1,))
                    for c in range(6):
                        A(lambda e, c=c: e.activation(out=erc[:, c * 64:(c + 1) * 64], in_=bT[:, c * 64:(c + 1) * 64], func=AF.Exp, scale=-1.0, bias=bT[:, c * 64 + 63:c * 64 + 64]), R=(bt1,), W=(bt1,))
                    V(lambda e: e.tensor_tensor(out=qtil2[pp][:], in0=hq[0][:], in1=eb2[pp][:], op=ALU.mult), R=(bt1, bp), W=(bp,))
                    V(lambda e: e.tensor_tensor(out=ktil2[pp][:], in0=hk[0][:], in1=enb[:], op=ALU.mult), R=(bt1,), W=(bp,))
                    V(lambda e: e.tensor_tensor(out=khT2[pp][:], in0=hk[0][:], in1=erc[:], op=ALU.mult), R=(bt1,), W=(bp,))

                def hg_s2(hd):
                    pp = hd % 2; bp = b_hp[pp]
                    qtil_ = qtil2[pp]; ktil_ = ktil2[pp]; khT_ = khT2[pp]; hv_ = hv2[pp]; eb_ = eb2[pp]; hgp = hg2[pp]
                    pk, bk = self.bank(4); psc, bsc = self.bank(5); pkv = self.psum[6]; bkv = self.kvb; po, bo_ = self.bank(7)
                    for tg in range(3):
                        MM(lambda e, tg=tg: e.matmul(pk[:, tg * 128:(tg + 1) * 128], khT_[:, tg * 128:(tg + 1) * 128], idf[:], start=True, stop=True), R=(bp, bcn), W=(bk,), sig=(tg == 2))
                    A(lambda e: e.activation(out=khat[:, 0], in_=pk[:, 0:NT].rearrange("p (a b) -> p a b", b=128), func=AF.Copy, scale=rmask[:, 0:1]), R=(bk, bcn), W=(b_khat,))
                    V(lambda e: e.tensor_scalar(out=khat[:, 1], in0=pk[:, 0:NT].rearrange("p (a b) -> p a b", b=128), scalar1=rmask[:, 1:2], scalar2=None, op0=ALU.mult), R=(bk, bcn), W=(b_khat,))
                    for tg in range(3):
                        MM(lambda e, tg=tg: e.matmul(psc[:, tg * 128:(tg + 1) * 128], ktil_[:, tg * 128:(tg + 1) * 128], qtil_[:, tg * 128:(tg + 1) * 128], start=True, stop=True), R=(bp,), W=(bsc,), sig=(tg == 2))
                    for tg in range(3):
                        V(lambda e, tg=tg: e.tensor_tensor(out=scm[:, tg, :], in0=psc[:, tg * 128:(tg + 1) * 128], in1=cst["tri2"][:], op=ALU.mult), R=(bsc, bcn), W=(b_scm,))
                    order = (0, 1, 2, 3, 4, 5)
                    for ci, c in enumerate(order):
                        s_ = c // 3; tg = c // 2; j = c % 2; r0 = 64 * j
                        bs_ = b_hst[l][s_][hd]
                        if c % 3 == 0:
                            A(lambda e, s_=s_: e.activation(out=sbf[:, s_, :], in_=hst[:, l, s_, hd, :], func=AF.Copy), R=(bs_,), W=(b_sbf[s_],))
                        MM(lambda e, c=c, s_=s_: e.matmul(po[:, c * 64:(c + 1) * 64], sbf[:, s_, :], qtil_[:, c * 64:(c + 1) * 64], start=True, stop=False), R=(b_sbf[s_], bp), W=(bo_,))
                        MM(lambda e, c=c, tg=tg, r0=r0: e.matmul(po[:, c * 64:(c + 1) * 64], hv_[:, tg, :], scm[:, tg, r0:r0 + 64], start=False, stop=True), R=(bp, b_scm), W=(bo_,), sig=(ci == 5))
                        kvs = (ci % 2) * 128
                        MM(lambda e, tg=tg, j=j, kvs=kvs: e.matmul(pkv[:, kvs:kvs + 128], khat[:, j, tg, :], hv_[:, tg, :], start=True, stop=True), R=(b_khat, bp), W=(bkv[ci % 2],), sig=True)
                        V(lambda e, c=c, s_=s_, kvs=kvs: e.scalar_tensor_tensor(out=hst[:, l, s_, hd, :], in0=hst[:, l, s_, hd, :], scalar=eb_[:, c * 64 + 63:c * 64 + 64], in1=pkv[:, kvs:kvs + 128], op0=ALU.mult, op1=ALU.add), R=(bkv[ci % 2], bp, bs_), W=(bs_,))
                        if c % 3 != 2:
                            A(lambda e, s_=s_: e.activation(out=sbf[:, s_, :], in_=hst[:, l, s_, hd, :], func=AF.Copy), R=(bs_,), W=(b_sbf[s_],))
                    A(lambda e: e.activation(out=osq[:], in_=po[:, 0:NT], func=AF.Square), R=(bo_,), W=(b_hr,))
                    MM(lambda e: e.matmul(pk[:, 0:NT], ones_bf[:], osq[:], start=True, stop=True), R=(b_hr, bcn), W=(bk,), sig=True)
                    A(lambda e: e.activation(out=otmp[:], in_=pk[:, 0:NT], func=AF.Sqrt, scale=1.0 / 128.0, bias=cst["eps"][:]), R=(bk, bcn), W=(b_hr,))
                    V(lambda e: e.reciprocal(out=otmp[:], in_=otmp[:]), R=(b_hr,), W=(b_hr,))
                    V(lambda e: e.tensor_tensor(out=otmp[:], in0=po[:, 0:NT], in1=otmp[:], op=ALU.mult), R=(bo_, b_hr), W=(b_hr,))
                    V(lambda e: e.scalar_tensor_tensor(out=mix[:, hd, :], in0=otmp[:], scalar=cst["hg_gain"][:, l, hd:hd + 1], in1=hgp[:], op0=ALU.mult, op1=ALU.mult), R=(b_hr, bp, bcn), W=(b_mix[hd],))

                import os
                if os.environ.get("NOPIPE"):
                    for hd in range(HH):
                        hg_s1(hd); hg_s2(hd)
                else:
                    hg_s1(0)
                    for hd in range(HH):
                        if hd + 1 < HH:
                            hg_s1(hd + 1)
                        hg_s2(hd)
                if self.stop_after == "hgrn" and l == 0:
                    self.debug_dump_bf(mix, b_mix, ti); stopped = True; break
                phase()
                for j in range(cfg.NU):
                    wv, wb = load_slab(sc["u"][j], KD, cfg.UW, (self.slab_ready[("u", l, j)],))
                    for bi in range(cfg.UW // 128):
                        blk = j * (cfg.UW // 128) + bi
                        ps, pb = self.bank(blk % 4)
                        for k in range(KD):
                            MM(lambda e, k=k, bi=bi, ps=ps: e.matmul(ps[:, 0:NT], wv[:, k, bi * 128:(bi + 1) * 128], xn[:, k, :], start=(k == 0), stop=(k == KD - 1)),
                               R=(wb, b_xn[k]), W=(pb,), sig=(k == KD - 1))
                        A(lambda e, blk=blk, ps=ps: e.activation(out=uT[:, blk, :], in_=ps[:, 0:NT], func=AF.Copy), R=(pb,), W=(b_uT[blk],))
                if self.stop_after == "s5a" and l == 0:
                    self.debug_dump_bf(mix, b_mix, ti); stopped = True; break
                r8re, r8im, r8imn, br8 = self.r8[l]
                selbf = cst["sel_bf"]
                lvl = {"s5b": 1, "s5c": 2, "s5d": 3, "s5e": 4, "s5f": 5}.get(self.stop_after, 99)
                for hf_ in range(NHALF):
                    g0 = hf_ * GH
                    Bv, Bb_ = load_slab(sc["B"][:, g0:g0 + GH, :], GH, 128, ())
                    for gb in range(GH // 8):
                        blk = (g0 // 8) + gb
                        ub = ubf[blk % 2]; bub = b_ubf[blk % 2]
                        V(lambda e, blk=blk, ub=ub: e.tensor_copy(out=ub[:], in_=uT[:, blk, :]), R=(b_uT[blk],), W=(bub,))
                        ps, pb = self.bank(blk % 2)
                        for jg in range(8):
                            for s_ in range(8):
                                MM(lambda e, jg=jg, s_=s_, ps=ps, ub=ub: e.matmul(ps[:, jg * NW:(jg + 1) * NW], selbf[:, jg, s_, :], ub[:, s_:NT:8], start=(s_ == 0), stop=(s_ == 7)),
                                   R=(bub, bcn), W=(pb,), sig=(s_ == 7 and jg == 7))
                        A(lambda e, gb=gb, ps=ps: e.activation(out=Uall[:, gb * 8:(gb + 1) * 8, :], in_=ps[:, 0:8 * NW].rearrange("p (g c) -> p g c", c=NW), func=AF.Copy), R=(pb,), W=(b_U,))
                    if lvl < 2:
                        continue
                    for g4 in range(GH // 4):
                        ps, pb = self.bank(2 + g4 % 2)
                        for gi in range(4):
                            g = g4 * 4 + gi
                            for ri in range(2):
                                MM(lambda e, g=g, gi=gi, ri=ri, ps=ps: e.matmul(ps[0:64, (gi * 2 + ri) * NW:(gi * 2 + ri + 1) * NW], Bv[:, g, ri * 64:(ri + 1) * 64], Uall[:, g, :], start=True, stop=True),
                                   R=(Bb_, b_U), W=(pb,), sig=(gi == 3 and ri == 1))
                        V(lambda e, g4=g4, ps=ps: e.tensor_copy(out=Wall[:, :, g4 * 4:(g4 + 1) * 4, :].rearrange("p r g c -> p g r c"), in_=ps[0:64, 0:8 * NW].rearrange("p (g r c) -> p g r c", r=2, c=NW)), R=(pb,), W=(b_W,))
                    if lvl < 3:
                        continue
                    cs_ = s5c[:, l, :, g0:g0 + GH, :]
                    Wv = Wall[:].rearrange("p r g (s m) -> p r g s m", s=2)
                    Sv = Sbf[:].rearrange("p r g (s m) -> p r g s m", s=2)
                    V(lambda e: e.tensor_copy(out=Sv[:, :, :, :, 0], in_=cs_), R=(b_s5c[l],), W=(b_S,))
                    for m in range(NWS):
                        prev = cs_ if m == 0 else Wv[:, :, :, :, m - 1]
                        cur = Wv[:, :, :, :, m]
                        rb = (b_W, br8, b_s5c[l])
                        V(lambda e, prev=prev: e.tensor_tensor(out=sA[:], in0=prev, in1=r8re[:, g0:g0 + GH, :].unsqueeze(1).broadcast_to([64, 2, GH, 2]), op=ALU.mult), R=rb, W=(b_sc,))
                        V(lambda e, prev=prev: e.tensor_tensor(out=sB[:, 0], in0=prev[:, 1], in1=r8imn[:, g0:g0 + GH, :], op=ALU.mult), R=rb, W=(b_sc,))
                        V(lambda e, prev=prev: e.tensor_tensor(out=sB[:, 1], in0=prev[:, 0], in1=r8im[:, g0:g0 + GH, :], op=ALU.mult), R=rb, W=(b_sc,))
                        V(lambda e: e.tensor_tensor(out=sA[:], in0=sA[:], in1=sB[:], op=ALU.add), R=(b_sc,), W=(b_sc,))
                        V(lambda e, cur=cur: e.tensor_tensor(out=cur, in0=cur, in1=sA[:], op=ALU.add), R=(b_sc, b_W), W=(b_W,))
                    for s_ in range(2):
                        V(lambda e, s_=s_: e.tensor_copy(out=Sv[:, :, :, s_, 1:NWS], in_=Wv[:, :, :, s_, 0:NWS - 1]), R=(b_W,), W=(b_S,))
                    V(lambda e: e.tensor_copy(out=cs_, in_=Wv[:, :, :, :, NWS - 1]), R=(b_W, b_S), W=(b_s5c[l],))
                    if lvl < 4:
                        continue
                    Mv, Mb = load_slab(sc["M"][:, g0:g0 + GH, :], GH, 128, ())
                    i_ = self.ws_i % NS; self.ws_i += 1
                    Cv = wsl[i_][0:64, 0:2 * GH * 128].rearrange("p (r g c) -> p r g c", r=2, c=128)
                    S.dma("qs", Cv, sc["C"][:, :, g0:g0 + GH, :], W=(b_wsl[i_],))
                    Cre_v = Cv[:, 0]; Cim_v = Cv[:, 1]; Cre_b = b_wsl[i_]; Cim_b = b_wsl[i_]
                    for gb in range(GH // 8):
                        ps, pb = self.bank(4 + gb % 2)
                        for gi in range(8):
                            g = gb * 8 + gi
                            osl = ps[:, gi * NW:(gi + 1) * NW]
                            MM(lambda e, g=g, osl=osl: e.matmul(osl, Mv[:, g, :], Uall[:, g, :], start=True, stop=False), R=(Mb, b_U), W=(pb,))
                            MM(lambda e, g=g, osl=osl: e.matmul(osl, Cre_v[0:64, g, :], Sbf[:, 0, g, :], start=False, stop=False), R=(Cre_b, b_S), W=(pb,))
                            MM(lambda e, g=g, osl=osl: e.matmul(osl, Cim_v[0:64, g, :], Sbf[:, 1, g, :], start=False, stop=True), R=(Cim_b, b_S), W=(pb,), sig=(gi == 7))
                        A(lambda e, gb=gb, ps=ps: e.activation(out=Yw[:, gb * 8:(gb + 1) * 8, :], in_=ps[:, 0:8 * NW].rearrange("p (g c) -> p g c", c=NW), func=AF.Copy), R=(pb,), W=(b_Yw,))
                    if lvl < 5:
                        continue
                    for gb in range(GH // 8):
                        blk = (g0 // 8) + gb
                        ps, pb = self.bank(6 + gb % 2)
                        for t_ in range(8):
                            for jg in range(8):
                                MM(lambda e, t_=t_, jg=jg, gb=gb, ps=ps: e.matmul(ps[:, t_:NT:8], selbf[:, t_, jg, :], Yw[:, gb * 8 + jg, :], start=(jg == 0), stop=(jg == 7)),
                                   R=(b_Yw, bcn), W=(pb,), sig=(jg == 7 and t_ == 7))
                        V(lambda e, blk=blk, ps=ps: e.scalar_tensor_tensor(out=ypre[:], in0=uT[:, blk, :], scalar=cst["s5_d"][:, l, blk:blk + 1], in1=ps[:, 0:NT], op0=ALU.mult, op1=ALU.add), R=(pb, b_uT[blk], bcn), W=(b_yt,))
                        V(lambda e: e.tensor_tensor(out=sgl[:], in0=ypre[:], in1=ypre[:], op=ALU.mult), R=(b_yt,), W=(b_yt,))
                        V(lambda e: e.tensor_scalar(out=sgl[:], in0=sgl[:], scalar1=0.044715, scalar2=1.0, op0=ALU.mult, op1=ALU.add), R=(b_yt,), W=(b_yt,))
                        V(lambda e: e.tensor_tensor(out=sgl[:], in0=sgl[:], in1=ypre[:], op=ALU.mult), R=(b_yt,), W=(b_yt,))
                        A(lambda e: e.activation(out=sgl[:], in_=sgl[:], func=AF.Sigmoid, scale=1.5957691216057308), R=(b_yt,), W=(b_yt,))
                        V(lambda e, blk=blk: e.tensor_tensor(out=yact[:, blk, :], in0=ypre[:], in1=sgl[:], op=ALU.mult), R=(b_yt,), W=(b_ya[blk],))
                        V(lambda e, blk=blk: e.tensor_copy(out=yabf[:, blk, :], in_=yact[:, blk, :]), R=(b_ya[blk],), W=(b_yb[blk],))
                if lvl < 99:
                    if self.stop_after == "s5f" and ti == 0:
                        dt_ = self.sb("dbgtmp", [128, 4096])
                        bd = Buf("dbgtmp")
                        def dd(ap, parts, n, off, rb):
                            S.op("dve", lambda e: e.tensor_copy(out=dt_[0:parts, 0:n], in_=ap), R=rb + (bd,), W=(bd,))
                            S.dma("qs", self.dram["dbg2"][0:parts, off:off + n], dt_[0:parts, 0:n], R=(bd,), W=(Buf("z"),))
                            tk_ = bd.r[("qs", (S.qs["qs"].i - 1) % 8)] if False else None
                        dd(Uall[:].rearrange("p g c -> p (g c)"), 128, GH * NW, 0, (b_U,))
                        dd(Wall[:].rearrange("p r g c -> p (r g c)"), 64, 2 * GH * NW, 2048, (b_W,))
                        dd(Sbf[:].rearrange("p r g c -> p (r g c)"), 64, 2 * GH * NW, 6144, (b_S,))
                        dd(Yw[:].rearrange("p g c -> p (g c)"), 128, GH * NW, 10240, (b_Yw,))
                        dd(yact[:].rearrange("p k c -> p (k c)"), 128, KS * NT, 12288, tuple(b_ya))
                        dd(uT[:].rearrange("p k c -> p (k c)"), 128, KS * NT, 16384, tuple(b_uT))
                    self.debug_dump_bf(mix, b_mix, ti); stopped = True; break
                for j in range(cfg.NU):
                    wv, wb = load_slab(sc["glu"][j], KS, cfg.UW, (self.slab_ready[("glu", l, j)],))
                    for bi in range(cfg.UW // 128):
                        blk = j * (cfg.UW // 128) + bi
                        ps, pb = self.bank(blk % 4)
                        for k in range(KS):
                            MM(lambda e, k=k, bi=bi, ps=ps: e.matmul(ps[:, 0:NT], wv[:, k, bi * 128:(bi + 1) * 128], yabf[:, k, :], start=(k == 0), stop=(k == KS - 1)),
                               R=(wb, b_yb[k]), W=(pb,), sig=(k == KS - 1))
                        A(lambda e, blk=blk, ps=ps: e.activation(out=sgl[:], in_=ps[:, 0:NT], func=AF.Sigmoid, bias=cst["b_glu"][:, l, blk:blk + 1]), R=(pb, bcn), W=(b_yt,))
                        V(lambda e, blk=blk: e.tensor_tensor(out=yact[:, blk, :], in0=yact[:, blk, :], in1=sgl[:], op=ALU.mult), R=(b_yt, b_ya[blk]), W=(b_ya[blk],))
                for k in range(KS):
                    Pl(lambda e, k=k: e.tensor_tensor(out=yabf[:, k, :], in0=yact[:, k, :], in1=yact[:, k, :], op=ALU.mult), R=(b_ya[k], b_yb[k]), W=(b_yb[k],))
                ps, pb = self.bank(4)
                for k in range(KS):
                    MM(lambda e, k=k: e.matmul(ps[:, 0:NT], ones_bf[:], yabf[:, k, :], start=(k == 0), stop=(k == KS - 1)), R=(b_yb[k], bcn), W=(pb,), sig=(k == KS - 1))
                A(lambda e: e.activation(out=std[:], in_=ps[:, 0:NT], func=AF.Sqrt, scale=1.0 / cfg.DS5, bias=cst["eps"][:]), R=(pb, bcn), W=(b_std,))
                V(lambda e: e.reciprocal(out=rstd[:], in_=std[:]), R=(b_std,), W=(b_std,))
                for k in range(KS):
                    V(lambda e, k=k: e.scalar_tensor_tensor(out=mix[:, HH + k, :], in0=yact[:, k, :], scalar=cst["s5_gain"][:, l, k:k + 1], in1=rstd[:], op0=ALU.mult, op1=ALU.mult), R=(b_ya[k], b_std, bcn), W=(b_mix[HH + k],))
                if self.stop_after == "mix" and l == 0:
                    self.debug_dump_bf(mix, b_mix, ti); stopped = True; break
                nob = cfg.OW // 128
                for j in range(cfg.NO):
                    wv, wb = load_slab(sc["out"][j], KD, cfg.OW, (self.slab_ready[("out", l, j)],))
                    for bi in range(nob):
                        ob = j * nob + bi
                        ps, pb = self.bank(ob % 4)
                        for k in range(KD):
                            MM(lambda e, k=k, bi=bi, ps=ps: e.matmul(ps[:, 0:NT], wv[:, k, bi * 128:(bi + 1) * 128], mix[:, k, :], start=(k == 0), stop=(k == KD - 1)),
                               R=(wb, b_mix[k]), W=(pb,), sig=(k == KD - 1))
                        V(lambda e, ob=ob, ps=ps: e.tensor_tensor(out=hT[:, ob, :], in0=hT[:, ob, :], in1=ps[:, 0:NT], op=ALU.add), R=(pb, b_h[ob]), W=(b_h[ob],))
                if self.stop_after == "attn" and l == 0:
                    self.write_out(hT, b_h, xin, b_xin, ti); stopped = True; break
                phase()
                rmsnorm_to_xn(lambda k: cst["g_ffn"][:, l, k:k + 1])
                nfb = cfg.FW // 128
                for j in range(cfg.NF):
                    gv, gbuf = load_slab(sc["gate"][j], KD, cfg.FW, (self.slab_ready[("gate", l, j)],))
                    uv, ubuf = load_slab(sc["up"][j], KD, cfg.FW, (self.slab_ready[("up", l, j)],))
                    for bi in range(nfb):
                        fb = j * nfb + bi; pp = fb % 2
                        pa, ba = self.bank(2 * pp); pu, bu = self.bank(2 * pp + 1)
                        for k in range(KD):
                            MM(lambda e, k=k, bi=bi, pa=pa: e.matmul(pa[:, 0:NT], gv[:, k, bi * 128:(bi + 1) * 128], xn[:, k, :], start=(k == 0), stop=(k == KD - 1)), R=(gbuf, b_xn[k]), W=(ba,), sig=(k == KD - 1))
                        for k in range(KD):
                            MM(lambda e, k=k, bi=bi, pu=pu: e.matmul(pu[:, 0:NT], uv[:, k, bi * 128:(bi + 1) * 128], xn[:, k, :], start=(k == 0), stop=(k == KD - 1)), R=(ubuf, b_xn[k]), W=(bu,), sig=(k == KD - 1))
                        ae = aext[pp]; ca = cacc[pp]; cg = csg[pp]; bf_ = b_ff[pp]
                        cw = cst["conv_w"]
                        Pl(lambda e, fb=fb, ae=ae: e.tensor_copy(out=ae[:, :, 0:2], in_=cvc[:, l, fb, :, :]), R=(b_cvc[l],), W=(bf_,))
                        A(lambda e, ae=ae, pa=pa: e.activation(out=ae[:, :, 2:TT + 2], in_=pa[:, 0:NT].rearrange("p (s t) -> p s t", s=2), func=AF.Copy), R=(ba,), W=(bf_,))
                        Pl(lambda e, fb=fb, ae=ae: e.tensor_copy(out=cvc[:, l, fb, :, :], in_=ae[:, :, TT:TT + 2]), R=(bf_,), W=(b_cvc[l],))
                        V(lambda e, fb=fb, ae=ae, ca=ca: e.tensor_scalar(out=ca[:], in0=ae[:, :, 2:TT + 2], scalar1=cw[:, l, 2, fb:fb + 1], scalar2=cst["conv_b"][:, l, fb:fb + 1], op0=ALU.mult, op1=ALU.add), R=(bf_, bcn), W=(bf_,))
                        V(lambda e, fb=fb, ae=ae, ca=ca: e.scalar_tensor_tensor(out=ca[:], in0=ae[:, :, 1:TT + 1], scalar=cw[:, l, 1, fb:fb + 1], in1=ca[:], op0=ALU.mult, op1=ALU.add), R=(bf_, bcn), W=(bf_,))
                        V(lambda e, fb=fb, ae=ae, ca=ca: e.scalar_tensor_tensor(out=ca[:], in0=ae[:, :, 0:TT], scalar=cw[:, l, 0, fb:fb + 1], in1=ca[:], op0=ALU.mult, op1=ALU.add), R=(bf_, bcn), W=(bf_,))
                        A(lambda e, ca=ca, cg=cg: e.activation(out=cg[:], in_=ca[:], func=AF.Silu), R=(bf_,), W=(bf_,))
                        V(lambda e, fb=fb, cg=cg, pu=pu: e.tensor_tensor(out=hid[:, fb, :], in0=cg[:].rearrange("p s t -> p (s t)"), in1=pu[:, 0:NT], op=ALU.mult), R=(bf_, bu), W=(b_hid[fb],))
                for j in range(cfg.NO):
                    banks = [self.bank(4 + bi) for bi in range(nob)]
                    for pi, (k0, nk) in enumerate(cfg.KPIECES):
                        wv, wb = load_slab(sc["down"][j, pi, :, 0:nk, :], nk, cfg.OW, (self.slab_ready[("down", l, j, pi)],))
                        for bi in range(nob):
                            ps, pb = banks[bi]
                            for kk in range(nk):
                                kf_ = k0 + kk
                                MM(lambda e, kk=kk, kf_=kf_, bi=bi, ps=ps: e.matmul(ps[:, 0:NT], wv[:, kk, bi * 128:(bi + 1) * 128], hid[:, kf_, :], start=(kf_ == 0), stop=(kf_ == KF - 1)),
                                   R=(wb, b_hid[kf_]), W=(pb,), sig=(kk == nk - 1))
                    for bi in range(nob):
                        ob = j * nob + bi; ps, pb = banks[bi]
                        V(lambda e, ob=ob, ps=ps: e.tensor_tensor(out=hT[:, ob, :], in0=hT[:, ob, :], in1=ps[:, 0:NT], op=ALU.add), R=(pb, b_h[ob]), W=(b_h[ob],))
                if self.stop_after == "l0" and l == 0:
                    self.write_out(hT, b_h, xin, b_xin, ti); stopped = True; break
            if stopped:
                continue
            if True:
                phase()
                for k in range(KD):
                    Pl(lambda e, k=k: e.tensor_tensor(out=xn[:, k, :], in0=hT[:, k, :], in1=hT[:, k, :], op=ALU.mult), R=(b_h[k],), W=(b_xn[k],))
                ps, pb = self.bank(0)
                for k in range(KD):
                    MM(lambda e, k=k: e.matmul(ps[:, 0:NT], ones_bf[:], xn[:, k, :], start=(k == 0), stop=(k == KD - 1)), R=(b_xn[k], bcn), W=(pb,), sig=(k == KD - 1))
                A(lambda e: e.activation(out=std[:], in_=ps[:, 0:NT], func=AF.Sqrt, scale=1.0 / cfg.D, bias=cst["eps"][:]), R=(pb, bcn), W=(b_std,))
                V(lambda e: e.reciprocal(out=rstd[:], in_=std[:]), R=(b_std,), W=(b_std,))
                for k in range(KD):
                    V(lambda e, k=k: e.scalar_tensor_tensor(out=hT[:, k, :], in0=hT[:, k, :], scalar=cst["g_fin"][:, k:k + 1], in1=rstd[:], op0=ALU.mult, op1=ALU.mult), R=(b_h[k], b_std, bcn), W=(b_h[k],))
                self.write_out(hT, b_h, xin, b_xin, ti)

    def write_out(self, src, b_src, xin, b_xin, ti, nblk=None):
        cfg = self.cfg; S = self.S; idf = self.cst["ident"]; bcn = self.b_const
        nblk = cfg.KD if nblk is None else nblk
        for tg in range(3):
            c0 = tg * 128
            for k in range(nblk):
                ps, pb = self.bank(k % 4)
                S.op("pe", lambda e, k=k, ps=ps: e.matmul(ps[:, 0:128], src[:, k, c0:c0 + 128], idf[:], start=True, stop=True), R=(b_src[k], bcn), W=(pb,), sig=True)
                S.op("act", lambda e, k=k, ps=ps: e.activation(out=xin[:, k * 128:(k + 1) * 128], in_=ps[:, 0:128], func=AF.Copy), R=(pb,), W=(b_xin,))
            c = c0
            while c < c0 + 128:
                s_ = c // TT; j = c % TT; n = min(TT - j, c0 + 128 - c)
                tk = S.dma("qs", self.dram["out"][s_, ti * TT + j:ti * TT + j + n, 0:nblk * 128], xin[c - c0:c - c0 + n, 0:nblk * 128], R=(b_xin,), W=(Buf("o"),))
                self.out_tks.append(tk)
                c += n

    def debug_dump_bf(self, src, b_src, ti):
        cfg = self.cfg; S = self.S
        if not hasattr(self, "dbg_t"):
            self.dbg_t = self.sb("dbg_t", [128, cfg.KD, NT]); self.dbg_b = [Buf(f"dbg{k}") for k in range(cfg.KD)]
        for k in range(cfg.KD):
            S.op("dve", lambda e, k=k: e.tensor_copy(out=self.dbg_t[:, k, :], in_=src[:, k, :]), R=(b_src[k],), W=(self.dbg_b[k],))
        xin = self.xin_dbg if hasattr(self, "xin_dbg") else None
        if xin is None:
            self.xin_dbg = self.sb("xin_dbg", [128, cfg.D]); self.xin_dbg_b = Buf("xind")
        self.write_out(self.dbg_t, self.dbg_b, self.xin_dbg, self.xin_dbg_b, ti)


_CACHE = {}


def run_cfg(inputs, cfg, stop_after=None):
    maps = host_prep(inputs, cfg)
    key = (cfg.D, cfg.DFF, cfg.SEQ, cfg.L, stop_after)
    if key not in _CACHE:
        _CACHE[key] = Builder(cfg, stop_after).build()
    nc = _CACHE[key]
    ncores = cfg.NB // 2
    res = run_bass_kernel_spmd(nc, maps, core_ids=list(range(ncores)))
    if stop_after == "s5p":
        return np.asarray(res.results[0]["dbg"])
    if stop_after == "s5f":
        return np.asarray(res.results[0]["dbg2"])
    outs = [np.asarray(r["out"]) for r in res.results]
    full = np.concatenate([o[:, 64:, :] for o in outs], axis=0)
    return np.ascontiguousarray(full.astype(np.float32))


def kernel(**inputs):
    cfg = Cfg()
    return run_cfg(inputs, cfg)
```

```python
import math
from contextlib import ExitStack
import numpy as np
import concourse.bass as bass
import concourse.mybir as mybir
from concourse.bass_utils import run_bass_kernel_spmd

F32 = mybir.dt.float32
BF16 = mybir.dt.bfloat16
I32 = mybir.dt.int32
AF = mybir.ActivationFunctionType
ALU = mybir.AluOpType
P = 128
EPS = 1e-6
F_FLOOR = 1e-6
TT = 192
NT = 2 * TT
NW = NT // 8
NWS = TT // 8


class Cfg:
    def __init__(self, D=2048, DFF=5632, SEQ=2048, L=2, NB=16):
        self.D = D; self.DFF = DFF; self.SEQ = SEQ; self.L = L; self.NB = NB
        self.DHG = D // 2; self.HH = self.DHG // 128
        self.DS5 = D - self.DHG; self.G = self.DS5 // 16; self.GB = self.G // 8
        self.KS = self.DS5 // 128
        self.DIN = 4 * self.DHG + self.DS5
        self.KD = D // 128; self.KF = DFF // 128
        self.LP = SEQ + 64
        assert self.LP % TT == 0
        self.NTILE = self.LP // TT
        self.UW = min(512, self.DS5); self.NU = self.DS5 // self.UW
        self.OW = min(512, D); self.NO = D // self.OW
        self.FW = 512 if DFF % 512 == 0 else (384 if DFF % 384 == 0 else 128)
        self.NF = DFF // self.FW
        self.KPIECES = [(k0, min(16, self.KF - k0)) for k0 in range(0, self.KF, 16)]


class Buf:
    __slots__ = ("name", "w", "r")

    def __init__(self, name):
        self.name = name; self.w = None; self.r = {}


class Eng:
    def __init__(self, name, h, kind):
        self.name = name; self.h = h; self.kind = kind; self.n = 0; self.waited = {}


class DQ:
    def __init__(self, name, eng, sems):
        self.name = name; self.eng = eng; self.sems = sems; self.i = 0


CAP = 20000


class Sync:
    def __init__(self, nc, es):
        self.nc = nc; self.es = es
        self.engs = {}
        self.qs = {}
        self.semlists = {}
        self.trace = {}

    def add_eng(self, name, h, kind="c"):
        self.engs[name] = Eng(name, h, kind); self.semlists[name] = []

    def add_q(self, name, engname, K=8):
        sems = [self.es.enter_context(self.nc.semaphore(f"q_{name}_{i}")) for i in range(K)]
        self.qs[name] = DQ(name, self.engs[engname], sems)

    def _sem(self, name, idx):
        lst = self.semlists[name]
        while len(lst) <= idx:
            lst.append(self.es.enter_context(self.nc.semaphore(f"e_{name}_{len(lst)}")))
        return lst[idx]

    def _wait(self, E, k, v):
        if v <= 0 or E.waited.get(k, 0) >= v:
            return
        if k[0] in self.qs:
            E.h.wait_ge(self.qs[k[0]].sems[k[1]], 16 * v)
        else:
            E.h.wait_ge(self._sem(k[0], k[1]), v)
        E.waited[k] = v
        self.trace.setdefault(E.name, []).append(("w", k, v))

    def _deps(self, E, R, W):
        deps = {}
        for b in R:
            t = b.w
            if t is not None and deps.get(t[0], 0) < t[1]:
                deps[t[0]] = t[1]
        for b in W:
            t = b.w
            if t is not None and deps.get(t[0], 0) < t[1]:
                deps[t[0]] = t[1]
            for t in b.r.values():
                if deps.get(t[0], 0) < t[1]:
                    deps[t[0]] = t[1]
        for k, v in deps.items():
            if k[0] == E.name and E.kind == "pe":
                continue
            self._wait(E, k, v)

    def _mark(self, tk, R, W):
        for b in R:
            o = b.r.get(tk[0])
            if o is None or o[1] < tk[1]:
                b.r[tk[0]] = tk
        for b in W:
            b.w = tk; b.r = {}

    def _tk(self, en, n):
        return ((en, (n - 1) // CAP), (n - 1) % CAP + 1)

    def op(self, en, fn, R=(), W=(), sig=True):
        E = self.engs[en]
        self._deps(E, R, W)
        ins = fn(E.h)
        if sig:
            E.n += 1
            tk = self._tk(en, E.n)
            ins.then_inc(self._sem(en, tk[0][1]), 1)
            self.trace.setdefault(en, []).append(("s", tk[0], tk[1]))
        else:
            tk = self._tk(en, E.n + 1)
        self._mark(tk, R, W)
        return tk

    def dma(self, qn, out, in_, R=(), W=()):
        q = self.qs[qn]; E = q.eng
        self._deps(E, R, W)
        K = len(q.sems); i = q.i; q.i += 1
        lane = i % K; cnt = i // K + 1
        self._wait(E, (qn, lane), cnt - 1)
        ins = E.h.dma_start(out=out, in_=in_)
        ins.then_inc(q.sems[lane], 16)
        tk = ((qn, lane), cnt)
        self.trace.setdefault(E.name, []).append(("s", tk[0], tk[1]))
        self._mark(tk, R, W)
        return tk

    def check_deadlock(self):
        pc = {e: 0 for e in self.trace}
        sem = {}
        progress = True
        while progress:
            progress = False
            for e, tr in self.trace.items():
                while pc[e] < len(tr):
                    kind, k, v = tr[pc[e]]
                    if kind == "w":
                        if sem.get(k, 0) >= v:
                            pc[e] += 1; progress = True
                        else:
                            break
                    else:
                        assert sem.get(k, 0) == v - 1, (e, k, v, sem.get(k, 0))
                        sem[k] = v; pc[e] += 1; progress = True
        stuck = {e: (pc[e], len(tr), tr[pc[e]]) for e, tr in self.trace.items() if pc[e] < len(tr)}
        return stuck

    def final_wait(self, en, tks):
        E = self.engs[en]
        for tk in tks:
            self._wait(E, tk[0], tk[1])
        for qn, q in self.qs.items():
            K = len(q.sems)
            for lane in range(K):
                cnt = (q.i - lane + K - 1) // K
                self._wait(E, (qn, lane), cnt)

    def barrier(self, en, bufs):
        tk = self.op(en, lambda e: e.memset(self.bar_t[:], 0.0), R=tuple(bufs), W=tuple(bufs) + (self.bar_b,))
        for E in self.engs.values():
            self._wait(E, tk[0], tk[1])


def make_consts():
    c = {}
    c["ident"] = np.eye(128, dtype=np.float32)
    c["ones"] = np.ones((128, 128), np.float32)
    s = np.arange(128)
    same = (s[:, None] // 64) == (s[None, :] // 64)
    c["tri2"] = (same & (s[:, None] <= s[None, :])).astype(np.float32)
    sel = np.zeros((128, 8, 8, 128), np.float32)
    for a_ in range(8):
        for b_ in range(8):
            for hi in range(16):
                sel[16 * a_ + hi, a_, b_, 16 * b_ + hi] = 1.0
    c["sel"] = sel
    a = np.arange(128) // 16
    c["mmask"] = (a[None, :] >= a[:, None]).astype(np.float32)
    cm = np.ones((128, NT), np.float32); cm[:, ::64] = 0.0
    c["cmask"] = cm
    return c


def host_prep(inp, cfg):
    L = cfg.L; f = np.float32
    sh = {}
    perm = []
    for h in range(cfg.HH):
        for part in range(4):
            perm.extend(range(part * cfg.DHG + h * 128, part * cfg.DHG + (h + 1) * 128))
    perm.extend(range(4 * cfg.DHG, cfg.DIN))
    sh["w_in"] = np.ascontiguousarray(np.asarray(inp["w_in"], f)[:, :, perm])
    for k in ("w_glu", "w_out", "w_ffn_gate", "w_ffn_up", "w_ffn_down"):
        sh[k] = np.ascontiguousarray(np.asarray(inp[k], f))

    def fm(v, nblk):
        return np.ascontiguousarray(np.asarray(v, f).reshape(L, nblk, 128).transpose(2, 0, 1))
    sh["g_mix"] = fm(inp["norm_mix"], cfg.KD)
    sh["g_ffn"] = fm(inp["norm_ffn"], cfg.KD)
    sh["g_fin"] = np.ascontiguousarray(np.asarray(inp["final_norm"], f).reshape(cfg.KD, 128).T)
    sh["hg_gain"] = fm(inp["hg_norm"], cfg.HH)
    sh["s5_gain"] = fm(inp["s5_norm"], cfg.KS)
    sh["s5_d"] = fm(inp["s5_d"], cfg.KS)
    sh["b_glu"] = fm(inp["b_glu"], cfg.KS)
    sh["lbl_fm"] = fm(inp["lb_logits"], cfg.HH)
    cw = np.asarray(inp["ffn_conv_w"], f)
    sh["conv_w"] = np.ascontiguousarray(cw.reshape(L, 3, cfg.KF, 128).transpose(3, 0, 1, 2))
    sh["conv_b"] = fm(inp["ffn_conv_b"], cfg.KF)
    sh["lam_re"] = np.ascontiguousarray(np.asarray(inp["s5_lambda_re"], f).transpose(2, 0, 1))
    sh["lam_im"] = np.ascontiguousarray(np.asarray(inp["s5_lambda_im"], f).transpose(2, 0, 1))
    sh["lstep"] = np.ascontiguousarray(np.broadcast_to(np.asarray(inp["s5_log_step"], f)[None], (64, L, cfg.G)))
    sh["b_re"] = np.ascontiguousarray(np.asarray(inp["s5_b_re"], f).transpose(2, 0, 1, 3))
    sh["b_im"] = np.ascontiguousarray(np.asarray(inp["s5_b_im"], f).transpose(2, 0, 1, 3))
    sh["c_re"] = np.ascontiguousarray(np.asarray(inp["s5_c_re"], f).transpose(3, 0, 1, 2))
    sh["c_im"] = np.ascontiguousarray(np.asarray(inp["s5_c_im"], f).transpose(3, 0, 1, 2))
    sh.update(make_consts())
    x = np.asarray(inp["x"], f); meta = np.asarray(inp["meta_tokens"], f)
    ncores = cfg.NB // 2
    maps = []
    for c in range(ncores):
        xp = np.zeros((2, cfg.LP, cfg.D), f)
        xp[:, 48:64] = meta[None]
        xp[:, 64:] = x[2 * c:2 * c + 2]
        m = dict(sh); m["xp"] = xp
        maps.append(m)
    return maps


def input_shapes(cfg):
    L = cfg.L
    return {
        "xp": [2, cfg.LP, cfg.D], "w_in": [L, cfg.D, cfg.DIN], "w_glu": [L, cfg.DS5, cfg.DS5],
        "w_out": [L, cfg.D, cfg.D], "w_ffn_gate": [L, cfg.D, cfg.DFF], "w_ffn_up": [L, cfg.D, cfg.DFF],
        "w_ffn_down": [L, cfg.DFF, cfg.D],
        "g_mix": [128, L, cfg.KD], "g_ffn": [128, L, cfg.KD], "g_fin": [128, cfg.KD],
        "hg_gain": [128, L, cfg.HH], "s5_gain": [128, L, cfg.KS], "s5_d": [128, L, cfg.KS],
        "b_glu": [128, L, cfg.KS], "lbl_fm": [128, L, cfg.HH],
        "conv_w": [128, L, 3, cfg.KF], "conv_b": [128, L, cfg.KF],
        "lam_re": [64, L, cfg.G], "lam_im": [64, L, cfg.G], "lstep": [64, L, cfg.G],
        "b_re": [64, L, cfg.G, 16], "b_im": [64, L, cfg.G, 16], "c_re": [64, L, cfg.G, 16], "c_im": [64, L, cfg.G, 16],
        "ident": [128, 128], "ones": [128, 128], "tri2": [128, 128],
        "sel": [128, 8, 8, 128], "mmask": [128, 128], "cmask": [128, NT],
    }


class Builder:
    def __init__(self, cfg, stop_after=None):
        self.cfg = cfg
        self.stop_after = stop_after

    def sb(self, name, shape, dt=F32):
        return self.es.enter_context(self.nc.sbuf_tensor(name, list(shape), dt))

    def build(self):
        cfg = self.cfg
        nc = bass.Bass("TRN2", target_bir_lowering=False)
        self.nc = nc
        with ExitStack() as es:
            self.es = es
            S = Sync(nc, es); self.S = S
            S.add_eng("pe", nc.tensor, "pe"); S.add_eng("act", nc.scalar); S.add_eng("dve", nc.vector)
            S.add_eng("pool", nc.gpsimd); S.add_eng("sp", nc.sync)
            S.add_q("qs", "sp", 8); S.add_q("qg", "pool", 8)
            S.bar_t = self.sb("bar_t", [128, 1]); S.bar_b = Buf("bar")
            self.dram = {}
            for k, shp in input_shapes(cfg).items():
                self.dram[k] = nc.dram_tensor(k, shp, F32, kind="ExternalInput").ap()
            self.dram["out"] = nc.dram_tensor("out", [2, cfg.LP, cfg.D], F32, kind="ExternalOutput").ap()
            if self.stop_after == "s5p":
                self.dram["dbg"] = nc.dram_tensor("dbg", [128, 8192], F32, kind="ExternalOutput").ap()
            if self.stop_after == "s5f":
                self.dram["dbg2"] = nc.dram_tensor("dbg2", [128, 32768], F32, kind="ExternalOutput").ap()
            self.psum = [es.enter_context(nc.psum_tensor(f"ps{i}", [128, 512], F32)) for i in range(8)]
            self.pbuf = [Buf(f"ps{i}") for i in range(8)]
            self.emit_all()
            stuck = S.check_deadlock()
            if stuck:
                raise RuntimeError(f"sync deadlock: {stuck}")
        return nc

    def bank(self, i):
        return (self.psum[i], self.pbuf[i])

    def emit_all(self):
        cfg = self.cfg; nc = self.nc; S = self.S; sb = self.sb
        L = cfg.L; KD = cfg.KD
        self.out_tks = []
        cst = {}
        self.cst = cst
        self.b_const = Buf("const")
        bc_ = self.b_const
        for key in ("g_mix", "g_ffn", "g_fin", "hg_gain", "s5_gain", "s5_d", "b_glu", "lbl_fm", "conv_w", "conv_b",
                    "ident", "ones", "tri2", "mmask", "cmask"):
            t = sb("c_" + key, input_shapes(cfg)[key], F32)
            S.dma("qs", t[:], self.dram[key], W=(bc_,))
            cst[key] = t
        cst["ones_bf"] = sb("ones_bf", [128, 128], BF16)
        cst["tri_bf"] = sb("tri_bf", [128, 128], BF16)
        cst["sel_bf"] = sb("sel_bf", [128, 8, 8, 128], BF16)
        cst["eps"] = sb("eps_t", [128, 1], F32)
        S.op("dve", lambda e: e.tensor_copy(out=cst["ones_bf"][:], in_=cst["ones"][:]), R=(bc_,), W=(bc_,))
        S.op("dve", lambda e: e.tensor_copy(out=cst["tri_bf"][:], in_=cst["tri2"][:]), R=(bc_,), W=(bc_,))
        S.op("dve", lambda e: e.memset(cst["eps"][:], EPS), W=(bc_,))
        with nc.sbuf_tensor("sel_tmp", [128, 8, 8, 128], F32) as selt:
            bt = Buf("selt")
            S.dma("qs", selt[:], self.dram["sel"], W=(bt,))
            S.op("dve", lambda e: e.tensor_copy(out=cst["sel_bf"][:], in_=selt[:]), R=(bt,), W=(bc_,))
            S.barrier("dve", (bt, bc_))
        self.scr = {}
        for l in range(L):
            d = {}
            d["head"] = nc.dram_tensor(f"s_head{l}", [cfg.HH, 128, KD, 512], BF16, kind="Internal").ap()
            d["u"] = nc.dram_tensor(f"s_u{l}", [cfg.NU, 128, KD, cfg.UW], BF16, kind="Internal").ap()
            d["glu"] = nc.dram_tensor(f"s_glu{l}", [cfg.NU, 128, cfg.KS, cfg.UW], BF16, kind="Internal").ap()
            d["out"] = nc.dram_tensor(f"s_out{l}", [cfg.NO, 128, KD, cfg.OW], BF16, kind="Internal").ap()
            d["gate"] = nc.dram_tensor(f"s_gate{l}", [cfg.NF, 128, KD, cfg.FW], BF16, kind="Internal").ap()
            d["up"] = nc.dram_tensor(f"s_up{l}", [cfg.NF, 128, KD, cfg.FW], BF16, kind="Internal").ap()
            d["down"] = nc.dram_tensor(f"s_down{l}", [cfg.NO, len(cfg.KPIECES), 128, 16, cfg.OW], BF16, kind="Internal").ap()
            d["M"] = nc.dram_tensor(f"s_M{l}", [128, cfg.G, 128], BF16, kind="Internal").ap()
            d["B"] = nc.dram_tensor(f"s_B{l}", [128, cfg.G, 128], BF16, kind="Internal").ap()
            d["C"] = nc.dram_tensor(f"s_C{l}", [64, 2, cfg.G, 128], BF16, kind="Internal").ap()
            self.scr[l] = d
        self.slab_ready = {}
        def fin():
            for b in list(self.slab_ready.values()) + [self.b_const]:
                if b.w is not None:
                    self.out_tks.append(b.w)
            S.barrier("dve", (self.b_const,))
            S.final_wait("sp", self.out_tks)
        if self.stop_after == "consts":
            return fin()
        import os
        if not os.environ.get("SKIP_PRE"):
            self.emit_precast()
        if self.stop_after == "precast":
            return fin()
        self.emit_lb()
        if self.stop_after == "lb":
            return fin()
        if not os.environ.get("SKIP_PRE"):
            self.emit_s5_prologue()
        if self.stop_after == "s5p":
            return fin()
        self.emit_main()
        S.final_wait("sp", self.out_tks)

    def emit_precast(self):
        cfg = self.cfg; S = self.S; D = self.dram; nc = self.nc
        jobs = []

        def v(ap):
            return ap.rearrange("(k p) c -> p k c", p=128)
        for l in range(cfg.L):
            sc = self.scr[l]
            for h in range(cfg.HH):
                jobs.append((("head", l, h), sc["head"][h], v(D["w_in"][l, :, h * 512:(h + 1) * 512]), cfg.KD, 512))
            for j in range(cfg.NU):
                c0 = cfg.HH * 512 + j * cfg.UW
                jobs.append((("u", l, j), sc["u"][j], v(D["w_in"][l, :, c0:c0 + cfg.UW]), cfg.KD, cfg.UW))
            for j in range(cfg.NU):
                jobs.append((("glu", l, j), sc["glu"][j], v(D["w_glu"][l, :, j * cfg.UW:(j + 1) * cfg.UW]), cfg.KS, cfg.UW))
            for j in range(cfg.NO):
                jobs.append((("out", l, j), sc["out"][j], v(D["w_out"][l, :, j * cfg.OW:(j + 1) * cfg.OW]), cfg.KD, cfg.OW))
            for j in range(cfg.NF):
                jobs.append((("gate", l, j), sc["gate"][j], v(D["w_ffn_gate"][l, :, j * cfg.FW:(j + 1) * cfg.FW]), cfg.KD, cfg.FW))
                jobs.append((("up", l, j), sc["up"][j], v(D["w_ffn_up"][l, :, j * cfg.FW:(j + 1) * cfg.FW]), cfg.KD, cfg.FW))
            for j in range(cfg.NO):
                for pi, (k0, nk) in enumerate(cfg.KPIECES):
                    jobs.append((("down", l, j, pi), sc["down"][j, pi, :, 0:nk, :],
                                 v(D["w_ffn_down"][l, k0 * 128:(k0 + nk) * 128, j * cfg.OW:(j + 1) * cfg.OW]), nk, cfg.OW))
        NB_ = 3
        with ExitStack() as es2:
            s32 = [es2.enter_context(nc.sbuf_tensor(f"pc32_{i}", [128, 8192], F32)) for i in range(NB_)]
            s16 = [es2.enter_context(nc.sbuf_tensor(f"pc16_{i}", [128, 8192], BF16)) for i in range(NB_)]
            b32 = [Buf(f"pc32_{i}") for i in range(NB_)]; b16 = [Buf(f"pc16_{i}") for i in range(NB_)]
            engs = ["dve", "act", "pool"]
            for n, (key, dst, src, nk, cw) in enumerate(jobs):
                i = n % NB_
                v32 = s32[i][:, 0:nk * cw].rearrange("p (k c) -> p k c", c=cw)
                v16 = s16[i][:, 0:nk * cw].rearrange("p (k c) -> p k c", c=cw)
                S.dma("qs", v32, src, W=(b32[i],))
                en = engs[n % 3]
                if en == "act":
                    S.op("act", lambda e, i=i, nk=nk, cw=cw: e.activation(out=s16[i][:, 0:nk * cw], in_=s32[i][:, 0:nk * cw], func=AF.Copy), R=(b32[i],), W=(b16[i],))
                else:
                    S.op(en, lambda e, i=i, nk=nk, cw=cw: e.tensor_copy(out=s16[i][:, 0:nk * cw], in_=s32[i][:, 0:nk * cw]), R=(b32[i],), W=(b16[i],))
                rb = Buf(str(key)); self.slab_ready[key] = rb
                S.dma("qs", dst, v16, R=(b16[i],), W=(rb,))
            S.barrier("dve", tuple(b32) + tuple(b16) + tuple(self.slab_ready.values()))

    def emit_lb(self):
        cfg = self.cfg; S = self.S; sb = self.sb; L = cfg.L; W_ = cfg.HH
        lg = self.cst["lbl_fm"]; b = Buf("lb"); rb = (self.b_const, b)
        mx = sb("lb_mx", [128, W_]); ex = sb("lb_ex", [128, L, W_]); sm = sb("lb_sm", [128, W_])
        lb = sb("lb_fm", [128, L, W_]); oml = sb("oml_fm", [128, L, W_]); noml = sb("noml_fm", [128, L, W_])
        V = lambda fn: S.op("dve", fn, R=rb, W=(b,))
        V(lambda e: e.tensor_copy(out=mx[:], in_=lg[:, 0, :]))
        for l in range(1, L):
            V(lambda e, l=l: e.tensor_tensor(out=mx[:], in0=mx[:], in1=lg[:, l, :], op=ALU.max))
        for l in range(L):
            V(lambda e, l=l: e.tensor_tensor(out=ex[:, l, :], in0=lg[:, l, :], in1=mx[:], op=ALU.subtract))
        S.op("act", lambda e: e.activation(out=ex[:], in_=ex[:], func=AF.Exp), R=(b,), W=(b,))
        V(lambda e: e.tensor_copy(out=sm[:], in_=ex[:, 0, :]))
        for l in range(1, L):
            V(lambda e, l=l: e.tensor_tensor(out=sm[:], in0=sm[:], in1=ex[:, l, :], op=ALU.add))
        V(lambda e: e.reciprocal(out=sm[:], in_=sm[:]))
        V(lambda e: e.memset(lb[:, 0, :], 0.0))
        for l in range(1, L):
            V(lambda e, l=l: e.tensor_tensor(out=ex[:, l, :], in0=ex[:, l, :], in1=sm[:], op=ALU.mult))
            V(lambda e, l=l: e.tensor_tensor(out=lb[:, l, :], in0=lb[:, l - 1, :], in1=ex[:, l, :], op=ALU.add))
        V(lambda e: e.tensor_scalar(out=oml[:], in0=lb[:], scalar1=-1.0, scalar2=1.0, op0=ALU.mult, op1=ALU.add))
        V(lambda e: e.tensor_scalar(out=noml[:], in0=oml[:], scalar1=-1.0, scalar2=None, op0=ALU.mult))
        self.lb = (lb, oml, noml, b)

    def emit_s5_prologue(self):
        cfg = self.cfg; S = self.S; nc = self.nc; G = cfg.G
        TWO_PI = 2.0 * math.pi
        self.r8 = {}
        self.s5_ready = {}
        for l in range(cfg.L):
            r8re = self.sb(f"r8re{l}", [64, G, 2]); r8im = self.sb(f"r8im{l}", [64, G, 2]); r8imn = self.sb(f"r8imn{l}", [64, G, 2])
            br8 = Buf(f"r8_{l}")
            self.r8[l] = (r8re, r8im, r8imn, br8)
            for nm in ("M", "B", "C"):
                self.s5_ready[(nm, l)] = Buf(f"s5r{nm}{l}")
            with ExitStack() as es2:
                def t(name, shape, dt=F32):
                    return es2.enter_context(nc.sbuf_tensor(f"p{l}_{name}", list(shape), dt))
                b = Buf("s5p")
                lre = t("lre", [64, G]); lim = t("lim", [64, G]); lst = t("lst", [64, G])
                S.dma("qs", lre[:], self.dram["lam_re"][:, l, :], W=(b,))
                S.dma("qs", lim[:], self.dram["lam_im"][:, l, :], W=(b,))
                S.dma("qs", lst[:], self.dram["lstep"][:, l, :], W=(b,))
                bre = t("bre", [64, G, 16]); bim = t("bim", [64, G, 16]); cre = t("cre", [64, G, 16]); cim = t("cim", [64, G, 16])
                for tt_, key in ((bre, "b_re"), (bim, "b_im"), (cre, "c_re"), (cim, "c_im")):
                    S.dma("qs", tt_[:], self.dram[key][:, l], W=(b,))
                are = t("are", [64, G]); dt_ = t("dt", [64, G]); xr = t("xr", [64, G]); th = t("th", [64, G])
                V = lambda fn: S.op("dve", fn, R=(b,), W=(b,))
                A = lambda fn: S.op("act", fn, R=(b,), W=(b,))
                V(lambda e: e.tensor_scalar(out=are[:], in0=lre[:], scalar1=-1e-4, scalar2=None, op0=ALU.min))
                A(lambda e: e.activation(out=dt_[:], in_=lst[:], func=AF.Exp))
                V(lambda e: e.tensor_tensor(out=xr[:], in0=are[:], in1=dt_[:], op=ALU.mult))
                V(lambda e: e.tensor_tensor(out=th[:], in0=lim[:], in1=dt_[:], op=ALU.mult))
                Ere = t("Ere", [64, 9, G]); Eim = t("Eim", [64, 9, G]); Nre = t("Nre", [64, 9, G]); Nim = t("Nim", [64, 9, G])
                mp = t("mp", [64, G]); mn = t("mn", [64, G]); ang = t("ang", [64, G]); kf = t("kf", [64, G]); ki = t("ki", [64, G], I32)
                sn = t("sn", [64, G]); cs = t("cs", [64, G])
                V(lambda e: e.memset(Ere[:, 0, :], 1.0)); V(lambda e: e.memset(Eim[:, 0, :], 0.0))
                V(lambda e: e.memset(Nre[:, 0, :], 1.0)); V(lambda e: e.memset(Nim[:, 0, :], 0.0))

                def sin_of(dst, k, shift):
                    V(lambda e: e.tensor_scalar(out=ang[:], in0=th[:], scalar1=float(k), scalar2=float(shift), op0=ALU.mult, op1=ALU.add))
                    V(lambda e: e.tensor_scalar(out=kf[:], in0=ang[:], scalar1=1.0 / TWO_PI, scalar2=None, op0=ALU.mult))
                    V(lambda e: e.tensor_copy(out=ki[:], in_=kf[:]))
                    V(lambda e: e.tensor_copy(out=kf[:], in_=ki[:]))
                    V(lambda e: e.scalar_tensor_tensor(out=ang[:], in0=kf[:], scalar=-TWO_PI, in1=ang[:], op0=ALU.mult, op1=ALU.add))
                    V(lambda e: e.tensor_scalar(out=ang[:], in0=ang[:], scalar1=math.pi, scalar2=-math.pi, op0=ALU.min, op1=ALU.max))
                    A(lambda e: e.activation(out=dst[:], in_=ang[:], func=AF.Sin))
                for k in range(1, 9):
                    A(lambda e, k=k: e.activation(out=mp[:], in_=xr[:], func=AF.Exp, scale=float(k)))
                    A(lambda e, k=k: e.activation(out=mn[:], in_=xr[:], func=AF.Exp, scale=float(-k)))
                    sin_of(sn, k, 0.0); sin_of(cs, k, math.pi / 2)
                    V(lambda e, k=k: e.tensor_tensor(out=Ere[:, k, :], in0=mp[:], in1=cs[:], op=ALU.mult))
                    V(lambda e, k=k: e.tensor_tensor(out=Eim[:, k, :], in0=mp[:], in1=sn[:], op=ALU.mult))
                    V(lambda e, k=k: e.tensor_tensor(out=Nre[:, k, :], in0=mn[:], in1=cs[:], op=ALU.mult))
                    V(lambda e, k=k: e.scalar_tensor_tensor(out=Nim[:, k, :], in0=mn[:], scalar=-1.0, in1=sn[:], op0=ALU.mult, op1=ALU.mult))
                for s_ in range(2):
                    S.op("dve", lambda e, s_=s_: e.tensor_copy(out=r8re[:, :, s_], in_=Ere[:, 8, :]), R=(b,), W=(br8,))
                    S.op("dve", lambda e, s_=s_: e.tensor_copy(out=r8im[:, :, s_], in_=Eim[:, 8, :]), R=(b,), W=(br8,))
                    S.op("dve", lambda e, s_=s_: e.tensor_scalar(out=r8imn[:, :, s_], in0=Eim[:, 8, :], scalar1=-1.0, scalar2=None, op0=ALU.mult), R=(b,), W=(br8,))
                den = t("den", [64, G]); zre = t("zre", [64, G]); zim = t("zim", [64, G]); t1 = t("t1", [64, G]); x1 = t("x1", [64, G])
                V(lambda e: e.tensor_tensor(out=den[:], in0=are[:], in1=are[:], op=ALU.mult))
                V(lambda e: e.tensor_tensor(out=t1[:], in0=lim[:], in1=lim[:], op=ALU.mult))
                V(lambda e: e.tensor_tensor(out=den[:], in0=den[:], in1=t1[:], op=ALU.add))
                V(lambda e: e.reciprocal(out=den[:], in_=den[:]))
                V(lambda e: e.tensor_scalar(out=x1[:], in0=Ere[:, 1, :], scalar1=-1.0, scalar2=None, op0=ALU.add))
                V(lambda e: e.tensor_tensor(out=zre[:], in0=x1[:], in1=are[:], op=ALU.mult))
                V(lambda e: e.tensor_tensor(out=t1[:], in0=Eim[:, 1, :], in1=lim[:], op=ALU.mult))
                V(lambda e: e.tensor_tensor(out=zre[:], in0=zre[:], in1=t1[:], op=ALU.add))
                V(lambda e: e.tensor_tensor(out=zre[:], in0=zre[:], in1=den[:], op=ALU.mult))
                V(lambda e: e.tensor_tensor(out=zim[:], in0=Eim[:, 1, :], in1=are[:], op=ALU.mult))
                V(lambda e: e.tensor_tensor(out=t1[:], in0=x1[:], in1=lim[:], op=ALU.mult))
                V(lambda e: e.tensor_tensor(out=zim[:], in0=zim[:], in1=t1[:], op=ALU.subtract))
                V(lambda e: e.tensor_tensor(out=zim[:], in0=zim[:], in1=den[:], op=ALU.mult))
                Bbre = t("Bbre", [64, G, 16]); Bbim = t("Bbim", [64, G, 16]); tg16 = t("tg16", [64, G, 16])

                def bcG(ap2, n):
                    return ap2.unsqueeze(2).broadcast_to([64, n, 16])

                def cmul(ore, oim, are_, aim_, bre_, bim_, tmp):
                    V(lambda e: e.tensor_tensor(out=ore, in0=bre_, in1=are_, op=ALU.mult))
                    V(lambda e: e.tensor_tensor(out=tmp, in0=bim_, in1=aim_, op=ALU.mult))
                    V(lambda e: e.tensor_tensor(out=ore, in0=ore, in1=tmp, op=ALU.subtract))
                    V(lambda e: e.tensor_tensor(out=oim, in0=bim_, in1=are_, op=ALU.mult))
                    V(lambda e: e.tensor_tensor(out=tmp, in0=bre_, in1=aim_, op=ALU.mult))
                    V(lambda e: e.tensor_tensor(out=oim, in0=oim, in1=tmp, op=ALU.add))
                cmul(Bbre[:], Bbim[:], bcG(zre[:], G), bcG(zim[:], G), bre[:], bim[:], tg16[:])
                dbg_on = (self.stop_after == "s5p" and l == 0)

                def dump(ap2, parts, n, off):
                    tk = S.dma("qs", self.dram["dbg"][0:parts, off:off + n], ap2, R=(b, bo), W=(Buf("d"),))
                    b.r[tk[0]] = tk
                if dbg_on:
                    bo = Buf("s5p_out")
                    dump(Ere[:].rearrange("p k g -> p (k g)"), 64, 9 * G, 0)
                    dump(Eim[:].rearrange("p k g -> p (k g)"), 64, 9 * G, 9 * G)
                    dump(zre[:], 64, G, 18 * G); dump(zim[:], 64, G, 19 * G)
                    dump(Bbre[:, 0:8, :].rearrange("p g h -> p (g h)"), 64, 128, 20 * G)
                    dump(Nre[:].rearrange("p k g -> p (k g)"), 64, 9 * G, 20 * G + 128)
                    dump(Nim[:].rearrange("p k g -> p (k g)"), 64, 9 * G, 29 * G + 128)
                if not dbg_on:
                    bo = Buf("s5p_out")
                Mbf = t("Mbf", [128, 8, 128], BF16); Bbf = t("Bbf", [128, 8, 128], BF16); Cbf = t("Cbf", [64, 2, 8, 128], BF16)
                Xre = t("Xre", [64, 8, 8, 16]); Xim = t("Xim", [64, 8, 8, 16]); BTre = t("BTre", [64, 8, 8, 16]); BTim = t("BTim", [64, 8, 8, 16])
                CRre = t("CRre", [64, 8, 8, 16]); CRim = t("CRim", [64, 8, 8, 16]); tb = t("tb", [64, 8, 16])
                ident = self.cst["ident"]
                for gb in range(cfg.GB):
                    gs = slice(gb * 8, gb * 8 + 8)
                    for s_ in range(8):
                        cmul(Xre[:, :, s_, :], Xim[:, :, s_, :], bcG(Nre[:, s_ + 1, gs], 8), bcG(Nim[:, s_ + 1, gs], 8), Bbre[:, gs, :], Bbim[:, gs, :], tb[:])
                        cmul(BTre[:, :, s_, :], BTim[:, :, s_, :], bcG(Ere[:, 7 - s_, gs], 8), bcG(Eim[:, 7 - s_, gs], 8), Bbre[:, gs, :], Bbim[:, gs, :], tb[:])
                        cmul(CRre[:, :, s_, :], CRim[:, :, s_, :], bcG(Ere[:, s_ + 1, gs], 8), bcG(Eim[:, s_ + 1, gs], 8), cre[:, gs, :], cim[:, gs, :], tb[:])
                    S.op("dve", lambda e: e.tensor_scalar(out=CRim[:], in0=CRim[:], scalar1=-1.0, scalar2=None, op0=ALU.mult), R=(b,), W=(b,))
                    if dbg_on and gb == 0:
                        dump(Xre[:].rearrange("p g s h -> p (g s h)"), 64, 1024, 1024)
                        dump(CRre[:].rearrange("p g s h -> p (g s h)"), 64, 1024, 2048)
                        dump(CRim[:].rearrange("p g s h -> p (g s h)"), 64, 1024, 3072)
                        dump(BTre[:].rearrange("p g s h -> p (g s h)"), 64, 1024, 4096)
                    S.op("dve", lambda e: e.tensor_copy(out=Cbf[:, 0], in_=CRre[:].rearrange("p g s h -> p g (s h)")), R=(b,), W=(bo,))
                    S.op("dve", lambda e: e.tensor_copy(out=Cbf[:, 1], in_=CRim[:].rearrange("p g s h -> p g (s h)")), R=(b,), W=(bo,))
                    for gi in range(8):
                        pm = self.bank(gi % 2); pb = self.bank(2 + gi % 2)
                        fl = lambda ap: ap.rearrange("p s h -> p (s h)")
                        S.op("pe", lambda e, gi=gi, pm=pm: e.matmul(pm[0][:, 0:128], fl(Xre[:, gi]), fl(CRre[:, gi]), start=True, stop=False), R=(b,), W=(pm[1],), sig=False)
                        S.op("pe", lambda e, gi=gi, pm=pm: e.matmul(pm[0][:, 0:128], fl(Xim[:, gi]), fl(CRim[:, gi]), start=False, stop=True), R=(b,), W=(pm[1],))
                        S.op("dve", lambda e, gi=gi, pm=pm: e.tensor_tensor(out=Mbf[:, gi, :], in0=pm[0][:, 0:128], in1=self.cst["mmask"][:], op=ALU.mult), R=(pm[1], self.b_const), W=(bo,))
                        if dbg_on and gb == 0 and gi == 0:
                            mdb = t("mdb", [128, 256])
                            S.op("dve", lambda e, pm=pm: e.tensor_tensor(out=mdb[:, 0:128], in0=pm[0][:, 0:128], in1=self.cst["mmask"][:], op=ALU.mult), R=(pm[1], self.b_const), W=(b,))
                            dump(mdb[:, 0:128], 128, 128, 5120)
                        S.op("pe", lambda e, gi=gi, pb=pb: e.matmul(pb[0][:, 0:64], fl(BTre[:, gi]), ident[0:64, 0:64], start=True, stop=True), R=(b, self.b_const), W=(pb[1],), sig=False)
                        S.op("pe", lambda e, gi=gi, pb=pb: e.matmul(pb[0][:, 64:128], fl(BTim[:, gi]), ident[0:64, 0:64], start=True, stop=True), R=(b, self.b_const), W=(pb[1],))
                        S.op("act", lambda e, gi=gi, pb=pb: e.activation(out=Bbf[:, gi, :], in_=pb[0][:, 0:128], func=AF.Copy), R=(pb[1],), W=(bo,))
                    sc = self.scr[l]
                    xb = Buf("x")
                    tk1 = S.dma("qs", sc["M"][:, gs, :], Mbf[:], R=(bo,), W=(xb,))
                    tk2 = S.dma("qs", sc["B"][:, gs, :], Bbf[:], R=(bo,), W=(xb,))
                    tk3 = S.dma("qs", sc["C"][:, :, gs, :], Cbf[:], R=(bo,), W=(xb,))
                    for nm, tk in (("M", tk1), ("B", tk2), ("C", tk3)):
                        rb_ = self.s5_ready[(nm, l)]
                        rb_.r[tk[0]] = tk
                S.barrier("dve", (b, bo, br8) + tuple(self.s5_ready[(nm, l)] for nm in ("M", "B", "C")))

    def emit_main(self):
        cfg = self.cfg; nc = self.nc; S = self.S; sb = self.sb; cst = self.cst
        L = cfg.L; KD = cfg.KD; HH = cfg.HH; KS = cfg.KS; KF = cfg.KF; G = cfg.G
        GH = min(32, G); NHALF = G // GH
        ones_bf = cst["ones_bf"]; bcn = self.b_const
        hT = sb("hT", [128, KD, NT]); b_h = [Buf(f"h{k}") for k in range(KD)]
        xn = sb("xn", [128, KD, NT], BF16); b_xn = [Buf(f"xn{k}") for k in range(KD)]
        mix = sb("mix", [128, KD, NT], BF16); b_mix = [Buf(f"mix{k}") for k in range(KD)]
        NS = 3
        wsl = [sb(f"wsl{i}", [128, 8192], BF16) for i in range(NS)]; b_wsl = [Buf(f"wsl{i}") for i in range(NS)]
        self.ws_i = 0
        hst = sb("hst", [128, L, 2, HH, 128]); b_hst = [[[Buf(f"hst{l}_{s}_{h}") for h in range(HH)] for s in range(2)] for l in range(L)]
        s5c = sb("s5c", [64, L, 2, G, 2]); b_s5c = [Buf(f"s5c{l}") for l in range(L)]
        cvc = sb("cvc", [128, L, KF, 2, 2]); b_cvc = [Buf(f"cvc{l}") for l in range(L)]
        std = sb("std", [128, NT]); rstd = sb("rstd", [128, NT]); b_std = Buf("std")
        S.op("dve", lambda e: e.memset(hst[:].rearrange("p a b c d -> p (a b c d)"), 0.0), W=tuple(b for a in b_hst for c in a for b in c))
        S.op("dve", lambda e: e.memset(s5c[:].rearrange("p a b c d -> p (a b c d)"), 0.0), W=tuple(b_s5c))
        S.op("dve", lambda e: e.memset(cvc[:].rearrange("p a b c d -> p (a b c d)"), 0.0), W=tuple(b_cvc))
        reg_bufs = []

        def RB(name):
            b_ = Buf(name); reg_bufs.append(b_); return b_
        sizes = {
            "io": cfg.D,
            "hg": 3 * NT + 2 * NT + 2 * NT + 2 * NT + 3 * NT + NT + (2 * 3 * 128 + 4 * NT + 2 * 3 * 128 + 3 * 128 + 2 * 128 + NT) // 2 + 16,
            "s5": KS * NT + 2 * GH * NW + 2 * 2 * GH * 2 + KS * NT + 2 * NT + (2 * NT + GH * NW + 2 * GH * NW + GH * NW + KS * NT) // 2 + 8,
            "ff": 2 * 2 * (TT + 2) + 2 * 2 * TT + 2 * 2 * TT + (KF * NT) // 2 + 8,
        }
        RW = max(sizes.values())
        reg = sb("reg", [128, RW])

        class RA:
            def __init__(s_): s_.off = 0
            def _shape(s_, v, shape):
                if len(shape) == 2: return v
                names = "abcd"[:len(shape) - 1]
                pat = "p (" + " ".join(names) + ") -> p " + " ".join(names)
                return v.rearrange(pat, **{n_: d_ for n_, d_ in zip(names[1:], shape[2:])})
            def f32(s_, shape, parts=128):
                n = int(np.prod(shape[1:])); v = reg[0:parts, s_.off:s_.off + n]; s_.off += n
                assert s_.off <= RW
                return s_._shape(v, shape)
            def bf(s_, shape, parts=128):
                n = int(np.prod(shape[1:])); nw = (n + 1) // 2
                v = reg[0:parts, s_.off:s_.off + nw].bitcast(BF16)[:, 0:n]; s_.off += nw
                assert s_.off <= RW
                return s_._shape(v, shape)
        ra = RA()
        xin = ra.f32([128, cfg.D]); b_xin = RB("xin")
        ra = RA()
        hq = [ra.f32([128, NT])]; hk = [ra.f32([128, NT])]; hf = [ra.f32([128, NT])]
        hg2 = [ra.f32([128, NT]) for i in range(2)]; hv2 = [ra.bf([128, 3, 128]) for i in range(2)]
        eb2 = [ra.f32([128, NT]) for i in range(2)]; qtil2 = [ra.bf([128, NT]) for i in range(2)]
        ktil2 = [ra.bf([128, NT]) for i in range(2)]; khT2 = [ra.f32([128, NT]) for i in range(2)]
        b_hp = [RB("hp0"), RB("hp1")]; b_t1 = RB("ht1")
        bT = ra.f32([128, NT]); enb = ra.f32([128, NT]); erc = ra.f32([128, NT])
        khat = ra.bf([128, 2, 3, 128]); scm = ra.bf([128, 3, 128])
        sbf = ra.bf([128, 2, 128]); osq = ra.bf([128, NT]); otmp = ra.f32([128, NT])
        b_hr = RB("hrest"); b_sbf = [RB("sbf0"), RB("sbf1")]; b_scm = RB("scm"); b_khat = RB("khat")
        ra = RA()
        uT = ra.f32([128, KS, NT]); b_uT = [RB(f"uT{k}") for k in range(KS)]
        ubf = [ra.bf([128, NT]) for i in range(2)]; b_ubf = [RB("ubf0"), RB("ubf1")]
        Uall = ra.bf([128, GH, NW]); b_U = RB("Uall")
        Wall = ra.f32([64, 2, GH, NW], parts=64); b_W = RB("Wall")
        Sbf = ra.bf([64, 2, GH, NW], parts=64); b_S = RB("Sbf")
        Yw = ra.bf([128, GH, NW]); b_Yw = RB("Yw")
        sA = ra.f32([64, 2, GH, 2], parts=64); sB = ra.f32([64, 2, GH, 2], parts=64); b_sc = RB("scan")
        yact = ra.f32([128, KS, NT]); b_ya = [RB(f"ya{k}") for k in range(KS)]
        yabf = ra.bf([128, KS, NT]); b_yb = [RB(f"yb{k}") for k in range(KS)]
        ypre = ra.f32([128, NT]); sgl = ra.f32([128, NT]); b_yt = RB("ytmp")
        ra = RA()
        hid = ra.bf([128, KF, NT]); b_hid = [RB(f"hid{k}") for k in range(KF)]
        aext = [ra.f32([128, 2, TT + 2]) for i in range(2)]; cacc = [ra.f32([128, 2, TT]) for i in range(2)]
        csg = [ra.f32([128, 2, TT]) for i in range(2)]; b_ff = [RB("ff0"), RB("ff1")]

        def phase():
            S.barrier("dve", tuple(reg_bufs))

        def V(fn, R=(), W=()): return S.op("dve", fn, R=R, W=W)
        def A(fn, R=(), W=()): return S.op("act", fn, R=R, W=W)
        def Pl(fn, R=(), W=()): return S.op("pool", fn, R=R, W=W)
        def MM(fn, R=(), W=(), sig=False): return S.op("pe", fn, R=R, W=W, sig=sig)

        def load_slab(src_ap, nk, cw, ready, parts=128):
            i = self.ws_i % NS; self.ws_i += 1
            view = wsl[i][0:parts, 0:nk * cw].rearrange("p (k c) -> p k c", c=cw)
            S.dma("qs", view, src_ap, R=tuple(ready), W=(b_wsl[i],))
            return view, b_wsl[i]

        def rmsnorm_to_xn(gam):
            import os
            for k in range(KD):
                if k % 2 == 0 or os.environ.get("NOSQ"):
                    Pl(lambda e, k=k: e.tensor_tensor(out=xn[:, k, :], in0=hT[:, k, :], in1=hT[:, k, :], op=ALU.mult), R=(b_h[k],), W=(b_xn[k],))
                else:
                    A(lambda e, k=k: e.activation(out=xn[:, k, :], in_=hT[:, k, :], func=AF.Square), R=(b_h[k],), W=(b_xn[k],))
            ps, pb = self.bank(0)
            for k in range(KD):
                MM(lambda e, k=k: e.matmul(ps[:, 0:NT], ones_bf[:], xn[:, k, :], start=(k == 0), stop=(k == KD - 1)),
                   R=(b_xn[k], bcn), W=(pb,), sig=(k == KD - 1))
            A(lambda e: e.activation(out=std[:], in_=ps[:, 0:NT], func=AF.Sqrt, scale=1.0 / cfg.D, bias=cst["eps"][:]), R=(pb, bcn), W=(b_std,))
            V(lambda e: e.reciprocal(out=rstd[:], in_=std[:]), R=(b_std,), W=(b_std,))
            for k in range(KD):
                V(lambda e, k=k: e.scalar_tensor_tensor(out=xn[:, k, :], in0=hT[:, k, :], scalar=gam(k), in1=rstd[:], op0=ALU.mult, op1=ALU.mult),
                  R=(b_h[k], b_std, bcn), W=(b_xn[k],))

        lbv, omlv, nomlv, b_lb = self.lb
        rmask = sb("rmask", [128, 2])
        S.op("dve", lambda e: e.memset(rmask[:], 0.0), W=(bcn,))
        S.op("dve", lambda e: e.memset(rmask[0:64, 0:1], 1.0), R=(bcn,), W=(bcn,))
        S.op("dve", lambda e: e.memset(rmask[64:128, 1:2], 1.0), R=(bcn,), W=(bcn,))
        self.kvb = [self.pbuf[6], self.pbuf[6]]
        idf = cst["ident"]

        for ti in range(cfg.NTILE):
            phase()
            for tg in range(3):
                c0 = tg * 128
                segs = []
                c = c0
                while c < c0 + 128:
                    s_ = c // TT; j = c % TT; n = min(TT - j, c0 + 128 - c)
                    segs.append((c - c0, s_, ti * TT + j, n)); c += n
                for (po, s_, tok, n) in segs:
                    S.dma("qs", xin[po:po + n, :], self.dram["xp"][s_, tok:tok + n, :], W=(b_xin,))
                for k in range(KD):
                    ps, pb = self.bank(k % 4)
                    MM(lambda e, k=k, ps=ps: e.matmul(ps[:, 0:128], xin[:, k * 128:(k + 1) * 128], idf[:], start=True, stop=True),
                       R=(b_xin, bcn), W=(pb,), sig=True)
                    A(lambda e, k=k, ps=ps, c0=c0: e.activation(out=hT[:, k, c0:c0 + 128], in_=ps[:, 0:128], func=AF.Copy), R=(pb,), W=(b_h[k],))
            if self.stop_after == "stage0":
                self.write_out(hT, b_h, xin, b_xin, ti); continue
            stopped = False
            for l in range(L):
                sc = self.scr[l]
                rmsnorm_to_xn(lambda k: cst["g_mix"][:, l, k:k + 1])
                if self.stop_after == "norm1" and l == 0:
                    self.debug_dump_bf(xn, b_xn, ti); stopped = True; break
                phase()
                def hg_s1(hd):
                    pp = hd % 2
                    wv, wb = load_slab(sc["head"][hd], KD, 512, (self.slab_ready[("head", l, hd)],))
                    pq, bq = self.bank(0); pz, bz = self.bank(1); pg, bg = self.bank(2); pv, bv = self.bank(3)
                    for (pst, bst, c0) in ((pz, bz, 128), (pq, bq, 0), (pg, bg, 384)):
                        for k in range(KD):
                            MM(lambda e, k=k, pst=pst, c0=c0: e.matmul(pst[:, 0:NT], wv[:, k, c0:c0 + 128], xn[:, k, :], start=(k == 0), stop=(k == KD - 1)),
                               R=(wb, b_xn[k]), W=(bst,), sig=(k == KD - 1))
                    for tg in range(3):
                        for k in range(KD):
                            MM(lambda e, k=k, tg=tg: e.matmul(pv[:, tg * 128:(tg + 1) * 128], xn[:, k, tg * 128:(tg + 1) * 128], wv[:, k, 256:384], start=(k == 0), stop=(k == KD - 1)),
                               R=(wb, b_xn[k]), W=(bv,), sig=(k == KD - 1 and tg == 2))
                    bp = b_hp[pp]; bt1 = b_t1
                    A(lambda e: e.activation(out=hk[0][:], in_=pz[:, 0:NT], func=AF.Sigmoid), R=(bz,), W=(bt1,))
                    A(lambda e: e.activation(out=hq[0][:], in_=pq[:, 0:NT], func=AF.Silu), R=(bq,), W=(bt1,))
                    A(lambda e: e.activation(out=hg2[pp][:], in_=pg[:, 0:NT], func=AF.Silu), R=(bg,), W=(bp,))
                    Pl(lambda e: e.tensor_copy(out=hv2[pp][:], in_=pv[:, 0:NT].rearrange("p (a b) -> p a b", b=128)), R=(bv,), W=(bp,)) if False else \
                        V(lambda e: e.tensor_copy(out=hv2[pp][:], in_=pv[:, 0:NT].rearrange("p (a b) -> p a b", b=128)), R=(bv,), W=(bp,))
                    V(lambda e: e.tensor_scalar(out=hf[0][:], in0=hk[0][:], scalar1=omlv[:, l, hd:hd + 1], scalar2=lbv[:, l, hd:hd + 1], op0=ALU.mult, op1=ALU.add), R=(bt1, b_lb), W=(bt1,))
                    V(lambda e: e.tensor_scalar(out=hf[0][:], in0=hf[0][:], scalar1=F_FLOOR, scalar2=None, op0=ALU.max), R=(bt1,), W=(bt1,))
                    V(lambda e: e.tensor_scalar(out=hk[0][:], in0=hk[0][:], scalar1=nomlv[:, l, hd:hd + 1], scalar2=omlv[:, l, hd:hd + 1], op0=ALU.mult, op1=ALU.add), R=(bt1, b_lb), W=(bt1,))
                    A(lambda e: e.activation(out=hf[0][:], in_=hf[0][:], func=AF.Ln), R=(bt1,), W=(bt1,))
                    V(lambda e: e.tensor_tensor_scan(out=bT[:], data0=cst["cmask"][:], data1=hf[0][:], initial=0.0, op0=ALU.mult, op1=ALU.add), R=(bt1, bcn), W=(bt1,))
                    A(lambda e: e.activation(out=eb2[pp][:], in_=bT[:], func=AF.Exp), R=(bt1,), W=(bp,))
                    A(lambda e: e.activation(out=enb[:], in_=bT[:], func=AF.Exp, scale=-1.0), R=(bt1,), W=(bt1,))
                    for c in range(6):
                        A(lambda e, c=c: e.activation(out=erc[:, c * 64:(c + 1) * 64], in_=bT[:, c * 64:(c + 1) * 64], func=AF.Exp, scale=-1.0, bias=bT[:, c * 64 + 63:c * 64 + 64]), R=(bt1,), W=(bt1,))
                    V(lambda e: e.tensor_tensor(out=qtil2[pp][:], in0=hq[0][:], in1=eb2[pp][:], op=ALU.mult), R=(bt1, bp), W=(bp,))
                    V(lambda e: e.tensor_tensor(out=ktil2[pp][:], in0=hk[0][:], in1=enb[:], op=ALU.mult), R=(bt1,), W=(bp,))
                    V(lambda e: e.tensor_tensor(out=khT2[pp][:], in0=hk[0][:], in1=erc[:], op=ALU.mult), R=(bt1,), W=(bp,))

                def hg_s2(hd):
                    pp = hd % 2; bp = b_hp[pp]
                    qtil_ = qtil2[pp]; ktil_ = ktil2[pp]; khT_ = khT2[pp]; hv_ = hv2[pp]; eb_ = eb2[pp]; hgp = hg2[pp]
                    pk, bk = self.bank(4); psc, bsc = self.bank(5); pkv = self.psum[6]; bkv = self.kvb; po, bo_ = self.bank(7)
                    for tg in range(3):
                        MM(lambda e, tg=tg: e.matmul(pk[:, tg * 128:(tg + 1) * 128], khT_[:, tg * 128:(tg + 1) * 128], idf[:], start=True, stop=True), R=(bp, bcn), W=(bk,), sig=(tg == 2))
                    A(lambda e: e.activation(out=khat[:, 0], in_=pk[:, 0:NT].rearrange("p (a b) -> p a b", b=128), func=AF.Copy, scale=rmask[:, 0:1]), R=(bk, bcn), W=(b_khat,))
                    V(lambda e: e.tensor_scalar(out=khat[:, 1], in0=pk[:, 0:NT].rearrange("p (a b) -> p a b", b=128), scalar1=rmask[:, 1:2], scalar2=None, op0=ALU.mult), R=(bk, bcn), W=(b_khat,))
                    for tg in range(3):
                        MM(lambda e, tg=tg: e.matmul(psc[:, tg * 128:(tg + 1) * 128], ktil_[:, tg * 128:(tg + 1) * 128], qtil_[:, tg * 128:(tg + 1) * 128], start=True, stop=True), R=(bp,), W=(bsc,), sig=(tg == 2))
                    for tg in range(3):
                        V(lambda e, tg=tg: e.tensor_tensor(out=scm[:, tg, :], in0=psc[:, tg * 128:(tg + 1) * 128], in1=cst["tri2"][:], op=ALU.mult), R=(bsc, bcn), W=(b_scm,))
                    order = (0, 1, 2, 3, 4, 5)
                    for ci, c in enumerate(order):
                        s_ = c // 3; tg = c // 2; j = c % 2; r0 = 64 * j
                        bs_ = b_hst[l][s_][hd]
                        if c % 3 == 0:
                            A(lambda e, s_=s_: e.activation(out=sbf[:, s_, :], in_=hst[:, l, s_, hd, :], func=AF.Copy), R=(bs_,), W=(b_sbf[s_],))
                        MM(lambda e, c=c, s_=s_: e.matmul(po[:, c * 64:(c + 1) * 64], sbf[:, s_, :], qtil_[:, c * 64:(c + 1) * 64], start=True, stop=False), R=(b_sbf[s_], bp), W=(bo_,))
                        MM(lambda e, c=c, tg=tg, r0=r0: e.matmul(po[:, c * 64:(c + 1) * 64], hv_[:, tg, :], scm[:, tg, r0:r0 + 64], start=False, stop=True), R=(bp, b_scm), W=(bo_,), sig=(ci == 5))
                        kvs = (ci % 2) * 128
                        MM(lambda e, tg=tg, j=j, kvs=kvs: e.matmul(pkv[:, kvs:kvs + 128], khat[:, j, tg, :], hv_[:, tg, :], start=True, stop=True), R=(b_khat, bp), W=(bkv[ci % 2],), sig=True)
                        V(lambda e, c=c, s_=s_, kvs=kvs: e.scalar_tensor_tensor(out=hst[:, l, s_, hd, :], in0=hst[:, l, s_, hd, :], scalar=eb_[:, c * 64 + 63:c * 64 + 64], in1=pkv[:, kvs:kvs + 128], op0=ALU.mult, op1=ALU.add), R=(bkv[ci % 2], bp, bs_), W=(bs_,))
                        if c % 3 != 2:
                            A(lambda e, s_=s_: e.activation(out=sbf[:, s_, :], in_=hst[:, l, s_, hd, :], func=AF.Copy), R=(bs_,), W=(b_sbf[s_],))
                    A(lambda e: e.activation(out=osq[:], in_=po[:, 0:NT], func=AF.Square), R=(bo_,), W=(b_hr,))
                    MM(lambda e: e.matmul(pk[:, 0:NT], ones_bf[:], osq[:], start=True, stop=True), R=(b_hr, bcn), W=(bk,), sig=True)
                    A(lambda e: e.activation(out=otmp[:], in_=pk[:, 0:NT], func=AF.Sqrt, scale=1.0 / 128.0, bias=cst["eps"][:]), R=(bk, bcn), W=(b_hr,))
                    V(lambda e: e.reciprocal(out=otmp[:], in_=otmp[:]), R=(b_hr,), W=(b_hr,))
                    V(lambda e: e.tensor_tensor(out=otmp[:], in0=po[:, 0:NT], in1=otmp[:], op=ALU.mult), R=(bo_, b_hr), W=(b_hr,))
                    V(lambda e: e.scalar_tensor_tensor(out=mix[:, hd, :], in0=otmp[:], scalar=cst["hg_gain"][:, l, hd:hd + 1], in1=hgp[:], op0=ALU.mult, op1=ALU.mult), R=(b_hr, bp, bcn), W=(b_mix[hd],))

                import os
                if os.environ.get("NOPIPE"):
                    for hd in range(HH):
                        hg_s1(hd); hg_s2(hd)
                else:
                    hg_s1(0)
                    for hd in range(HH):
                        if hd + 1 < HH:
                            hg_s1(hd + 1)
                        hg_s2(hd)
                if self.stop_after == "hgrn" and l == 0:
                    self.debug_dump_bf(mix, b_mix, ti); stopped = True; break
                phase()
                for j in range(cfg.NU):
                    wv, wb = load_slab(sc["u"][j], KD, cfg.UW, (self.slab_ready[("u", l, j)],))
                    for bi in range(cfg.UW // 128):
                        blk = j * (cfg.UW // 128) + bi
                        ps, pb = self.bank(blk % 4)
                        for k in range(KD):
                            MM(lambda e, k=k, bi=bi, ps=ps: e.matmul(ps[:, 0:NT], wv[:, k, bi * 128:(bi + 1) * 128], xn[:, k, :], start=(k == 0), stop=(k == KD - 1)),
                               R=(wb, b_xn[k]), W=(pb,), sig=(k == KD - 1))
                        A(lambda e, blk=blk, ps=ps: e.activation(out=uT[:, blk, :], in_=ps[:, 0:NT], func=AF.Copy), R=(pb,), W=(b_uT[blk],))
                if self.stop_after == "s5a" and l == 0:
                    self.debug_dump_bf(mix, b_mix, ti); stopped = True; break
                r8re, r8im, r8imn, br8 = self.r8[l]
                selbf = cst["sel_bf"]
                lvl = {"s5b": 1, "s5c": 2, "s5d": 3, "s5e": 4, "s5f": 5}.get(self.stop_after, 99)
                for hf_ in range(NHALF):
                    g0 = hf_ * GH
                    Bv, Bb_ = load_slab(sc["B"][:, g0:g0 + GH, :], GH, 128, ())
                    for gb in range(GH // 8):
                        blk = (g0 // 8) + gb
                        ub = ubf[blk % 2]; bub = b_ubf[blk % 2]
                        V(lambda e, blk=blk, ub=ub: e.tensor_copy(out=ub[:], in_=uT[:, blk, :]), R=(b_uT[blk],), W=(bub,))
                        ps, pb = self.bank(blk % 2)
                        for jg in range(8):
                            for s_ in range(8):
                                MM(lambda e, jg=jg, s_=s_, ps=ps, ub=ub: e.matmul(ps[:, jg * NW:(jg + 1) * NW], selbf[:, jg, s_, :], ub[:, s_:NT:8], start=(s_ == 0), stop=(s_ == 7)),
                                   R=(bub, bcn), W=(pb,), sig=(s_ == 7 and jg == 7))
                        A(lambda e, gb=gb, ps=ps: e.activation(out=Uall[:, gb * 8:(gb + 1) * 8, :], in_=ps[:, 0:8 * NW].rearrange("p (g c) -> p g c", c=NW), func=AF.Copy), R=(pb,), W=(b_U,))
                    if lvl < 2:
                        continue
                    for g4 in range(GH // 4):
                        ps, pb = self.bank(2 + g4 % 2)
                        for gi in range(4):
                            g = g4 * 4 + gi
                            for ri in range(2):
                                MM(lambda e, g=g, gi=gi, ri=ri, ps=ps: e.matmul(ps[0:64, (gi * 2 + ri) * NW:(gi * 2 + ri + 1) * NW], Bv[:, g, ri * 64:(ri + 1) * 64], Uall[:, g, :], start=True, stop=True),
                                   R=(Bb_, b_U), W=(pb,), sig=(gi == 3 and ri == 1))
                        V(lambda e, g4=g4, ps=ps: e.tensor_copy(out=Wall[:, :, g4 * 4:(g4 + 1) * 4, :].rearrange("p r g c -> p g r c"), in_=ps[0:64, 0:8 * NW].rearrange("p (g r c) -> p g r c", r=2, c=NW)), R=(pb,), W=(b_W,))
                    if lvl < 3:
                        continue
                    cs_ = s5c[:, l, :, g0:g0 + GH, :]
                    Wv = Wall[:].rearrange("p r g (s m) -> p r g s m", s=2)
                    Sv = Sbf[:].rearrange("p r g (s m) -> p r g s m", s=2)
                    V(lambda e: e.tensor_copy(out=Sv[:, :, :, :, 0], in_=cs_), R=(b_s5c[l],), W=(b_S,))
                    for m in range(NWS):
                        prev = cs_ if m == 0 else Wv[:, :, :, :, m - 1]
                        cur = Wv[:, :, :, :, m]
                        rb = (b_W, br8, b_s5c[l])
                        V(lambda e, prev=prev: e.tensor_tensor(out=sA[:], in0=prev, in1=r8re[:, g0:g0 + GH, :].unsqueeze(1).broadcast_to([64, 2, GH, 2]), op=ALU.mult), R=rb, W=(b_sc,))
                        V(lambda e, prev=prev: e.tensor_tensor(out=sB[:, 0], in0=prev[:, 1], in1=r8imn[:, g0:g0 + GH, :], op=ALU.mult), R=rb, W=(b_sc,))
                        V(lambda e, prev=prev: e.tensor_tensor(out=sB[:, 1], in0=prev[:, 0], in1=r8im[:, g0:g0 + GH, :], op=ALU.mult), R=rb, W=(b_sc,))
                        V(lambda e: e.tensor_tensor(out=sA[:], in0=sA[:], in1=sB[:], op=ALU.add), R=(b_sc,), W=(b_sc,))
                        V(lambda e, cur=cur: e.tensor_tensor(out=cur, in0=cur, in1=sA[:], op=ALU.add), R=(b_sc, b_W), W=(b_W,))
                    for s_ in range(2):
                        V(lambda e, s_=s_: e.tensor_copy(out=Sv[:, :, :, s_, 1:NWS], in_=Wv[:, :, :, s_, 0:NWS - 1]), R=(b_W,), W=(b_S,))
                    V(lambda e: e.tensor_copy(out=cs_, in_=Wv[:, :, :, :, NWS - 1]), R=(b_W, b_S), W=(b_s5c[l],))
                    if lvl < 4:
                        continue
                    Mv, Mb = load_slab(sc["M"][:, g0:g0 + GH, :], GH, 128, ())
                    i_ = self.ws_i % NS; self.ws_i += 1
                    Cv = wsl[i_][0:64, 0:2 * GH * 128].rearrange("p (r g c) -> p r g c", r=2, c=128)
                    S.dma("qs", Cv, sc["C"][:, :, g0:g0 + GH, :], W=(b_wsl[i_],))
                    Cre_v = Cv[:, 0]; Cim_v = Cv[:, 1]; Cre_b = b_wsl[i_]; Cim_b = b_wsl[i_]
                    for gb in range(GH // 8):
                        ps, pb = self.bank(4 + gb % 2)
                        for gi in range(8):
                            g = gb * 8 + gi
                            osl = ps[:, gi * NW:(gi + 1) * NW]
                            MM(lambda e, g=g, osl=osl: e.matmul(osl, Mv[:, g, :], Uall[:, g, :], start=True, stop=False), R=(Mb, b_U), W=(pb,))
                            MM(lambda e, g=g, osl=osl: e.matmul(osl, Cre_v[0:64, g, :], Sbf[:, 0, g, :], start=False, stop=False), R=(Cre_b, b_S), W=(pb,))
                            MM(lambda e, g=g, osl=osl: e.matmul(osl, Cim_v[0:64, g, :], Sbf[:, 1, g, :], start=False, stop=True), R=(Cim_b, b_S), W=(pb,), sig=(gi == 7))
                        A(lambda e, gb=gb, ps=ps: e.activation(out=Yw[:, gb * 8:(gb + 1) * 8, :], in_=ps[:, 0:8 * NW].rearrange("p (g c) -> p g c", c=NW), func=AF.Copy), R=(pb,), W=(b_Yw,))
                    if lvl < 5:
                        continue
                    for gb in range(GH // 8):
                        blk = (g0 // 8) + gb
                        ps, pb = self.bank(6 + gb % 2)
                        for t_ in range(8):
                            for jg in range(8):
                                MM(lambda e, t_=t_, jg=jg, gb=gb, ps=ps: e.matmul(ps[:, t_:NT:8], selbf[:, t_, jg, :], Yw[:, gb * 8 + jg, :], start=(jg == 0), stop=(jg == 7)),
                                   R=(b_Yw, bcn), W=(pb,), sig=(jg == 7 and t_ == 7))
                        V(lambda e, blk=blk, ps=ps: e.scalar_tensor_tensor(out=ypre[:], in0=uT[:, blk, :], scalar=cst["s5_d"][:, l, blk:blk + 1], in1=ps[:, 0:NT], op0=ALU.mult, op1=ALU.add), R=(pb, b_uT[blk], bcn), W=(b_yt,))
                        V(lambda e: e.tensor_tensor(out=sgl[:], in0=ypre[:], in1=ypre[:], op=ALU.mult), R=(b_yt,), W=(b_yt,))
                        V(lambda e: e.tensor_scalar(out=sgl[:], in0=sgl[:], scalar1=0.044715, scalar2=1.0, op0=ALU.mult, op1=ALU.add), R=(b_yt,), W=(b_yt,))
                        V(lambda e: e.tensor_tensor(out=sgl[:], in0=sgl[:], in1=ypre[:], op=ALU.mult), R=(b_yt,), W=(b_yt,))
                        A(lambda e: e.activation(out=sgl[:], in_=sgl[:], func=AF.Sigmoid, scale=1.5957691216057308), R=(b_yt,), W=(b_yt,))
                        V(lambda e, blk=blk: e.tensor_tensor(out=yact[:, blk, :], in0=ypre[:], in1=sgl[:], op=ALU.mult), R=(b_yt,), W=(b_ya[blk],))
                        V(lambda e, blk=blk: e.tensor_copy(out=yabf[:, blk, :], in_=yact[:, blk, :]), R=(b_ya[blk],), W=(b_yb[blk],))
                if lvl < 99:
                    if self.stop_after == "s5f" and ti == 0:
                        dt_ = self.sb("dbgtmp", [128, 4096])
                        bd = Buf("dbgtmp")
                        def dd(ap, parts, n, off, rb):
                            S.op("dve", lambda e: e.tensor_copy(out=dt_[0:parts, 0:n], in_=ap), R=rb + (bd,), W=(bd,))
                            S.dma("qs", self.dram["dbg2"][0:parts, off:off + n], dt_[0:parts, 0:n], R=(bd,), W=(Buf("z"),))
                            tk_ = bd.r[("qs", (S.qs["qs"].i - 1) % 8)] if False else None
                        dd(Uall[:].rearrange("p g c -> p (g c)"), 128, GH * NW, 0, (b_U,))
                        dd(Wall[:].rearrange("p r g c -> p (r g c)"), 64, 2 * GH * NW, 2048, (b_W,))
                        dd(Sbf[:].rearrange("p r g c -> p (r g c)"), 64, 2 * GH * NW, 6144, (b_S,))
                        dd(Yw[:].rearrange("p g c -> p (g c)"), 128, GH * NW, 10240, (b_Yw,))
                        dd(yact[:].rearrange("p k c -> p (k c)"), 128, KS * NT, 12288, tuple(b_ya))
                        dd(uT[:].rearrange("p k c -> p (k c)"), 128, KS * NT, 16384, tuple(b_uT))
                    self.debug_dump_bf(mix, b_mix, ti); stopped = True; break
                for j in range(cfg.NU):
                    wv, wb = load_slab(sc["glu"][j], KS, cfg.UW, (self.slab_ready[("glu", l, j)],))
                    for bi in range(cfg.UW // 128):
                        blk = j * (cfg.UW // 128) + bi
                        ps, pb = self.bank(blk % 4)
                        for k in range(KS):
                            MM(lambda e, k=k, bi=bi, ps=ps: e.matmul(ps[:, 0:NT], wv[:, k, bi * 128:(bi + 1) * 128], yabf[:, k, :], start=(k == 0), stop=(k == KS - 1)),
                               R=(wb, b_yb[k]), W=(pb,), sig=(k == KS - 1))
                        A(lambda e, blk=blk, ps=ps: e.activation(out=sgl[:], in_=ps[:, 0:NT], func=AF.Sigmoid, bias=cst["b_glu"][:, l, blk:blk + 1]), R=(pb, bcn), W=(b_yt,))
                        V(lambda e, blk=blk: e.tensor_tensor(out=yact[:, blk, :], in0=yact[:, blk, :], in1=sgl[:], op=ALU.mult), R=(b_yt, b_ya[blk]), W=(b_ya[blk],))
                for k in range(KS):
                    Pl(lambda e, k=k: e.tensor_tensor(out=yabf[:, k, :], in0=yact[:, k, :], in1=yact[:, k, :], op=ALU.mult), R=(b_ya[k], b_yb[k]), W=(b_yb[k],))
                ps, pb = self.bank(4)
                for k in range(KS):
                    MM(lambda e, k=k: e.matmul(ps[:, 0:NT], ones_bf[:], yabf[:, k, :], start=(k == 0), stop=(k == KS - 1)), R=(b_yb[k], bcn), W=(pb,), sig=(k == KS - 1))
                A(lambda e: e.activation(out=std[:], in_=ps[:, 0:NT], func=AF.Sqrt, scale=1.0 / cfg.DS5, bias=cst["eps"][:]), R=(pb, bcn), W=(b_std,))
                V(lambda e: e.reciprocal(out=rstd[:], in_=std[:]), R=(b_std,), W=(b_std,))
                for k in range(KS):
                    V(lambda e, k=k: e.scalar_tensor_tensor(out=mix[:, HH + k, :], in0=yact[:, k, :], scalar=cst["s5_gain"][:, l, k:k + 1], in1=rstd[:], op0=ALU.mult, op1=ALU.mult), R=(b_ya[k], b_std, bcn), W=(b_mix[HH + k],))
                if self.stop_after == "mix" and l == 0:
                    self.debug_dump_bf(mix, b_mix, ti); stopped = True; break
                nob = cfg.OW // 128
                for j in range(cfg.NO):
                    wv, wb = load_slab(sc["out"][j], KD, cfg.OW, (self.slab_ready[("out", l, j)],))
                    for bi in range(nob):
                        ob = j * nob + bi
                        ps, pb = self.bank(ob % 4)
                        for k in range(KD):
                            MM(lambda e, k=k, bi=bi, ps=ps: e.matmul(ps[:, 0:NT], wv[:, k, bi * 128:(bi + 1) * 128], mix[:, k, :], start=(k == 0), stop=(k == KD - 1)),
                               R=(wb, b_mix[k]), W=(pb,), sig=(k == KD - 1))
                        V(lambda e, ob=ob, ps=ps: e.tensor_tensor(out=hT[:, ob, :], in0=hT[:, ob, :], in1=ps[:, 0:NT], op=ALU.add), R=(pb, b_h[ob]), W=(b_h[ob],))
                if self.stop_after == "attn" and l == 0:
                    self.write_out(hT, b_h, xin, b_xin, ti); stopped = True; break
                phase()
                rmsnorm_to_xn(lambda k: cst["g_ffn"][:, l, k:k + 1])
                nfb = cfg.FW // 128
                for j in range(cfg.NF):
                    gv, gbuf = load_slab(sc["gate"][j], KD, cfg.FW, (self.slab_ready[("gate", l, j)],))
                    uv, ubuf = load_slab(sc["up"][j], KD, cfg.FW, (self.slab_ready[("up", l, j)],))
                    for bi in range(nfb):
                        fb = j * nfb + bi; pp = fb % 2
                        pa, ba = self.bank(2 * pp); pu, bu = self.bank(2 * pp + 1)
                        for k in range(KD):
                            MM(lambda e, k=k, bi=bi, pa=pa: e.matmul(pa[:, 0:NT], gv[:, k, bi * 128:(bi + 1) * 128], xn[:, k, :], start=(k == 0), stop=(k == KD - 1)), R=(gbuf, b_xn[k]), W=(ba,), sig=(k == KD - 1))
                        for k in range(KD):
                            MM(lambda e, k=k, bi=bi, pu=pu: e.matmul(pu[:, 0:NT], uv[:, k, bi * 128:(bi + 1) * 128], xn[:, k, :], start=(k == 0), stop=(k == KD - 1)), R=(ubuf, b_xn[k]), W=(bu,), sig=(k == KD - 1))
                        ae = aext[pp]; ca = cacc[pp]; cg = csg[pp]; bf_ = b_ff[pp]
                        cw = cst["conv_w"]
                        Pl(lambda e, fb=fb, ae=ae: e.tensor_copy(out=ae[:, :, 0:2], in_=cvc[:, l, fb, :, :]), R=(b_cvc[l],), W=(bf_,))
                        A(lambda e, ae=ae, pa=pa: e.activation(out=ae[:, :, 2:TT + 2], in_=pa[:, 0:NT].rearrange("p (s t) -> p s t", s=2), func=AF.Copy), R=(ba,), W=(bf_,))
                        Pl(lambda e, fb=fb, ae=ae: e.tensor_copy(out=cvc[:, l, fb, :, :], in_=ae[:, :, TT:TT + 2]), R=(bf_,), W=(b_cvc[l],))
                        V(lambda e, fb=fb, ae=ae, ca=ca: e.tensor_scalar(out=ca[:], in0=ae[:, :, 2:TT + 2], scalar1=cw[:, l, 2, fb:fb + 1], scalar2=cst["conv_b"][:, l, fb:fb + 1], op0=ALU.mult, op1=ALU.add), R=(bf_, bcn), W=(bf_,))
                        V(lambda e, fb=fb, ae=ae, ca=ca: e.scalar_tensor_tensor(out=ca[:], in0=ae[:, :, 1:TT + 1], scalar=cw[:, l, 1, fb:fb + 1], in1=ca[:], op0=ALU.mult, op1=ALU.add), R=(bf_, bcn), W=(bf_,))
                        V(lambda e, fb=fb, ae=ae, ca=ca: e.scalar_tensor_tensor(out=ca[:], in0=ae[:, :, 0:TT], scalar=cw[:, l, 0, fb:fb + 1], in1=ca[:], op0=ALU.mult, op1=ALU.add), R=(bf_, bcn), W=(bf_,))
                        A(lambda e, ca=ca, cg=cg: e.activation(out=cg[:], in_=ca[:], func=AF.Silu), R=(bf_,), W=(bf_,))
                        V(lambda e, fb=fb, cg=cg, pu=pu: e.tensor_tensor(out=hid[:, fb, :], in0=cg[:].rearrange("p s t -> p (s t)"), in1=pu[:, 0:NT], op=ALU.mult), R=(bf_, bu), W=(b_hid[fb],))
                for j in range(cfg.NO):
                    banks = [self.bank(4 + bi) for bi in range(nob)]
                    for pi, (k0, nk) in enumerate(cfg.KPIECES):
                        wv, wb = load_slab(sc["down"][j, pi, :, 0:nk, :], nk, cfg.OW, (self.slab_ready[("down", l, j, pi)],))
                        for bi in range(nob):
                            ps, pb = banks[bi]
                            for kk in range(nk):
                                kf_ = k0 + kk
                                MM(lambda e, kk=kk, kf_=kf_, bi=bi, ps=ps: e.matmul(ps[:, 0:NT], wv[:, kk, bi * 128:(bi + 1) * 128], hid[:, kf_, :], start=(kf_ == 0), stop=(kf_ == KF - 1)),
                                   R=(wb, b_hid[kf_]), W=(pb,), sig=(kk == nk - 1))
                    for bi in range(nob):
                        ob = j * nob + bi; ps, pb = banks[bi]
                        V(lambda e, ob=ob, ps=ps: e.tensor_tensor(out=hT[:, ob, :], in0=hT[:, ob, :], in1=ps[:, 0:NT], op=ALU.add), R=(pb, b_h[ob]), W=(b_h[ob],))
                if self.stop_after == "l0" and l == 0:
                    self.write_out(hT, b_h, xin, b_xin, ti); stopped = True; break
            if stopped:
                continue
            if True:
                phase()
                for k in range(KD):
                    Pl(lambda e, k=k: e.tensor_tensor(out=xn[:, k, :], in0=hT[:, k, :], in1=hT[:, k, :], op=ALU.mult), R=(b_h[k],), W=(b_xn[k],))
                ps, pb = self.bank(0)
                for k in range(KD):
                    MM(lambda e, k=k: e.matmul(ps[:, 0:NT], ones_bf[:], xn[:, k, :], start=(k == 0), stop=(k == KD - 1)), R=(b_xn[k], bcn), W=(pb,), sig=(k == KD - 1))
                A(lambda e: e.activation(out=std[:], in_=ps[:, 0:NT], func=AF.Sqrt, scale=1.0 / cfg.D, bias=cst["eps"][:]), R=(pb, bcn), W=(b_std,))
                V(lambda e: e.reciprocal(out=rstd[:], in_=std[:]), R=(b_std,), W=(b_std,))
                for k in range(KD):
                    V(lambda e, k=k: e.scalar_tensor_tensor(out=hT[:, k, :], in0=hT[:, k, :], scalar=cst["g_fin"][:, k:k + 1], in1=rstd[:], op0=ALU.mult, op1=ALU.mult), R=(b_h[k], b_std, bcn), W=(b_h[k],))
                self.write_out(hT, b_h, xin, b_xin, ti)

    def write_out(self, src, b_src, xin, b_xin, ti, nblk=None):
        cfg = self.cfg; S = self.S; idf = self.cst["ident"]; bcn = self.b_const
        nblk = cfg.KD if nblk is None else nblk
        for tg in range(3):
            c0 = tg * 128
            for k in range(nblk):
                ps, pb = self.bank(k % 4)
                S.op("pe", lambda e, k=k, ps=ps: e.matmul(ps[:, 0:128], src[:, k, c0:c0 + 128], idf[:], start=True, stop=True), R=(b_src[k], bcn), W=(pb,), sig=True)
                S.op("act", lambda e, k=k, ps=ps: e.activation(out=xin[:, k * 128:(k + 1) * 128], in_=ps[:, 0:128], func=AF.Copy), R=(pb,), W=(b_xin,))
            c = c0
            while c < c0 + 128:
                s_ = c // TT; j = c % TT; n = min(TT - j, c0 + 128 - c)
                tk = S.dma("qs", self.dram["out"][s_, ti * TT + j:ti * TT + j + n, 0:nblk * 128], xin[c - c0:c - c0 + n, 0:nblk * 128], R=(b_xin,), W=(Buf("o"),))
                self.out_tks.append(tk)
                c += n

    def debug_dump_bf(self, src, b_src, ti):
        cfg = self.cfg; S = self.S
        if not hasattr(self, "dbg_t"):
            self.dbg_t = self.sb("dbg_t", [128, cfg.KD, NT]); self.dbg_b = [Buf(f"dbg{k}") for k in range(cfg.KD)]
        for k in range(cfg.KD):
            S.op("dve", lambda e, k=k: e.tensor_copy(out=self.dbg_t[:, k, :], in_=src[:, k, :]), R=(b_src[k],), W=(self.dbg_b[k],))
        xin = self.xin_dbg if hasattr(self, "xin_dbg") else None
        if xin is None:
            self.xin_dbg = self.sb("xin_dbg", [128, cfg.D]); self.xin_dbg_b = Buf("xind")
        self.write_out(self.dbg_t, self.dbg_b, self.xin_dbg, self.xin_dbg_b, ti)


_CACHE = {}


def run_cfg(inputs, cfg, stop_after=None):
    maps = host_prep(inputs, cfg)
    key = (cfg.D, cfg.DFF, cfg.SEQ, cfg.L, stop_after)
    if key not in _CACHE:
        _CACHE[key] = Builder(cfg, stop_after).build()
    nc = _CACHE[key]
    ncores = cfg.NB // 2
    res = run_bass_kernel_spmd(nc, maps, core_ids=list(range(ncores)))
    if stop_after == "s5p":
        return np.asarray(res.results[0]["dbg"])
    if stop_after == "s5f":
        return np.asarray(res.results[0]["dbg2"])
    outs = [np.asarray(r["out"]) for r in res.results]
    full = np.concatenate([o[:, 64:, :] for o in outs], axis=0)
    return np.ascontiguousarray(full.astype(np.float32))


def kernel(**inputs):
    cfg = Cfg()
    return run_cfg(inputs, cfg)
```
